# Optimizing a Trainium2 kernel written in Bass

```python
import math
import jax, jax.numpy as jnp
from jax import lax
import numpy as np

D_MODEL = 2048
BATCH = 4
SEQ = 2048
DEPTH = 1
DEC_BATCH = 128
DEC_SEQ = 8
PAST_LEN = 8192
PAGE_SIZE = 128

SSM_EXPAND = 2
D_INNER = SSM_EXPAND * D_MODEL
SSM_HEAD_DIM = 64
SSM_HEADS = D_INNER // SSM_HEAD_DIM
SSM_GROUPS = 8
SSM_HPG = SSM_HEADS // SSM_GROUPS
SSM_STATE = 128
CONV_WIDTH = 4
GN = SSM_GROUPS * SSM_STATE
CONV_DIM = D_INNER + 2 * GN
SSD_CHUNK = 128

ATTN_HEAD_DIM = 64
ATTN_Q_HEADS = D_MODEL // ATTN_HEAD_DIM
ATTN_KV_HEADS = ATTN_Q_HEADS // 4
ATTN_REP = ATTN_Q_HEADS // ATTN_KV_HEADS
ATTN_WIDTH = ATTN_Q_HEADS * ATTN_HEAD_DIM
KV_WIDTH = ATTN_KV_HEADS * ATTN_HEAD_DIM
WINDOW = 128

N_BUCKETS = 32
MAX_EXACT = N_BUCKETS // 2
MAX_DISTANCE = WINDOW

RMS_EPS = 1e-6

OFF_Z = 0
OFF_XBC = OFF_Z + D_INNER
OFF_DT = OFF_XBC + CONV_DIM
OFF_Q = OFF_DT + SSM_HEADS
OFF_K = OFF_Q + ATTN_WIDTH
OFF_V = OFF_K + KV_WIDTH
OFF_GA = OFF_V + KV_WIDTH
OFF_MIX_S = OFF_GA + ATTN_WIDTH
OFF_MIX_A = OFF_MIX_S + D_MODEL
IN_COLS = OFF_MIX_A + D_MODEL

kernel_name = 'hybrid_ssd_swa_gated_step'


def _rmsnorm(x, w):
    xf = x.astype(jnp.float32)
    xf = xf * lax.rsqrt(jnp.mean(xf * xf, axis=-1, keepdims=True) + RMS_EPS)
    return (xf * w.astype(jnp.float32)).astype(x.dtype)


def _gated_group_rmsnorm(y, z, w):
    g = y.astype(jnp.float32) * jax.nn.silu(z.astype(jnp.float32))
    shp = g.shape
    g = g.reshape(shp[:-1] + (SSM_GROUPS, shp[-1] // SSM_GROUPS))
    g = g * lax.rsqrt(jnp.mean(g * g, axis=-1, keepdims=True) + RMS_EPS)
    return (g.reshape(shp) * w.astype(jnp.float32)).astype(y.dtype)


def _t5_bucket(dist):
    n = jnp.maximum(dist, 0)
    nf = jnp.maximum(n, 1).astype(jnp.float32)
    large = MAX_EXACT + (jnp.log(nf / MAX_EXACT) / math.log(MAX_DISTANCE / MAX_EXACT)
                         * (N_BUCKETS - MAX_EXACT)).astype(jnp.int32)
    large = jnp.minimum(large, N_BUCKETS - 1)
    return jnp.where(n < MAX_EXACT, n, large)


def _rel_bias(dist, table):
    q, k = dist.shape
    bias = table[_t5_bucket(dist)].astype(jnp.float32)
    return jnp.moveaxis(bias, -1, 0).reshape(ATTN_KV_HEADS, ATTN_REP, q, k)


def _sink_attention(q, k, v, bias, valid, sinks):
    scale = ATTN_HEAD_DIM ** -0.5
    s = jnp.einsum('...qhrd,...khd->...hrqk', q, k).astype(jnp.float32) * scale + bias
    s = jnp.where(valid, s, -jnp.inf)
    sink = sinks.astype(jnp.float32).reshape(ATTN_KV_HEADS, ATTN_REP, 1, 1)
    m = jnp.maximum(jnp.max(s, axis=-1, keepdims=True), sink)
    p = jnp.exp(s - m)
    denom = jnp.sum(p, axis=-1, keepdims=True) + jnp.exp(sink - m)
    return jnp.einsum('...hrqk,...khd->...qhrd', (p / denom).astype(v.dtype), v)


def _banded_window_attention(q, k, v, rel_table, sinks):
    b, l = q.shape[:2]
    nb = l // WINDOW
    qb = q.reshape(b, nb, WINDOW, ATTN_KV_HEADS, ATTN_REP, ATTN_HEAD_DIM)
    kb = k.reshape(b, nb, WINDOW, ATTN_KV_HEADS, ATTN_HEAD_DIM)
    vb = v.reshape(b, nb, WINDOW, ATTN_KV_HEADS, ATTN_HEAD_DIM)

    def with_prev(t):
        prev = jnp.concatenate([jnp.zeros_like(t[:, :1]), t[:, :-1]], axis=1)
        return jnp.concatenate([prev, t], axis=2)

    kk, vv = with_prev(kb), with_prev(vb)
    qi = jnp.arange(WINDOW)[:, None]
    kj = jnp.arange(2 * WINDOW)[None, :]
    dist = qi + WINDOW - kj
    band = (dist >= 0) & (dist <= WINDOW)
    kpos = jnp.arange(nb)[:, None, None] * WINDOW + kj[None] - WINDOW
    valid = (band[None] & (kpos >= 0))[None, :, None, None]
    o = _sink_attention(qb, kk, vv, _rel_bias(dist, rel_table), valid, sinks)
    return o.reshape(b, l, ATTN_KV_HEADS, ATTN_REP, ATTN_HEAD_DIM)


def _window_decode_attention(q, kk, vv, rel_table, sinks):
    l = q.shape[1]
    kl = kk.shape[1]
    w_buf = kl - l
    dist = jnp.arange(l)[:, None] + w_buf - jnp.arange(kl)[None, :]
    valid = (dist >= 0) & (dist <= WINDOW)
    return _sink_attention(q, kk, vv, _rel_bias(dist, rel_table), valid, sinks)


def _causal_conv(xbc, buf, w, b):
    l = xbc.shape[1]
    xp = jnp.concatenate([buf.astype(xbc.dtype), xbc], axis=1)
    y = b
    for i in range(CONV_WIDTH):
        y = y + xp[:, i:i + l] * w[i]
    return jax.nn.silu(y), xp[:, -(CONV_WIDTH - 1):]


def _ssd(xs, dt, A, Bm, Cm, init_state, chunk):
    f32 = jnp.float32
    b, l = xs.shape[:2]
    c = l // chunk
    X = (xs.astype(f32) * dt[..., None]).reshape(b, c, chunk, SSM_GROUPS, SSM_HPG, SSM_HEAD_DIM)
    dA = (dt * A).reshape(b, c, chunk, SSM_GROUPS, SSM_HPG)
    Bc = Bm.astype(f32).reshape(b, c, chunk, SSM_GROUPS, SSM_STATE)
    Cc = Cm.astype(f32).reshape(b, c, chunk, SSM_GROUPS, SSM_STATE)
    Acs = jnp.cumsum(dA, axis=2)
    causal = jnp.tril(jnp.ones((chunk, chunk), bool))[:, :, None, None]
    seg = Acs[:, :, :, None] - Acs[:, :, None, :]
    Lmat = jnp.where(causal, jnp.exp(jnp.where(causal, seg, 0.0)), 0.0)
    CB = jnp.einsum('bclgn,bcsgn->bclsg', Cc, Bc)
    y_diag = jnp.einsum('bclsg,bclsgr,bcsgrp->bclgrp', CB, Lmat, X)
    decay_states = jnp.exp(Acs[:, :, -1:] - Acs)
    states = jnp.einsum('bclgn,bclgr,bclgrp->bcgrpn', Bc, decay_states, X)
    chunk_decay = jnp.exp(Acs[:, :, -1])

    def step(s, inp):
        dec, st = inp
        return s * dec[..., None, None] + st, s

    s0 = init_state.astype(f32).reshape(b, SSM_GROUPS, SSM_HPG, SSM_HEAD_DIM, SSM_STATE)
    final, prev = lax.scan(step, s0, (jnp.moveaxis(chunk_decay, 1, 0), jnp.moveaxis(states, 1, 0)))
    prev = jnp.moveaxis(prev, 0, 1)
    y_off = jnp.einsum('bclgn,bcgrpn,bclgr->bclgrp', Cc, prev, jnp.exp(Acs))
    y = (y_diag + y_off).reshape(b, l, SSM_HEADS, SSM_HEAD_DIM)
    return y.astype(xs.dtype), final.reshape(b, SSM_HEADS, SSM_HEAD_DIM, SSM_STATE).astype(xs.dtype)


def _layer(x, conv_buf, ssm_init, k_buf, v_buf, norm_w, w_in, conv_w, conv_b, dt_bias, a_log, d_skip,
           ssm_norm_w, w_ssm_branch, attn_sinks, w_attn_branch, w_out, rel_table):
    b, l, _ = x.shape
    h = _rmsnorm(x, norm_w)
    u = jnp.einsum('bld,de->ble', h, w_in)
    z = u[..., OFF_Z:OFF_XBC]
    xbc_raw = u[..., OFF_XBC:OFF_DT]
    dt_raw = u[..., OFF_DT:OFF_Q]
    q = u[..., OFF_Q:OFF_K].reshape(b, l, ATTN_KV_HEADS, ATTN_REP, ATTN_HEAD_DIM)
    k = u[..., OFF_K:OFF_V].reshape(b, l, ATTN_KV_HEADS, ATTN_HEAD_DIM)
    v = u[..., OFF_V:OFF_GA].reshape(b, l, ATTN_KV_HEADS, ATTN_HEAD_DIM)
    g_attn = u[..., OFF_GA:OFF_MIX_S]
    mix_s = u[..., OFF_MIX_S:OFF_MIX_A]
    mix_a = u[..., OFF_MIX_A:IN_COLS]

    xbc, new_conv = _causal_conv(xbc_raw, conv_buf, conv_w, conv_b)
    xs = xbc[..., :D_INNER].reshape(b, l, SSM_HEADS, SSM_HEAD_DIM)
    Bm = xbc[..., D_INNER:D_INNER + GN].reshape(b, l, SSM_GROUPS, SSM_STATE)
    Cm = xbc[..., D_INNER + GN:].reshape(b, l, SSM_GROUPS, SSM_STATE)
    dt = jax.nn.softplus(dt_raw.astype(jnp.float32) + dt_bias.astype(jnp.float32))
    A = -jnp.exp(a_log.astype(jnp.float32))
    chunk = SSD_CHUNK if l % SSD_CHUNK == 0 else l
    y, new_ssm = _ssd(xs, dt, A, Bm, Cm, ssm_init, chunk)
    y = y + d_skip[:, None] * xs
    y = _gated_group_rmsnorm(y.reshape(b, l, D_INNER), z, ssm_norm_w)
    p_ssm = jnp.einsum('bli,id->bld', y, w_ssm_branch)

    if k_buf is None:
        o = _banded_window_attention(q, k, v, rel_table, attn_sinks)
        new_k, new_v = k[:, -WINDOW:], v[:, -WINDOW:]
    else:
        kk = jnp.concatenate([k_buf.astype(k.dtype), k], axis=1)
        vv = jnp.concatenate([v_buf.astype(v.dtype), v], axis=1)
        o = _window_decode_attention(q, kk, vv, rel_table, attn_sinks)
        new_k, new_v = kk[:, -k_buf.shape[1]:], vv[:, -v_buf.shape[1]:]
    o = o.reshape(b, l, ATTN_WIDTH) * jax.nn.silu(g_attn)
    p_attn = jnp.einsum('bla,ad->bld', o, w_attn_branch)

    merged = jax.nn.sigmoid(mix_s) * p_ssm + jax.nn.sigmoid(mix_a) * p_attn
    x = x + jnp.einsum('bld,de->ble', merged, w_out)
    return x, new_conv, new_ssm, new_k, new_v


def setup_inputs(seed: int = 0) -> dict:
    key = jax.random.key(seed)
    ks = jax.random.split(key, 20)
    f32 = jnp.float32
    w_buf = min(WINDOW, PAST_LEN)

    def nrm(k, shape, scale):
        return scale * jax.random.normal(k, shape, f32)

    dt0 = jnp.exp(jax.random.uniform(ks[10], (DEPTH, SSM_HEADS), f32, math.log(1e-3), math.log(1e-1)))
    return {
        'x_prompt': nrm(ks[0], (BATCH, SEQ, D_MODEL), 1.0),
        'x_sample': nrm(ks[1], (DEC_BATCH, DEC_SEQ, D_MODEL), 1.0),
        'cache_k': nrm(ks[2], (DEPTH, DEC_BATCH, w_buf, ATTN_KV_HEADS, ATTN_HEAD_DIM), 1.0),
        'cache_v': nrm(ks[3], (DEPTH, DEC_BATCH, w_buf, ATTN_KV_HEADS, ATTN_HEAD_DIM), 1.0),
        'state_conv': nrm(ks[4], (DEPTH, DEC_BATCH, CONV_WIDTH - 1, CONV_DIM), 1.0),
        'state_ssm': nrm(ks[5], (DEPTH, DEC_BATCH, SSM_HEADS, SSM_HEAD_DIM, SSM_STATE), 0.1),
        'norm_w': 1.0 + nrm(ks[6], (DEPTH, D_MODEL), 0.02),
        'w_in': nrm(ks[7], (DEPTH, D_MODEL, IN_COLS), D_MODEL ** -0.5),
        'conv_w': nrm(ks[8], (DEPTH, CONV_WIDTH, CONV_DIM), CONV_WIDTH ** -0.5),
        'conv_b': nrm(ks[9], (DEPTH, CONV_DIM), 0.02),
        'dt_bias': dt0 + jnp.log(-jnp.expm1(-dt0)),
        'a_log': jnp.log(jax.random.uniform(ks[11], (DEPTH, SSM_HEADS), f32, 1.0, 16.0)),
        'd_skip': 1.0 + nrm(ks[12], (DEPTH, SSM_HEADS), 0.1),
        'ssm_norm_w': 1.0 + nrm(ks[13], (DEPTH, D_INNER), 0.02),
        'w_ssm_branch': nrm(ks[14], (DEPTH, D_INNER, D_MODEL), D_INNER ** -0.5),
        'attn_sinks': nrm(ks[15], (DEPTH, ATTN_Q_HEADS), 0.5),
        'w_attn_branch': nrm(ks[16], (DEPTH, ATTN_WIDTH, D_MODEL), ATTN_WIDTH ** -0.5),
        'w_out': nrm(ks[17], (DEPTH, D_MODEL, D_MODEL), D_MODEL ** -0.5),
        'rel_bias': nrm(ks[18], (N_BUCKETS, ATTN_Q_HEADS), 0.5),
        'final_norm_w': 1.0 + nrm(ks[19], (D_MODEL,), 0.02),
    }


def reference(x_prompt, x_sample, cache_k, cache_v, state_conv, state_ssm, norm_w, w_in, conv_w, conv_b,
              dt_bias, a_log, d_skip, ssm_norm_w, w_ssm_branch, attn_sinks, w_attn_branch, w_out,
              rel_bias, final_norm_w):
    xp, xs = x_prompt, x_sample
    kp_l, vp_l, cp_l, sp_l = [], [], [], []
    ks_l, vs_l, cs_l, ss_l = [], [], [], []
    for layer in range(DEPTH):
        weights = (norm_w[layer], w_in[layer], conv_w[layer], conv_b[layer], dt_bias[layer], a_log[layer],
                   d_skip[layer], ssm_norm_w[layer], w_ssm_branch[layer], attn_sinks[layer],
                   w_attn_branch[layer], w_out[layer], rel_bias)
        conv0 = jnp.zeros((xp.shape[0], CONV_WIDTH - 1, CONV_DIM), xp.dtype)
        ssm0 = jnp.zeros((xp.shape[0], SSM_HEADS, SSM_HEAD_DIM, SSM_STATE), xp.dtype)
        xp, c_p, s_p, k_p, v_p = _layer(xp, conv0, ssm0, None, None, *weights)
        xs, c_s, s_s, k_s, v_s = _layer(xs, state_conv[layer], state_ssm[layer], cache_k[layer],
                                        cache_v[layer], *weights)
        kp_l.append(k_p); vp_l.append(v_p); cp_l.append(c_p); sp_l.append(s_p)
        ks_l.append(k_s); vs_l.append(v_s); cs_l.append(c_s); ss_l.append(s_s)
    y_prompt = _rmsnorm(xp, final_norm_w)
    y_sample = _rmsnorm(xs, final_norm_w)
    return (y_prompt, y_sample, jnp.stack(kp_l), jnp.stack(vp_l), jnp.stack(cp_l), jnp.stack(sp_l),
            jnp.stack(ks_l), jnp.stack(vs_l), jnp.stack(cs_l), jnp.stack(ss_l))
```

```python
import math
import numpy as np
import concourse.bass as bass
import concourse.mybir as mybir
from concourse.bass_utils import run_bass_kernel_spmd

F32 = mybir.dt.float32
BF16 = mybir.dt.bfloat16
AF = mybir.ActivationFunctionType
ALU = mybir.AluOpType
AX = mybir.AxisListType

D = 2048
DI = 4096
NH = 64
HP = 64
NG = 8
NS = 128
GN = 1024
CONV_DIM = 6144
OFF_Z = 0
OFF_XBC = 4096
OFF_DT = OFF_XBC + CONV_DIM
OFF_Q = OFF_DT + 64
OFF_K = OFF_Q + 2048
OFF_V = OFF_K + 512
OFF_GA = OFF_V + 512
OFF_MS = OFF_GA + 2048
OFF_MA = OFF_MS + 2048
IN_COLS = OFF_MA + 2048
EPS = 1e-6
NEG = -30000.0
NCORES = 8
HALF = 1024
TM = 256


class Reg:
    __slots__ = ("name", "w", "r", "alias")

    def __init__(self, name):
        self.name = name
        self.w = None
        self.r = {}
        self.alias = []


def _deps(r, w):
    toks = []
    for x in r:
        toks.append(x.w)
        for y in x.alias:
            toks.append(y.w)
    for x in w:
        toks.append(x.w)
        toks.extend(x.r.values())
        for y in x.alias:
            toks.append(y.w)
            toks.extend(y.r.values())
    return toks


class Eng:
    EPOCH = 6000

    def __init__(self, K, name, h, inorder=True):
        self.K = K
        self.name = name
        self.h = h
        self.inorder = inorder
        self.n = 0
        self.sems = []
        self.waited = {}
        self.ring = []
        self.ringpos = 0
        self.ringtok = []

    def _wait(self, tok):
        if tok is None:
            return
        sem, val, src = tok
        key = id(sem)
        if self.waited.get(key, 0) >= val:
            return
        if src is self and (self.name == "pe" or self.K.nosame):
            return
        self.h.wait_ge(sem, val)
        self.waited[key] = val

    def emit(self, fn, r=(), w=()):
        if self.K.count >= self.K.maxi:
            return None
        self.K.count += 1
        for t in _deps(r, w):
            self._wait(t)
        ins = fn()
        if self.K.trace and self.K.trace[0] <= self.K.count <= self.K.trace[1]:
            print(self.K.count, str(ins)[:330])
        ep = self.n // self.EPOCH
        while len(self.sems) <= ep:
            self.sems.append(self.K.newsem(f"{self.name}{len(self.sems)}"))
        self.n += 1
        tok = (self.sems[ep], self.n - ep * self.EPOCH, self)
        ins.then_inc(tok[0], 1)
        for x in r:
            x.r[self.name] = tok
        for x in w:
            x.w = tok
            x.r = {}
        return tok

    def dma(self, out, in_, r=(), w=()):
        if self.K.count >= self.K.maxi:
            return None
        self.K.count += 1
        for t in _deps(r, w):
            self._wait(t)
        if not self.ring:
            for i in range(8):
                self.ring.append([self.K.newsem(f"{self.name}d{i}"), 0])
            self.ringtok = [None] * 8
        i = self.ringpos % len(self.ring)
        self.ringpos += 1
        if self.ringtok[i] is not None:
            self._wait(self.ringtok[i])
        slot = self.ring[i]
        if slot[1] + 16 > 16 * 2000:
            slot[0] = self.K.newsem(f"{self.name}d{i}x{self.ringpos}")
            slot[1] = 0
        slot[1] += 16
        ins = self.h.dma_start(out=out, in_=in_, max_dma_last_dim=8192) if self.name == "pool" else \
            self.h.dma_start(out=out, in_=in_)
        if self.K.trace and self.K.trace[0] <= self.K.count <= self.K.trace[1]:
            print(self.K.count, str(ins)[:330])
        ins.then_inc(slot[0], 16)
        tok = (slot[0], slot[1], None)
        self.ringtok[i] = tok
        self.K.dmacount += 1
        for x in r:
            x.r[f"{self.name}_dma{self.K.dmacount}"] = tok
        for x in w:
            x.w = tok
            x.r = {}
        return tok


class K:
    def __init__(self, nc, stack):
        self.nc = nc
        self.stack = stack
        self.dmacount = 0
        self.nsem = 0
        import os as _os
        self.count = 0
        self.nosame = _os.environ.get("KNOSAME", "0") == "1"
        self.maxi = int(_os.environ.get("KMAXI", "100000000"))
        tr = _os.environ.get("KTRACE")
        self.trace = [int(v) for v in tr.split(":")] if tr else None
        self.pe = Eng(self, "pe", nc.tensor)
        self.act = Eng(self, "act", nc.scalar)
        self.dve = Eng(self, "dve", nc.vector)
        self.pool = Eng(self, "pool", nc.gpsimd)
        self.sp = Eng(self, "sp", nc.sync)
        self.regs = {}
        self.nbank = 0

    def newsem(self, name):
        self.nsem += 1
        return self.stack.enter_context(self.nc.semaphore(f"s_{name}_{self.nsem}"))

    def sb(self, name, shape, dt=F32):
        t = self.stack.enter_context(self.nc.sbuf_tensor("sb_" + name, list(shape), dt))
        return t

    def reg(self, name):
        if name not in self.regs:
            self.regs[name] = Reg(name)
        return self.regs[name]

    def link(self, a, b):
        ra, rb = self.reg(a), self.reg(b)
        if rb not in ra.alias:
            ra.alias.append(rb)
        if ra not in rb.alias:
            rb.alias.append(ra)


def _bucket_table():
    out = []
    for d in range(0, 129):
        if d < 16:
            out.append(d)
        else:
            v = np.float32(np.log(np.float32(d) / np.float32(16.0))) / np.float32(math.log(128 / 16)) * np.float32(16)
            out.append(min(16 + int(np.int32(v)), 31))
    return out


def _host_consts(nb):
    L = 128 // nb
    idx = np.arange(128)
    same = (idx[:, None] // L) == (idx[None, :] // L)
    c = {}
    c["ident"] = np.eye(128, dtype=np.float32)
    c["ut"] = ((idx[:, None] <= idx[None, :]) & same).astype(np.float32)
    c["sl"] = ((idx[:, None] > idx[None, :]) & same).astype(np.float32)
    c["cm"] = ((idx[None, :] >= idx[:, None]) & same).astype(np.float32)
    c["same"] = same.astype(np.float32)
    return np.stack([c["ident"], c["ut"], c["sl"], c["cm"], c["same"]], axis=1).astype(np.float32)


def _win_blocks():
    bl = []
    for i in range(16):
        bl.append((OFF_Z + 256 * i, 256))
        bl.append((OFF_XBC + 256 * i, 256))
    for g in range(8):
        bl.append((OFF_XBC + DI + 128 * g, 128))
        bl.append((OFF_XBC + DI + GN + 128 * g, 128))
    bl.append((OFF_DT, 64))
    for i in range(8):
        bl.append((OFF_Q + 256 * i, 256))
        bl.append((OFF_GA + 256 * i, 256))
        bl.append((OFF_MS + 256 * i, 256))
        bl.append((OFF_MA + 256 * i, 256))
        bl.append((OFF_K + 64 * i, 64))
    for jp in range(4):
        bl.append((OFF_K + 128 * jp, 128))
        bl.append((OFF_V + 128 * jp, 128))
    off, table = 0, {}
    for (c0, n) in bl:
        table[(c0, n)] = off
        off += 16 * n
    return bl, table, off


_WBL, _WTAB, _WTOT = _win_blocks()


class _WinProxy:
    def __init__(self, ap):
        self.ap = ap

    def __getitem__(self, key):
        sl = key[1]
        c0, n = sl.start, sl.stop - sl.start
        off = _WTAB[(c0, n)]
        blk = self.ap[:, off:off + 16 * n].rearrange("p (k c) -> p k c", k=16)

        class _V:
            def rearrange(self_, *a, **kw):
                return blk
        return _V()

def build_program(do_sample=True):
    from contextlib import ExitStack
    nc = bass.Bass("TRN2", target_bir_lowering=False)
    stack = ExitStack()
    k = K(nc, stack)
    pe, act, dve, pool, sp = k.pe, k.act, k.dve, k.pool, k.sp

    def din(name, shape, dt=F32):
        return nc.dram_tensor(name, list(shape), dt, kind="ExternalInput").ap()

    def dout(name, shape, dt=F32):
        return nc.dram_tensor(name, list(shape), dt, kind="ExternalOutput").ap()

    xm = din("xm", [HALF, D])
    xp = din("xp", [HALF, D])
    xs = din("xs", [128, D])
    flag_d = din("flag", [128, 2])
    w_in = _WinProxy(din("w_in_t", [128, _WTOT]))
    w_ssm = din("w_ssm", [DI, D])
    w_attn = din("w_attn", [D, D])
    w_out = din("w_out", [D, D])
    normw_d = din("normw", [128, 16])
    cw_d = din("cw", [128, 48, 4])
    cb_d = din("cb", [128, 48])
    snw_d = din("snw", [128, 32])
    dtb_d = din("dtb", [64])
    alog_d = din("alog", [64])
    dsk_d = din("dsk", [64])
    sink_d = din("sinks", [32])
    fnw_d = din("fnw", [D])
    relb_d = din("relb", [32, 32])
    sel_d = din("sel", [32, 383])
    negm_d = din("negm", [128, 383])
    cst_d = din("cst", [128, 5, 128])
    cstb_d = din("cstb", [128, 5, 128])
    bsel_d = din("bsel", [128, 16])
    cmask_d = din("cmask", [128, 2048])
    sinkS_d = din("sinkS", [32, 8])
    ck_d = din("ck", [16, 128, 512])
    cv_d = din("cv", [16, 128, 512])
    sconv_d = din("sconv", [48, CONV_DIM])
    sssm_d = din("sssm", [16, DI, NS])

    y_main = dout("y_main", [HALF, D])
    y_samp = dout("y_samp", [128, D])
    k_p = dout("k_p", [128, 512])
    v_p = dout("v_p", [128, 512])
    conv_p = dout("conv_p", [3, CONV_DIM])
    ssm_p = dout("ssm_p", [DI, NS])
    k_s = dout("k_s", [16, 128, 512])
    v_s = dout("v_s", [16, 128, 512])
    conv_s = dout("conv_s", [48, CONV_DIM])
    ssm_s = dout("ssm_s", [16, DI, NS])
    scr = nc.dram_tensor("scr", [32, 128, 383], F32).ap()

    hT = k.sb("hT", [128, 16, TM], BF16)
    ynT = k.sb("ynT", [128, 32, TM], BF16)
    oT = k.sb("oT", [128, 16, TM], BF16)
    mT = k.sb("mT", [128, 16, TM], BF16)
    state = k.sb("state", [128, NG, 512], F32)
    state_bf = k.sb("state_bf", [128, 512], BF16)
    NSLOT = 3
    wslot = [k.sb(f"wslot{i}", [128, 4096], BF16) for i in range(NSLOT)]
    bias = k.sb("bias", [128, 32, 256], F32)
    cst = k.sb("cst", [128, 5, 128], F32)
    cst_bf = k.sb("cst_bf", [128, 128], BF16)
    normw = k.sb("normw", [128, 16], F32)
    cw = k.sb("cw", [128, 48, 4], F32)
    cb = k.sb("cb", [128, 48], F32)
    snw = k.sb("snw", [128, 32], F32)
    dtb = k.sb("dtb", [128, 64], F32)
    Arow = k.sb("Arow", [128, 64], F32)
    Drow = k.sb("Drow", [128, 64], F32)
    sinkb = k.sb("sinkb", [128, 32], F32)
    fnw = k.sb("fnw", [128, D], F32)
    flag = k.sb("flag", [128, 2], F32)
    carry = k.sb("carry", [128, 3, 48], F32)
    cstage = [k.sb(f"cstage{i}", [128, 128], F32) for i in range(2)]
    ctmp = k.sb("ctmp", [128, 48], F32)
    kcar = k.sb("kcar", [128, 8, 128], BF16)
    vcar = k.sb("vcar", [128, 512], BF16)
    xt = k.sb("xt", [128, D], F32)
    xn = k.sb("xn", [128, D], F32)
    st4 = k.sb("st4", [128, 8], F32)
    zsbig = k.sb("zsbig", [128, 1024], F32)
    zsL = [zsbig[:, 0:512], zsbig[:, 512:1024]]
    ktok = zsbig
    k.link("ktok", "zs0")
    k.link("ktok", "zs1")
    xraw = k.sb("xraw", [128, 4, 260], F32)
    xaccL = [k.sb(f"xacc{i}", [128, TM], F32) for i in range(2)]
    xcTL = [k.sb(f"xcT{i}", [128, 4, TM], BF16) for i in range(2)]
    BTL = [k.sb(f"BT{i}", [128, TM], BF16) for i in range(2)]
    CTL = [k.sb(f"CT{i}", [128, TM], BF16) for i in range(2)]
    dtt = k.sb("dtt", [128, TM // 128, 64], F32)
    dAt = k.sb("dAt", [128, TM // 128, 64], F32)
    et = k.sb("et", [128, 64], F32)
    cdt = k.sb("cdt", [128, 64], F32)
    wsgL = [k.sb(f"wsg{i}", [128, 32], F32) for i in range(2)]
    dtwL = [k.sb(f"dtw{i}", [128, 8], F32) for i in range(2)]
    XtmL = [k.sb(f"Xtm{i}", [128, 512], F32) for i in range(2)]
    XdtL = [k.sb(f"Xdt{i}", [128, 512], BF16) for i in range(2)]
    wXL = [k.sb(f"wX{i}", [128, 512], BF16) for i in range(2)]
    BtmL = [k.sb(f"Btm{i}", [128, 128], BF16) for i in range(2)]
    cur = {"ci": 0}
    rseg = k.sb("rseg", [128, 8, 128], F32)
    LT = k.sb("LT", [128, 8, 128], F32)
    MT = k.sb("MT", [128, 8, 128], BF16)
    CBm = k.sb("CBm", [128, 128], F32)
    ta = k.sb("ta", [128, 512], F32)
    tb = k.sb("tb", [128, 512], F32)
    tc_ = k.sb("tc", [128, 512], F32)
    junkE = ynT[:].rearrange("p k t -> p (k t)")[:, 0:D]
    QT = k.sb("QT", [128, 4, TM], BF16)
    KT = k.sb("KT", [128, 2, 128 + TM], BF16)
    Vt = k.sb("Vt", [128, 1 + TM // 128, 128], BF16)
    gaT = k.sb("gaT", [128, 4, TM], BF16)
    NAB = 4
    sc = [k.sb(f"sc{i}", [128, 256], F32) for i in range(NAB)]
    Pn = [k.sb(f"Pn{i}", [128, 256], BF16) for i in range(NAB)]
    PT = [k.sb(f"PT{i}", [128, 256], BF16) for i in range(NAB)]
    sm = k.sb("sm", [128, 4 * NAB], F32)
    sgs = k.sb("sgs", [128, TM], F32)
    sga = k.sb("sga", [128, TM], F32)
    sgt = k.sb("sgt", [128, TM], F32)
    xo = [xt, xn]

    banks = [stack.enter_context(nc.psum_tensor(f"bank{i}", [128, 512], F32)) for i in range(7)]
    bankb = stack.enter_context(nc.psum_tensor("bankb", [128, 1024], BF16))
    RB = [k.reg(f"bank{i}") for i in range(7)]
    RBb = k.reg("bankb")
    for i_ in range(4):
        k.link("bankb", f"bankbP{i_}")
    k.link("bankb", "bankbK")
    for i_ in range(4):
        k.link("bankb", f"bankbS{i_}")
        k.link("bankbP2", f"bankbS{i_}")
    for i_ in range(2):
        k.link("bankb", f"bankbK{i_}")
        k.link("bankbP0", f"bankbK{i_}")
        k.link("bankbP1", f"bankbK{i_}")
    k.link("bankbP0", "bankbK")

    reserved = set()

    def nb_():
        while True:
            i = k.nbank % 7
            k.nbank += 1
            if i not in reserved:
                return banks[i], RB[i]

    def bank_index(pb):
        for i, b in enumerate(banks):
            if b is pb:
                return i
        raise ValueError

    R = k.reg

    def ld(dst, src, name):
        sp.dma(dst, src, w=[R(name)])

    ld(cst[:], cst_d, "cst")
    ld(normw[:], normw_d, "normw")
    ld(cw[:], cw_d, "cw")
    ld(cb[:], cb_d, "cb")
    ld(snw[:], snw_d, "snw")
    ld(dtb[:], dtb_d.partition_broadcast(128), "dtb")
    ld(Arow[:], alog_d.partition_broadcast(128), "Arow")
    ld(Drow[:], dsk_d.partition_broadcast(128), "Drow")
    ld(sinkb[:], sink_d.partition_broadcast(128), "sinkb")
    ld(fnw[:], fnw_d.partition_broadcast(128), "fnw")
    ld(flag[:], flag_d, "flag")
    ident = cst[:, 0, :]
    act.emit(lambda: nc.scalar.activation(out=Arow[:], in_=Arow[:], func=AF.Exp), r=[R("Arow")], w=[R("Arow")])
    dve.emit(lambda: nc.vector.tensor_scalar(out=Arow[:], in0=Arow[:], scalar1=-1.0, scalar2=None, op0=ALU.mult),
             r=[R("Arow")], w=[R("Arow")])
    dve.emit(lambda: nc.vector.tensor_copy(out=cst_bf[:], in_=cst[:, 0, :]), r=[R("cst")], w=[R("cst_bf")])
    dve.emit(lambda: nc.vector.memset(carry[:], 0.0), w=[R("carry")])
    dve.emit(lambda: nc.vector.memset(state[:], 0.0), w=[R("state")])
    dve.emit(lambda: nc.vector.memset(kcar[:], 0.0), w=[R("kcar")])
    dve.emit(lambda: nc.vector.memset(vcar[:], 0.0), w=[R("vcar")])

    ynf = ynT[:].rearrange("p k t -> p (k t)").bitcast(F32)
    relb = ynf[0:32, 0:32]
    relbb = ynf[0:32, 32:160]
    sel = ynf[0:32, 160:543]
    negm = ynf[:, 544:927]
    grow = [ynf[:, 928:1311], ynf[:, 1312:1695]]
    for nm in ("relb", "relbb", "sel", "negm", "grow0", "grow1"):
        k.link("ynT", nm)
    ld(relb, relb_d, "relb")
    ld(sel, sel_d, "sel")
    ld(negm, negm_d, "negm")
    for h in range(32):
        gi = h % 2
        dve.emit(lambda: nc.vector.tensor_copy(out=relbb, in_=relb[:, h:h + 1].to_broadcast([32, 128])),
                 r=[R("relb")], w=[R("relbb")])
        pb, rb = nb_()
        pe.emit(lambda: nc.tensor.matmul(pb[:, 0:383], lhsT=relbb, rhs=sel, start=True, stop=True),
                r=[R("relbb"), R("sel")], w=[rb])
        dve.emit(lambda: nc.vector.tensor_tensor(out=grow[gi], in0=pb[:, 0:383], in1=negm, op=ALU.add),
                 r=[rb, R("negm")], w=[R(f"grow{gi}")])
        sp.dma(scr[h], grow[gi], r=[R(f"grow{gi}")], w=[R(f"scr{h}")])
        src = bass.AP(tensor=scr.tensor, offset=h * 128 * 383 + 127, ap=[[382, 128], [1, 256]])
        sp.dma(bias[:, h, :], src, r=[R(f"scr{h}")], w=[R("bias")])

    wstate = {"i": 0}

    def wload_multi(parts):
        i = wstate["i"] % NSLOT
        wstate["i"] += 1
        reg = R(f"wslot{i}")
        toks = []
        prev_w, prev_r = reg.w, dict(reg.r)
        for part in parts:
            src, view = part[0], part[1]
            pp = part[2] if len(part) > 2 else 128
            reg.w, reg.r = prev_w, dict(prev_r)
            toks.append(pool.dma(view(wslot[i]), src.rearrange("(k p) c -> p k c", p=pp), w=[reg]))
        reg.w = toks[-1]
        reg.r = {}
        return wslot[i], reg, [t for t in toks[:-1] if t is not None]

    def wview(kt, cols, coff=0, tot=None):
        tot = tot or cols

        def f(slot):
            return slot[:, 0:kt * tot].rearrange("p (k c) -> p k c", k=kt)[:, :, coff:coff + cols]
        return f

    def wait_extra(eng, toks):
        for t in toks:
            eng._wait(t)

    def rstd_from_ss(ss_ap, n, out_ap, rname):
        dve.emit(lambda: nc.vector.tensor_scalar(out=out_ap, in0=ss_ap, scalar1=1.0 / n, scalar2=EPS, op0=ALU.mult,
                                                 op1=ALU.add), r=[R(rname)], w=[R(rname)])
        act.emit(lambda: nc.scalar.activation(out=out_ap, in_=out_ap, func=AF.Sqrt), r=[R(rname)], w=[R(rname)])
        dve.emit(lambda: nc.vector.reciprocal(out=out_ap, in_=out_ap), r=[R(rname)], w=[R(rname)])

    def phase_a(xsrc, ntile):
        for i in range(ntile):
            sp.dma(xt[:], xsrc[i * 128:(i + 1) * 128, :], w=[R("xt")])
            act.emit(lambda: nc.scalar.activation(out=xn[:], in_=xt[:], func=AF.Square, accum_out=st4[:, 0:1]),
                     r=[R("xt")], w=[R("xn"), R("st4")])
            rstd_from_ss(st4[:, 0:1], D, st4[:, 1:2], "st4")
            dve.emit(lambda: nc.vector.tensor_scalar(out=xn[:], in0=xt[:], scalar1=st4[:, 1:2], scalar2=None,
                                                     op0=ALU.mult), r=[R("xt"), R("st4")], w=[R("xn")])
            for kk in range(0, 16, 4):
                pb, rb = nb_()
                for j in range(4):
                    pe.emit(lambda: nc.tensor.transpose(out=pb[:, j * 128:(j + 1) * 128],
                                                        in_=xn[:, (kk + j) * 128:(kk + j + 1) * 128], identity=ident),
                            r=[R("xn"), R("cst")], w=[rb])
                for j in range(4):
                    eng = act if (kk // 4) % 2 == 0 else dve
                    if eng is act:
                        act.emit(lambda: nc.scalar.activation(out=hT[:, kk + j, i * 128:(i + 1) * 128],
                                                              in_=pb[:, j * 128:(j + 1) * 128], func=AF.Identity,
                                                              scale=normw[:, kk + j:kk + j + 1]),
                                 r=[rb, R("normw")], w=[R("hT")])
                    else:
                        dve.emit(lambda: nc.vector.tensor_scalar(out=hT[:, kk + j, i * 128:(i + 1) * 128],
                                                                 in0=pb[:, j * 128:(j + 1) * 128],
                                                                 scalar1=normw[:, kk + j:kk + j + 1], scalar2=None,
                                                                 op0=ALU.mult), r=[rb, R("normw")], w=[R("hT")])

    def proj_F(wsl, wreg, wv, ncol_tiles, T, evac):
        for ct in range(ncol_tiles):
            pb, rb = nb_()
            for kk in range(16):
                pe.emit(lambda: nc.tensor.matmul(pb[:, 0:T], lhsT=wv[:, kk, ct * 128:(ct + 1) * 128],
                                                 rhs=hT[:, kk, 0:T], start=(kk == 0), stop=(kk == 15)),
                        r=[wreg, R("hT")], w=[rb])
            evac(ct, pb, rb)

    def proj_T(wreg, wv, ncols, tok0, evac):
        pb, rb = nb_()
        for kk in range(16):
            pe.emit(lambda: nc.tensor.matmul(pb[:, 0:ncols], lhsT=hT[:, kk, tok0:tok0 + 128], rhs=wv[:, kk, 0:ncols],
                                             start=(kk == 0), stop=(kk == 15)), r=[wreg, R("hT")], w=[rb])
        evac(pb, rb)

    def conv_tile(ct_global, pb, rb, T, nblk, dst, dstreg, slot6, ctx_src=None):
        Lb = T // nblk
        xr = xraw[:, slot6, 0:nblk * (Lb + 3)].rearrange("p (b l) -> p b l", b=nblk)
        rr = R(f"xraw{slot6}")
        if nblk == 1:
            dve.emit(lambda: nc.vector.tensor_copy(out=xr[:, 0, 0:3], in_=carry[:, :, ct_global]),
                     r=[R("carry")], w=[rr])
        else:
            ctx_src(xr, rr)
        act.emit(lambda: nc.scalar.copy(out=xr[:, :, 3:3 + Lb], in_=pb[:, 0:T].rearrange("p (b l) -> p b l", b=nblk)),
                 r=[rb], w=[rr])
        if nblk == 1:
            dve.emit(lambda: nc.vector.tensor_copy(out=carry[:, :, ct_global], in_=xr[:, 0, Lb:Lb + 3]),
                     r=[rr], w=[R("carry")])
        else:
            cs_i = ct_global % 2
            dve.emit(lambda: nc.vector.tensor_copy(out=ctmp[:].rearrange("p (b t) -> p b t", t=3),
                                                   in_=xr[:, :, Lb:Lb + 3]), r=[rr], w=[R("ctmp")])
            pbc, rbc = nb_()
            pe.emit(lambda: nc.tensor.transpose(out=pbc[0:48, 0:128], in_=ctmp[:], identity=ident),
                    r=[R("ctmp"), R("cst")], w=[rbc])
            act.emit(lambda: nc.scalar.copy(out=cstage[cs_i][0:48, :], in_=pbc[0:48, 0:128]), r=[rbc],
                     w=[R(f"cstage{cs_i}")])
            act.dma(conv_s[:, ct_global * 128:(ct_global + 1) * 128], cstage[cs_i][0:48, :],
                    r=[R(f"cstage{cs_i}")], w=[R("convsout")])
        xi_ = cur.get("xi", 0) % 2
        cur["xi"] = cur.get("xi", 0) + 1
        xacc = xaccL[xi_]
        rxa = R(f"xacc{xi_}")
        xa = xacc[:, 0:T].rearrange("p (b l) -> p b l", b=nblk)
        dve.emit(lambda: nc.vector.tensor_scalar(out=xa, in0=xr[:, :, 0:Lb], scalar1=cw[:, ct_global, 0:1],
                                                 scalar2=None, op0=ALU.mult), r=[rr, R("cw")], w=[rxa])
        for i in range(1, 4):
            dve.emit(lambda: nc.vector.scalar_tensor_tensor(out=xa, in0=xr[:, :, i:i + Lb],
                                                            scalar=cw[:, ct_global, i:i + 1], in1=xa, op0=ALU.mult,
                                                            op1=ALU.add), r=[rr, R("cw"), rxa], w=[rxa])
        act.emit(lambda: nc.scalar.activation(out=dst, in_=xacc[:, 0:T], func=AF.Silu,
                                              bias=cb[:, ct_global:ct_global + 1], scale=1.0),
                 r=[rxa, R("cb")], w=[dstreg])

    def ssd_phase(T, mode, sample_ctx=None):
        ntile = T // 128
        sample = mode == "sample"
        C = cstb if sample else cst
        creg = R("cstb") if sample else R("cst")
        UT, SL, CM = C[:, 1, :], C[:, 2, :], C[:, 3, :]
        nblk = 16 if sample else 1
        wsl, wreg, ex = wload_multi([(w_in[:, OFF_DT:OFF_DT + 64], wview(16, 64))])
        wv = wview(16, 64)(wsl)
        for i in range(ntile):
            def ev(pb, rb, i=i):
                dve.emit(lambda: nc.vector.tensor_tensor(out=dtt[:, i, :], in0=pb[:, 0:64], in1=dtb[:], op=ALU.add),
                         r=[rb, R("dtb")], w=[R("dtt")])
                dve.emit(lambda: nc.vector.scalar_tensor_tensor(out=et[:], in0=dtt[:, i, :], scalar=-1.0,
                                                                in1=dtt[:, i, :], op0=ALU.mult, op1=ALU.max),
                         r=[R("dtt")], w=[R("et")])
                act.emit(lambda: nc.scalar.activation(out=et[:], in_=et[:], func=AF.Exp, scale=-1.0),
                         r=[R("et")], w=[R("et")])
                act.emit(lambda: nc.scalar.activation(out=et[:], in_=et[:], func=AF.Ln, bias=1.0, scale=1.0),
                         r=[R("et")], w=[R("et")])
                dve.emit(lambda: nc.vector.scalar_tensor_tensor(out=dtt[:, i, :], in0=dtt[:, i, :], scalar=0.0,
                                                                in1=et[:], op0=ALU.max, op1=ALU.add),
                         r=[R("dtt"), R("et")], w=[R("dtt")])
                dve.emit(lambda: nc.vector.tensor_tensor(out=dAt[:, i, :], in0=dtt[:, i, :], in1=Arow[:],
                                                         op=ALU.mult), r=[R("dtt"), R("Arow")], w=[R("dAt")])
            proj_T(wreg, wv, 64, i * 128, ev)

        def gen_A(g, par):
            xcT, BT, CT = xcTL[par], BTL[par], CTL[par]
            rxcT, rBT, rCT = R(f"xcT{par}"), R(f"BT{par}"), R(f"CT{par}")
            for half in range(2):
                c0 = OFF_XBC + 512 * g + 256 * half
                wsl, wreg, _ = wload_multi([(w_in[:, c0:c0 + 256], wview(16, 256))])
                wv = wview(16, 256)(wsl)

                def evx(ct, pb, rb, half=half):
                    ctg = 4 * g + 2 * half + ct
                    conv_tile(ctg, pb, rb, T, nblk, xcT[:, 2 * half + ct, 0:T], rxcT, ct % 2,
                              ctx_src=(lambda xr, rr, ctg=ctg: sample_ctx(ctg, xr, rr)) if sample else None)
                proj_F(wsl, wreg, wv, 2, T, evx)
                yield
            cB = OFF_XBC + DI + 128 * g
            cC = OFF_XBC + DI + GN + 128 * g
            wsl, wreg, ex = wload_multi([(w_in[:, cB:cB + 128], wview(16, 128, 0, 256)),
                                         (w_in[:, cC:cC + 128], wview(16, 128, 128, 256))])
            wait_extra(pe, ex)
            wv = wview(16, 256)(wsl)

            def evbc(ct, pb, rb):
                ctg = (32 + g) if ct == 0 else (40 + g)
                conv_tile(ctg, pb, rb, T, nblk, (BT if ct == 0 else CT)[:, 0:T], rBT if ct == 0 else rCT,
                          2 + ct, ctx_src=(lambda xr, rr, ctg=ctg: sample_ctx(ctg, xr, rr)) if sample else None)
            proj_F(wsl, wreg, wv, 2, T, evbc)
            yield
            if mode != "state":
                c0 = OFF_Z + 512 * g
                wslz = []
                for half in range(2):
                    wsl, wreg, _ = wload_multi([(w_in[:, c0 + 256 * half:c0 + 256 * (half + 1)], wview(16, 256))])
                    wslz.append((wsl, wreg))
                zw[par] = wslz
            yield


        def gen_B(g, par):
            xcT, BT, CT = xcTL[par], BTL[par], CTL[par]
            rxcT, rBT, rCT = R(f"xcT{par}"), R(f"BT{par}"), R(f"CT{par}")
            cur.update(CT=CT, rCT=rCT)
            wslz = zw.get(par)
            if mode != "state":
                for c in range(ntile):
                    t0 = c * 128
                    zs = zsL[c]
                    rzs = R(f"zs{c}")
                    for half in range(2):
                        wsl, wreg = wslz[half]
                        wv = wview(16, 256)(wsl)

                        def evz(pb, rb, half=half, zs=zs, rzs=rzs):
                            act.emit(lambda: nc.scalar.activation(out=zs[:, 256 * half:256 * (half + 1)],
                                                                  in_=pb[:, 0:256], func=AF.Silu),
                                     r=[rb], w=[rzs])
                        proj_T(wreg, wv, 256, t0, evz)
            for c in range(ntile):
                t0 = c * 128
                ci = cur["ci"] % 2
                cur["ci"] += 1
                zs, wsg, dtw, Xtm, Xdt, wX, Btm = zsL[c], wsgL[ci], dtwL[ci], XtmL[ci], XdtL[ci], wXL[ci], BtmL[ci]
                rzs, rwsg, rdtw, rXtm, rXdt, rwX, rBtm = (R(f"zs{c}"), R(f"wsg{ci}"), R(f"dtw{ci}"), R(f"Xtm{ci}"),
                                                          R(f"Xdt{ci}"), R(f"wX{ci}"), R(f"Btm{ci}"))
                cur.update(Btm=Btm, rBtm=rBtm, wX=wX, rwX=rwX)
                yield
                pbS, rbS = nb_()
                dA_g = dAt[:, c, 8 * g:8 * g + 8]
                pe.emit(lambda: nc.tensor.matmul(pbS[:, 0:8], lhsT=SL, rhs=dA_g, start=True, stop=True),
                        r=[creg, R("dAt")], w=[rbS])
                pe.emit(lambda: nc.tensor.matmul(pbS[:, 8:16], lhsT=UT, rhs=dA_g, start=True, stop=True),
                        r=[creg, R("dAt")], w=[rbS])
                if not sample:
                    pe.emit(lambda: nc.tensor.matmul(pbS[:, 16:24], lhsT=C[:, 4, :], rhs=dA_g, start=True, stop=True),
                            r=[creg, R("dAt")], w=[rbS])
                ncs = 16 if sample else 24
                act.emit(lambda: nc.scalar.activation(out=wsg[:, 0:ncs], in_=pbS[:, 0:ncs], func=AF.Exp),
                         r=[rbS], w=[rwsg])
                dve.emit(lambda: nc.vector.tensor_tensor(out=dtw[:, 0:8], in0=dtt[:, c, 8 * g:8 * g + 8],
                                                         in1=wsg[:, 0:8], op=ALU.mult),
                         r=[R("dtt"), rwsg], w=[rdtw])
                pbX, rbX = nb_()
                for j in range(4):
                    pe.emit(lambda: nc.tensor.transpose(out=bankb[:, j * 128:(j + 1) * 128],
                                                        in_=xcT[:, j, t0:t0 + 128], identity=cst_bf[:]),
                            r=[rxcT, R("cst_bf")], w=[RBb])
                pe.emit(lambda: nc.tensor.transpose(out=bankb[:, 512:640], in_=BT[:, t0:t0 + 128], identity=cst_bf[:]),
                        r=[rBT, R("cst_bf")], w=[RBb])
                act.emit(lambda: nc.scalar.copy(out=Xtm[:], in_=bankb[:, 0:512]), r=[RBb], w=[rXtm])
                act.emit(lambda: nc.scalar.copy(out=Btm[:], in_=bankb[:, 512:640]), r=[RBb], w=[rBtm])
                X3 = Xtm[:].rearrange("p (h d) -> p h d", h=8)
                dve.emit(lambda: nc.vector.tensor_tensor(out=wX[:].rearrange("p (h d) -> p h d", h=8), in0=X3,
                                                         in1=dtw[:, 0:8].unsqueeze(2).to_broadcast([128, 8, 64]),
                                                         op=ALU.mult), r=[rXtm, rdtw], w=[rwX])
                yield
                if mode != "state":
                    dve.emit(lambda: nc.vector.tensor_tensor(out=Xdt[:].rearrange("p (h d) -> p h d", h=8), in0=X3,
                                                             in1=dtt[:, c, 8 * g:8 * g + 8].unsqueeze(2).to_broadcast(
                                                                 [128, 8, 64]), op=ALU.mult),
                             r=[rXtm, R("dtt")], w=[rXdt])
                    dve.emit(lambda: nc.vector.tensor_tensor(out=rseg[:],
                                                             in0=dA_g.unsqueeze(2).to_broadcast([128, 8, 128]),
                                                             in1=UT.unsqueeze(1).to_broadcast([128, 8, 128]),
                                                             op=ALU.mult), r=[R("dAt"), creg], w=[R("rseg")])
                    pbs = []
                    for hh in range(2):
                        pb, rb = nb_()
                        pbs.append((pb, rb))
                        pe.emit(lambda: nc.tensor.matmul(pb[:], lhsT=SL,
                                                         rhs=rseg[:, 4 * hh:4 * hh + 4, :].rearrange("p h l -> p (h l)"),
                                                         start=True, stop=True), r=[creg, R("rseg")], w=[rb])
                    for hh in range(2):
                        pb, rb = pbs[hh]
                        act.emit(lambda: nc.scalar.activation(
                            out=LT[:, 4 * hh:4 * hh + 4, :].rearrange("p h l -> p (h l)"), in_=pb[:], func=AF.Exp),
                            r=[rb], w=[R("LT")])
                    yield
                    pbC, rbC = nb_()
                    pe.emit(lambda: nc.tensor.matmul(pbC[:, 0:128], lhsT=BT[:, t0:t0 + 128], rhs=CT[:, t0:t0 + 128],
                                                     start=True, stop=True), r=[rBT, rCT], w=[rbC])
                    dve.emit(lambda: nc.vector.tensor_tensor(out=CBm[:], in0=pbC[:, 0:128], in1=CM, op=ALU.mult),
                             r=[rbC, creg], w=[R("CBm")])
                    dve.emit(lambda: nc.vector.tensor_tensor(out=MT[:], in0=LT[:],
                                                             in1=CBm[:].unsqueeze(1).to_broadcast([128, 8, 128]),
                                                             op=ALU.mult), r=[R("LT"), R("CBm")], w=[R("MT")])
                    pbY, rbY = nb_()
                    for h in range(8):
                        pe.emit(lambda: nc.tensor.matmul(pbY[:, h * 64:(h + 1) * 64], lhsT=MT[:, h, :],
                                                         rhs=Xdt[:, h * 64:(h + 1) * 64], start=True, stop=True),
                                r=[R("MT"), rXdt], w=[rbY])
                    pbO, rbO = nb_()
                    if not sample:
                        dve.emit(lambda: nc.vector.tensor_copy(out=state_bf[:], in_=state[:, g, :]),
                                 r=[R("state")], w=[R("state_bf")])
                        pe.emit(lambda: nc.tensor.matmul(pbO[:], lhsT=CT[:, t0:t0 + 128], rhs=state_bf[:], start=True,
                                                         stop=True), r=[rCT, R("state_bf")], w=[rbO])
                    else:
                        reserved.add(bank_index(pbY))
                        sample_ctx("yoff", g, pbO, rbO)
                        reserved.discard(bank_index(pbY))
                    dve.emit(lambda: nc.vector.tensor_tensor(out=ta[:].rearrange("p (h d) -> p h d", h=8),
                                                             in0=pbO[:].rearrange("p (h d) -> p h d", h=8),
                                                             in1=wsg[:, 8:16].unsqueeze(2).to_broadcast([128, 8, 64]),
                                                             op=ALU.mult), r=[rbO, rwsg], w=[R("ta")])
                    dve.emit(lambda: nc.vector.tensor_tensor(out=ta[:], in0=ta[:], in1=pbY[:], op=ALU.add),
                             r=[rbY, R("ta")], w=[R("ta")])
                    dve.emit(lambda: nc.vector.tensor_tensor(out=tb[:].rearrange("p (h d) -> p h d", h=8), in0=X3,
                                                             in1=Drow[:, 8 * g:8 * g + 8].unsqueeze(2).to_broadcast(
                                                                 [128, 8, 64]), op=ALU.mult),
                             r=[rXtm, R("Drow")], w=[R("tb")])
                    dve.emit(lambda: nc.vector.tensor_tensor(out=ta[:], in0=ta[:], in1=tb[:], op=ALU.add),
                             r=[R("ta"), R("tb")], w=[R("ta")])
                    yield
                    dve.emit(lambda: nc.vector.tensor_tensor(out=ta[:], in0=ta[:], in1=zs[:], op=ALU.mult),
                             r=[R("ta"), rzs], w=[R("ta")])
                    act.emit(lambda: nc.scalar.activation(out=tb[:], in_=ta[:], func=AF.Square,
                                                          accum_out=st4[:, 2:3]), r=[R("ta")], w=[R("tb"), R("st4")])
                    rstd_from_ss(st4[:, 2:3], 512, st4[:, 3:4], "st4")
                    dve.emit(lambda: nc.vector.tensor_scalar(out=tc_[:], in0=ta[:], scalar1=st4[:, 3:4], scalar2=None,
                                                             op0=ALU.mult), r=[R("ta"), R("st4")], w=[R("tc")])
                    pbT, rbT = nb_()
                    for j in range(4):
                        pe.emit(lambda: nc.tensor.transpose(out=pbT[:, j * 128:(j + 1) * 128],
                                                            in_=tc_[:, j * 128:(j + 1) * 128], identity=ident),
                                r=[R("tc"), R("cst")], w=[rbT])
                    for j in range(4):
                        act.emit(lambda: nc.scalar.activation(out=ynT[:, 4 * g + j, t0:t0 + 128],
                                                              in_=pbT[:, j * 128:(j + 1) * 128], func=AF.Identity,
                                                              scale=snw[:, 4 * g + j:4 * g + j + 1]),
                                 r=[rbT, R("snw")], w=[R("ynT")])
                yield
                if not sample:
                    pbN, rbN = nb_()
                    pe.emit(lambda: nc.tensor.matmul(pbN[:], lhsT=Btm[:], rhs=wX[:], start=True, stop=True),
                            r=[rBtm, rwX], w=[rbN])
                    S3 = state[:, g, :].rearrange("p (h d) -> p h d", h=8)
                    dve.emit(lambda: nc.vector.tensor_tensor(out=S3, in0=S3,
                                                             in1=wsg[:, 16:24].unsqueeze(2).to_broadcast([128, 8, 64]),
                                                             op=ALU.mult), r=[R("state"), rwsg], w=[R("state")])
                    dve.emit(lambda: nc.vector.tensor_tensor(out=state[:, g, :], in0=state[:, g, :], in1=pbN[:],
                                                             op=ALU.add), r=[R("state"), rbN], w=[R("state")])
                else:
                    sample_ctx("state", g, None, None)


        zw = {}

        def run(gen):
            for _ in gen:
                pass

        if sample:
            for g in range(NG):
                run(gen_A(g, 0))
                run(gen_B(g, 0))
        else:
            run(gen_A(0, 0))
            for g in range(NG):
                ga_ = gen_A(g + 1, (g + 1) % 2) if g + 1 < NG else iter(())
                gb_ = gen_B(g, g % 2)
                alive_a, alive_b = True, True
                while alive_a or alive_b:
                    if alive_b:
                        try:
                            next(gb_)
                        except StopIteration:
                            alive_b = False
                    if alive_a:
                        try:
                            next(ga_)
                        except StopIteration:
                            alive_a = False

    def attn_phase(T, first_chunk_mask, kv_only=False, emit_kv=False):
        ntile = T // 128
        for jp in range(4):
            ka = OFF_K + 64 * (2 * jp)
            kb_ = ka + 64
            va = OFF_V + 128 * jp
            wsl, wreg, ex = wload_multi([
                (w_in[:, ka:ka + 64], wview(16, 64, 0, 256)), (w_in[:, ka:ka + 64], wview(16, 64, 64, 256)),
                (w_in[:, kb_:kb_ + 64], wview(16, 64, 128, 256)), (w_in[:, kb_:kb_ + 64], wview(16, 64, 192, 256))])
            wait_extra(pe, ex)
            wv = wview(16, 256)(wsl)
            tlo = T - 128 if kv_only else 0
            for a in range(2):
                pb, rb = nb_()
                for kk in range(16):
                    pe.emit(lambda: nc.tensor.matmul(pb[:, 0:T - tlo], lhsT=wv[:, kk, a * 128:(a + 1) * 128],
                                                     rhs=hT[:, kk, tlo:T], start=(kk == 0), stop=(kk == 15)),
                            r=[wreg, R("hT")], w=[rb])
                if not kv_only:
                    dve.emit(lambda: nc.vector.tensor_copy(out=KT[:, a, 0:128], in_=kcar[:, 2 * jp + a, :]),
                             r=[R("kcar")], w=[R("KT")])
                    act.emit(lambda: nc.scalar.copy(out=KT[:, a, 128:128 + T], in_=pb[:, 0:T]), r=[rb], w=[R("KT")])
                    dve.emit(lambda: nc.vector.tensor_copy(out=kcar[:, 2 * jp + a, :], in_=KT[:, a, T:T + 128]),
                             r=[R("KT")], w=[R("kcar")])
                else:
                    act.emit(lambda: nc.scalar.copy(out=kcar[:, 2 * jp + a, :], in_=pb[:, 0:128]), r=[rb],
                             w=[R("kcar")])
            if emit_kv:
                pbk, rbk = nb_()
                for kk in range(16):
                    pe.emit(lambda: nc.tensor.matmul(pbk[:, 0:256], lhsT=hT[:, kk, T - 128:T],
                                                     rhs=wv[:, kk, 0:256], start=(kk == 0), stop=(kk == 15)),
                            r=[wreg, R("hT")], w=[rbk])
                kt3 = ktok[:, 128 * jp:128 * (jp + 1)].rearrange("p (a d) -> p a d", a=2)
                dve.emit(lambda: nc.vector.tensor_scalar(
                    out=kt3, in0=pbk[:, 0:256].rearrange("p (a e d) -> p a e d", a=2, e=2)[:, :, 0, :], scalar1=1.0,
                    scalar2=None, op0=ALU.mult), r=[rbk], w=[R("ktok")])
            wsl, wreg, _ = wload_multi([(w_in[:, va:va + 128], wview(16, 128))])
            wv = wview(16, 128)(wsl)
            if not kv_only:
                dve.emit(lambda: nc.vector.tensor_copy(out=Vt[:, 0, :], in_=vcar[:, 128 * jp:128 * (jp + 1)]),
                         r=[R("vcar")], w=[R("Vt")])
            for i in range(tlo // 128, ntile):
                def evv(pb, rb, i=i):
                    if not kv_only:
                        act.emit(lambda: nc.scalar.copy(out=Vt[:, 1 + i, :], in_=pb[:, 0:128]), r=[rb], w=[R("Vt")])
                    if i == ntile - 1:
                        act.emit(lambda: nc.scalar.copy(out=vcar[:, 128 * jp:128 * (jp + 1)], in_=pb[:, 0:128]),
                                 r=[rb], w=[R("vcar")])
                        if emit_kv:
                            act.emit(lambda: nc.scalar.copy(out=ktok[:, 512 + 128 * jp:512 + 128 * (jp + 1)],
                                                            in_=pb[:, 0:128]), r=[rb], w=[R("ktok")])
                pbv, rbv = nb_()
                for kk in range(16):
                    pe.emit(lambda: nc.tensor.matmul(pbv[:, 0:128], lhsT=hT[:, kk, i * 128:(i + 1) * 128],
                                                     rhs=wv[:, kk, 0:128], start=(kk == 0), stop=(kk == 15)),
                            r=[wreg, R("hT")], w=[rbv])
                evv(pbv, rbv)
            if kv_only:
                continue
            for half in range(2):
                c0 = OFF_Q + 512 * jp + 256 * half
                wsl, wreg, _ = wload_multi([(w_in[:, c0:c0 + 256], wview(16, 256))])
                wv = wview(16, 256)(wsl)

                def evq(ct, pb, rb, half=half):
                    act.emit(lambda: nc.scalar.activation(out=QT[:, 2 * half + ct, 0:T], in_=pb[:, 0:T], func=AF.Copy,
                                                          scale=0.125), r=[rb], w=[R("QT")])
                proj_F(wsl, wreg, wv, 2, T, evq)
            for half in range(2):
                c0 = OFF_GA + 512 * jp + 256 * half
                wsl, wreg, _ = wload_multi([(w_in[:, c0:c0 + 256], wview(16, 256))])
                wv = wview(16, 256)(wsl)

                def evg(ct, pb, rb, half=half):
                    act.emit(lambda: nc.scalar.activation(out=gaT[:, 2 * half + ct, 0:T], in_=pb[:, 0:T],
                                                          func=AF.Silu), r=[rb], w=[R("gaT")])
                proj_F(wsl, wreg, wv, 2, T, evg)
            items = [(c, hq) for c in range(ntile) for hq in range(8)]
            ostate = {}

            def S1(i):
                c, hq = items[i]
                a = hq // 4
                qt = hq // 2
                base = 64 * (hq % 2)
                hglob = 8 * jp + hq
                bi = i % NAB
                pb, rb = nb_()
                pe.emit(lambda: nc.tensor.matmul(pb[:, 0:256], lhsT=QT[base:base + 64, qt, c * 128:(c + 1) * 128],
                                                 rhs=KT[base:base + 64, a, c * 128:c * 128 + 256], start=True,
                                                 stop=True), r=[R("QT"), R("KT")], w=[rb])
                rs = R(f"sc{bi}")
                dve.emit(lambda: nc.vector.tensor_tensor(out=sc[bi][:], in0=pb[:, 0:256], in1=bias[:, hglob, :],
                                                         op=ALU.add), r=[rb, R("bias")], w=[rs])
                if first_chunk_mask and c == 0:
                    dve.emit(lambda: nc.vector.tensor_scalar(out=sc[bi][:, 0:128], in0=sc[bi][:, 0:128],
                                                             scalar1=flag[:, 1:2], scalar2=None, op0=ALU.add),
                             r=[rs, R("flag")], w=[rs])
                rsm = R(f"sm{bi}")
                o4 = 4 * bi
                dve.emit(lambda: nc.vector.reduce_max(out=sm[:, o4:o4 + 1], in_=sc[bi][:], axis=AX.X),
                         r=[rs], w=[rsm])
                dve.emit(lambda: nc.vector.tensor_scalar(out=sm[:, o4:o4 + 1], in0=sm[:, o4:o4 + 1],
                                                         scalar1=sinkb[:, hglob:hglob + 1], scalar2=-1.0,
                                                         op0=ALU.max, op1=ALU.mult),
                         r=[rsm, R("sinkb")], w=[rsm])
                act.emit(lambda: nc.scalar.activation(out=sc[bi][:], in_=sc[bi][:], func=AF.Exp,
                                                      bias=sm[:, o4:o4 + 1], scale=1.0,
                                                      accum_out=sm[:, o4 + 1:o4 + 2]), r=[rs, rsm], w=[rs, rsm])
                act.emit(lambda: nc.scalar.activation(out=sm[:, o4 + 2:o4 + 3], in_=sinkb[:, hglob:hglob + 1],
                                                      func=AF.Exp, bias=sm[:, o4:o4 + 1], scale=1.0),
                         r=[rsm, R("sinkb")], w=[rsm])

            def S1c(i):
                bi = i % NAB
                rs = R(f"sc{bi}")
                rsm = R(f"sm{bi}")
                o4 = 4 * bi
                dve.emit(lambda: nc.vector.tensor_tensor(out=sm[:, o4 + 1:o4 + 2], in0=sm[:, o4 + 1:o4 + 2],
                                                         in1=sm[:, o4 + 2:o4 + 3], op=ALU.add), r=[rsm], w=[rsm])
                dve.emit(lambda: nc.vector.reciprocal(out=sm[:, o4 + 3:o4 + 4], in_=sm[:, o4 + 1:o4 + 2]),
                         r=[rsm], w=[rsm])
                act.emit(lambda: nc.scalar.activation(out=Pn[bi][:], in_=sc[bi][:], func=AF.Identity,
                                                      scale=sm[:, o4 + 3:o4 + 4]), r=[rs, rsm], w=[R(f"Pn{bi}")])

            def S2(i):
                bi = i % NAB
                pbt, rbP = nb_()
                ptb = pbt[:].bitcast(BF16)
                for blk in range(2):
                    pe.emit(lambda: nc.tensor.transpose(
                        out=ptb[:, blk * 128:(blk + 1) * 128],
                        in_=Pn[bi][:, blk * 128:(blk + 1) * 128], identity=cst_bf[:]),
                        r=[R(f"Pn{bi}"), R("cst_bf")], w=[rbP])
                act.emit(lambda: nc.scalar.copy(out=PT[bi][:], in_=ptb[:, 0:256]), r=[rbP],
                         w=[R(f"PT{bi}")])

            def S3(i):
                c, hq = items[i]
                a = hq // 4
                qt = hq // 2
                base = 64 * (hq % 2)
                bi = i % NAB
                if hq % 2 == 0:
                    ostate["o"] = nb_()
                    reserved.add(bank_index(ostate["o"][0]))
                pbo, rbo = ostate["o"]
                for blk in range(2):
                    pe.emit(lambda: nc.tensor.matmul(pbo[base:base + 64, 0:128],
                                                     lhsT=Vt[:, c + blk, 64 * a:64 * a + 64],
                                                     rhs=PT[bi][:, blk * 128:(blk + 1) * 128], start=(blk == 0),
                                                     stop=(blk == 1)), r=[R("Vt"), R(f"PT{bi}")], w=[rbo])
                if hq % 2 == 1:
                    dve.emit(lambda: nc.vector.tensor_tensor(out=oT[:, 4 * jp + qt, c * 128:(c + 1) * 128],
                                                             in0=pbo[:, 0:128],
                                                             in1=gaT[:, qt, c * 128:(c + 1) * 128], op=ALU.mult),
                             r=[rbo, R("gaT")], w=[R("oT")])
                    reserved.discard(bank_index(pbo))

            nit = len(items)
            for step in range(nit + 4):
                if step < nit:
                    S1(step)
                if 0 <= step - 1 < nit:
                    S1c(step - 1)
                if 0 <= step - 3 < nit:
                    S2(step - 3)
                if 0 <= step - 4 < nit:
                    S3(step - 4)

    def phase_de(T, xsrc, ydst, sample=False):
        ntile = T // 128
        t1b = [sgs, sga]
        for jp2 in range(8):
            c0 = 256 * jp2
            wl = []
            for hk in range(2):
                wsl, wreg, _ = wload_multi([(w_ssm[2048 * hk:2048 * (hk + 1), c0:c0 + 256], wview(16, 256))])
                wl.append((wview(16, 256)(wsl), wreg))
            pAs = []
            for t in range(2):
                pA, rA = nb_()
                reserved.add(bank_index(pA))
                for kk in range(32):
                    wv, wreg = wl[kk // 16]
                    pe.emit(lambda: nc.tensor.matmul(pA[:, 0:T], lhsT=wv[:, kk % 16, t * 128:(t + 1) * 128],
                                                     rhs=ynT[:, kk, 0:T], start=(kk == 0), stop=(kk == 31)),
                            r=[wreg, R("ynT")], w=[rA])
                pAs.append((pA, rA))
            pBs = []
            if sample:
                wl = []
                for hk in range(2):
                    wsl, wreg, _ = wload_multi([(w_attn[1024 * hk:1024 * (hk + 1), c0:c0 + 256],
                                                 lambda sl: sl[0:64, :].rearrange("p (k c) -> p k c", k=16), 64)])
                    wl.append((wsl[0:64, :].rearrange("p (k c) -> p k c", k=16), wreg))
                for t in range(2):
                    pB, rB = nb_()
                    reserved.add(bank_index(pB))
                    for kk in range(32):
                        wv, wreg = wl[kk // 16]
                        pe.emit(lambda: nc.tensor.matmul(pB[:, 0:T], lhsT=wv[:, kk % 16, t * 128:(t + 1) * 128],
                                                         rhs=oTs[:, kk, 0:T], start=(kk == 0), stop=(kk == 31)),
                                r=[wreg, R("oTs")], w=[rB])
                    pBs.append((pB, rB))
            else:
                wsl, wreg, _ = wload_multi([(w_attn[:, c0:c0 + 256], wview(16, 256))])
                wv = wview(16, 256)(wsl)
                for t in range(2):
                    pB, rB = nb_()
                    reserved.add(bank_index(pB))
                    for kk in range(16):
                        pe.emit(lambda: nc.tensor.matmul(pB[:, 0:T], lhsT=wv[:, kk, t * 128:(t + 1) * 128],
                                                         rhs=oT[:, kk, 0:T], start=(kk == 0), stop=(kk == 15)),
                                r=[wreg, R("oT")], w=[rB])
                    pBs.append((pB, rB))
            wsl, wreg, _ = wload_multi([(w_in[:, OFF_MS + c0:OFF_MS + c0 + 256], wview(16, 256))])
            wv = wview(16, 256)(wsl)
            for t in range(2):
                pC, rC = nb_()
                for kk in range(16):
                    pe.emit(lambda: nc.tensor.matmul(pC[:, 0:T], lhsT=wv[:, kk, t * 128:(t + 1) * 128],
                                                     rhs=hT[:, kk, 0:T], start=(kk == 0), stop=(kk == 15)),
                            r=[wreg, R("hT")], w=[rC])
                rt1 = R(f"t1b{t}")
                act.emit(lambda: nc.scalar.activation(out=t1b[t][:, 0:T], in_=pC[:, 0:T], func=AF.Sigmoid), r=[rC],
                         w=[rt1])
                pA, rA = pAs[t]
                dve.emit(lambda: nc.vector.tensor_tensor(out=t1b[t][:, 0:T], in0=t1b[t][:, 0:T], in1=pA[:, 0:T],
                                                         op=ALU.mult), r=[rt1, rA], w=[rt1])
                reserved.discard(bank_index(pA))
            wsl, wreg, _ = wload_multi([(w_in[:, OFF_MA + c0:OFF_MA + c0 + 256], wview(16, 256))])
            wv = wview(16, 256)(wsl)
            for t in range(2):
                pD, rD = nb_()
                for kk in range(16):
                    pe.emit(lambda: nc.tensor.matmul(pD[:, 0:T], lhsT=wv[:, kk, t * 128:(t + 1) * 128],
                                                     rhs=hT[:, kk, 0:T], start=(kk == 0), stop=(kk == 15)),
                            r=[wreg, R("hT")], w=[rD])
                act.emit(lambda: nc.scalar.activation(out=sgt[:, 0:T], in_=pD[:, 0:T], func=AF.Sigmoid), r=[rD],
                         w=[R("sgt")])
                pB, rB = pBs[t]
                dve.emit(lambda: nc.vector.tensor_tensor(out=sgt[:, 0:T], in0=sgt[:, 0:T], in1=pB[:, 0:T],
                                                         op=ALU.mult), r=[R("sgt"), rB], w=[R("sgt")])
                reserved.discard(bank_index(pB))
                dve.emit(lambda: nc.vector.tensor_tensor(out=mT[:, 2 * jp2 + t, 0:T], in0=t1b[t][:, 0:T],
                                                         in1=sgt[:, 0:T], op=ALU.add),
                         r=[R(f"t1b{t}"), R("sgt")], w=[R("mT")])
        for ip in range(0, ntile, 2):
            tl = list(range(ip, min(ip + 2, ntile)))
            for i in tl:
                sp.dma(xo[i % 2][:], xsrc[i * 128:(i + 1) * 128, :], w=[R(["xt", "xn"][i % 2])])
            for cbk in range(8):
                c0 = 256 * cbk
                wsl, wreg, _ = wload_multi([(w_out[:, c0:c0 + 256], wview(16, 256))])
                wv = wview(16, 256)(wsl)
                for i in tl:
                    pb, rb = nb_()
                    for kk in range(16):
                        pe.emit(lambda: nc.tensor.matmul(pb[:, 0:256], lhsT=mT[:, kk, i * 128:(i + 1) * 128],
                                                         rhs=wv[:, kk, :], start=(kk == 0), stop=(kk == 15)),
                                r=[wreg, R("mT")], w=[rb])
                    dve.emit(lambda: nc.vector.tensor_tensor(out=xo[i % 2][:, c0:c0 + 256],
                                                             in0=xo[i % 2][:, c0:c0 + 256], in1=pb[:, 0:256],
                                                             op=ALU.add), r=[rb, R(["xt", "xn"][i % 2])], w=[R(["xt", "xn"][i % 2])])
            for i in tl:
                rx = R(["xt", "xn"][i % 2])
                act.emit(lambda: nc.scalar.activation(out=junkE, in_=xo[i % 2][:], func=AF.Square,
                                                      accum_out=st4[:, 4:5]), r=[rx], w=[R("ynT"), R("st4")])
                rstd_from_ss(st4[:, 4:5], D, st4[:, 5:6], "st4")
                dve.emit(lambda: nc.vector.scalar_tensor_tensor(out=xo[i % 2][:], in0=xo[i % 2][:],
                                                                scalar=st4[:, 5:6], in1=fnw[:], op0=ALU.mult,
                                                                op1=ALU.mult), r=[rx, R("st4"), R("fnw")], w=[rx])
                sp.dma(ydst[i * 128:(i + 1) * 128, :], xo[i % 2][:], r=[rx], w=[R("yout")])

    def raw_xbc_rows(tok0, dst):
        for cbk in range(24):
            c0 = OFF_XBC + 256 * cbk
            wsl, wreg, _ = wload_multi([(w_in[:, c0:c0 + 256], wview(16, 256))])
            wv = wview(16, 256)(wsl)

            def ev(pb, rb, cbk=cbk):
                act.emit(lambda: nc.scalar.copy(out=xt[:, (cbk % 8) * 256:(cbk % 8 + 1) * 256], in_=pb[:, 0:256]),
                         r=[rb], w=[R("xt")])
            proj_T(wreg, wv, 256, tok0, ev)
            if cbk % 8 == 7:
                o0 = 2048 * (cbk // 8)
                sp.dma(dst[:, o0:o0 + 2048], xt[:], r=[R("xt")], w=[R("convout")])


    pools_ = [[bias[:].rearrange("p h c -> p (h c)"), 32 * 256, 0],
              [state[:].rearrange("p g c -> p (g c)"), NG * 512, 0],
              [oT[:].rearrange("p k t -> p (k t)").bitcast(F32), 16 * TM // 2, 0],
              [QT[:].rearrange("p k t -> p (k t)").bitcast(F32), 4 * TM // 2, 0],
              [gaT[:].rearrange("p k t -> p (k t)").bitcast(F32), 4 * TM // 2, 0],
              [KT[:].rearrange("p k t -> p (k t)").bitcast(F32), (128 + TM), 0]]
    for i_ in range(NAB):
        pools_.append([sc[i_][:], 256, 0])
        pools_.append([Pn[i_][:].bitcast(F32), 128, 0])
        pools_.append([PT[i_][:].bitcast(F32), 128, 0])
    sample_regs = []
    specs = [
        ("oTs", [64, 32, 128], BF16), ("cstb", [128, 5, 128], F32), ("sct", [48, 768], F32),
        ("s0n0", [128, 4, 128], F32), ("s0n1", [128, 4, 128], F32), ("s0n2", [128, 4, 128], F32),
        ("s0n3", [128, 4, 128], F32), ("dAx", [128, 512], F32),
        ("decP", [128, 64], F32), ("S0bf0", [128, 512], BF16), ("S0bf1", [128, 512], BF16),
        ("fin0", [128, 512], F32), ("fin1", [128, 512], F32), ("wXm0", [128, 512], BF16), ("wXm1", [128, 512], BF16),
        ("cmask", [128, 2048], BF16),
        ("CTm", [128, 16, 128], BF16), ("bsel", [128, 16], F32), ("QTs", [64, 1024], BF16), ("gaS", [64, 8, 128], BF16),
        ("KTn", [64, 2, 128], BF16), ("biasS", [32, 8, 136], F32), ("sinkS", [32, 8], F32),
        ("smS", [32, 16], F32), ("Vnb", [8, 16, 128], BF16),
    ]
    for i_ in range(4):
        specs += [(f"ckbf{i_}", [128, 128], BF16), (f"cvbf{i_}", [128, 128], BF16), (f"KTc{i_}", [64, 2, 128], BF16),
                  (f"scS{i_}", [32, 136], F32), (f"PnS{i_}", [32, 136], BF16), (f"PTs{i_}", [128, 32], BF16),
                  (f"PTn{i_}", [8, 32], BF16)]

    def _words(shape, dt):
        n = 1
        for d_ in shape[1:]:
            n *= d_
        w_ = (n * (2 if dt == BF16 else 4) + 3) // 4
        return n, (w_ + 7) // 8 * 8

    SV = {}
    for (name, shape, dt) in sorted(specs, key=lambda t: -_words(t[1], t[2])[1]):
        n, words = _words(shape, dt)
        for pl in pools_:
            if pl[2] + words <= pl[1]:
                o = pl[2]
                pl[2] += words
                flat = pl[0][0:shape[0], o:o + words]
                break
        else:
            raise ValueError("carve overflow " + name)
        flat = flat.bitcast(BF16)[:, 0:n] if dt == BF16 else flat[:, 0:n]
        sample_regs.append(R(name))
        SV[name] = flat if len(shape) == 2 else flat.rearrange("p (a b) -> p a b", a=shape[1])
    oTs, cstb, sct, dAx, decP, cmask, CTm, bsel = (SV[n_] for n_ in ("oTs", "cstb", "sct", "dAx", "decP", "cmask", "CTm",
                                                                     "bsel"))
    s0n = [SV[f"s0n{i_}"] for i_ in range(4)]
    S0bf = [SV["S0bf0"], SV["S0bf1"]]
    fin = [SV["fin0"], SV["fin1"]]
    wXm = [SV["wXm0"], SV["wXm1"]]
    QTs, gaS, KTn, biasS, sinkS, smS, Vnb = (SV[n_] for n_ in ("QTs", "gaS", "KTn", "biasS", "sinkS", "smS", "Vnb"))
    QTs4 = QTs.rearrange("p (b h l) -> p b h l", b=16, h=8)
    ckbf = [SV[f"ckbf{i_}"] for i_ in range(4)]
    cvbf = [SV[f"cvbf{i_}"] for i_ in range(4)]
    KTc = [SV[f"KTc{i_}"] for i_ in range(4)]
    scS = [SV[f"scS{i_}"] for i_ in range(4)]
    PnS = [SV[f"PnS{i_}"] for i_ in range(4)]
    PTs = [SV[f"PTs{i_}"] for i_ in range(4)]
    PTn = [SV[f"PTn{i_}"] for i_ in range(4)]

    def sample_begin():
        tok = dve.emit(lambda: nc.vector.memset(bias[:, 0, 0:2], 0.0), w=[R("bias"), R("state"), R("oT"), R("QT"), R("gaT"), R("KT"), R("sc0"), R("sc1"), R("sc2"), R("sc3"),
                          R("Pn0"), R("Pn1"), R("Pn2"), R("Pn3"), R("PT0"), R("PT1"), R("PT2"), R("PT3")])
        for rg in sample_regs:
            rg.w = tok
            rg.r = {}
        sp.dma(bsel, bsel_d, w=[R("bsel")])
        sp.dma(cstb, cstb_d, w=[R("cstb")])
        pool.dma(cmask, cmask_d, w=[R("cmask")])
        sp.dma(sinkS, sinkS_d, w=[R("sinkS")])
        for h in range(32):
            kv, r_ = h // 4, h % 4
            src = bass.AP(tensor=scr.tensor, offset=h * 128 * 383 + 127, ap=[[382, 8], [1, 136]])
            sp.dma(biasS[8 * r_:8 * r_ + 8, kv, :], src, r=[R(f"scr{h}")], w=[R(f"biasS{h}")])
        return

    sst = {"g": -1}

    def sample_ctx(what, *args):
        if what == "state":
            return
        if what == "yoff":
            g, pbO, rbO = args
            reserved.add(bank_index(pbO))
            dA_g = dAt[:, 0, 8 * g:8 * g + 8]
            dve.emit(lambda: nc.vector.tensor_copy(out=dAx.rearrange("p (h d) -> p h d", h=8),
                                                   in_=dA_g.unsqueeze(2).to_broadcast([128, 8, 64])),
                     r=[R("dAt")], w=[R("dAx")])
            pbd, rbdk = nb_()
            for j in range(4):
                pe.emit(lambda: nc.tensor.matmul(pbd[:, j * 16:(j + 1) * 16], lhsT=dAx[:, j * 128:(j + 1) * 128],
                                                 rhs=bsel, start=True, stop=True), r=[R("dAx"), R("bsel")], w=[rbdk])
            act.emit(lambda: nc.scalar.activation(out=decP, in_=pbd[:, 0:64], func=AF.Exp), r=[rbdk], w=[R("decP")])
            dve.emit(lambda: nc.vector.tensor_tensor(out=CTm,
                                                     in0=cur["CT"][:, 0:128].unsqueeze(1).to_broadcast([128, 16, 128]),
                                                     in1=cmask.rearrange("p (b l) -> p b l", b=16), op=ALU.mult),
                     r=[cur["rCT"], R("cmask")], w=[R("CTm")])

            def ld_s0(b_):
                sp.dma(s0n[b_ % 4], sssm_d[b_, 512 * g:512 * (g + 1), :].rearrange("(j p) n -> p j n", p=128),
                       w=[R(f"s0n{b_ % 4}")])

            ld_s0(0)
            ld_s0(1)
            for b in range(16):
                bi = b % 2
                b4 = b % 4
                if b + 2 < 16:
                    ld_s0(b + 2)
                pbt, rbt = nb_()
                for j in range(4):
                    pe.emit(lambda: nc.tensor.transpose(out=pbt[:, j * 128:(j + 1) * 128], in_=s0n[b4][:, j, :],
                                                        identity=ident), r=[R(f"s0n{b4}"), R("cst")], w=[rbt])
                act.emit(lambda: nc.scalar.copy(out=S0bf[bi], in_=pbt[:]), r=[rbt], w=[R(f"S0bf{bi}")])
                pe.emit(lambda: nc.tensor.matmul(pbO[:], lhsT=CTm[:, b, :], rhs=S0bf[bi], start=(b == 0),
                                                 stop=(b == 15)), r=[R("CTm"), R(f"S0bf{bi}")], w=[rbO])
                dve.emit(lambda: nc.vector.tensor_scalar(out=wXm[bi], in0=cur["wX"][:], scalar1=bsel[:, b:b + 1],
                                                         scalar2=None, op0=ALU.mult),
                         r=[cur["rwX"], R("bsel")], w=[R(f"wXm{bi}")])
                pbn, rbn = nb_()
                for j in range(4):
                    pe.emit(lambda: nc.tensor.matmul(pbn[:, j * 128:(j + 1) * 128],
                                                     lhsT=wXm[bi][:, j * 128:(j + 1) * 128], rhs=cur["Btm"][:],
                                                     start=True, stop=True),
                            r=[R(f"wXm{bi}"), cur["rBtm"]], w=[rbn])
                for j in range(4):
                    dve.emit(lambda: nc.vector.scalar_tensor_tensor(
                        out=fin[bi][:, j * 128:(j + 1) * 128], in0=s0n[b4][:, j, :],
                        scalar=decP[:, j * 16 + b:j * 16 + b + 1], in1=pbn[:, j * 128:(j + 1) * 128], op0=ALU.mult,
                        op1=ALU.add), r=[R(f"s0n{b4}"), R("decP"), rbn], w=[R(f"fin{bi}")])
                sp.dma(ssm_s[b, 512 * g:512 * (g + 1), :].rearrange("(j p) n -> p j n", p=128),
                       fin[bi].rearrange("p (j n) -> p j n", j=4), r=[R(f"fin{bi}")], w=[R("ssmsout")])
            reserved.discard(bank_index(pbO))
            return
        ctg, xr, rr = what, args[0], args[1]
        if ctg < 32:
            g = ctg // 4
            c0 = (ctg % 4) * 128
        elif ctg < 40:
            g = ctg - 32
            c0 = 512
        else:
            g = ctg - 40
            c0 = 640
        if sst["g"] != g:
            sst["g"] = g
            sp.dma(sct[:, 0:512], sconv_d[:, 512 * g:512 * (g + 1)], w=[R("sct")])
            sp.dma(sct[:, 512:640], sconv_d[:, DI + 128 * g:DI + 128 * (g + 1)], w=[R("sctB")])
            sp.dma(sct[:, 640:768], sconv_d[:, DI + GN + 128 * g:DI + GN + 128 * (g + 1)], w=[R("sctC")])
        pbx, rbx = nb_()
        pe.emit(lambda: nc.tensor.transpose(out=pbx[:, 0:48], in_=sct[0:48, c0:c0 + 128], identity=cst[0:48, 0, 0:48]),
                r=[R("sct"), R("sctB"), R("sctC"), R("cst")], w=[rbx])
        act.emit(lambda: nc.scalar.copy(out=xr[:, :, 0:3], in_=pbx[:, 0:48].rearrange("p (b t) -> p b t", t=3)),
                 r=[rbx], w=[rr])

    def attn_sample():
        T = 128
        sp.dma(k_s[:, 0:120, :], ck_d[:, 8:128, :], w=[R("ksout")])
        sp.dma(v_s[:, 0:120, :], cv_d[:, 8:128, :], w=[R("vsout")])
        cnt = 0
        for jp in range(4):
            ka = OFF_K + 128 * jp
            va = OFF_V + 128 * jp
            wsl, wreg, ex = wload_multi([(w_in[:, ka:ka + 128], wview(16, 128, 0, 256)),
                                         (w_in[:, va:va + 128], wview(16, 128, 128, 256))])
            wait_extra(pe, ex)
            wv = wview(16, 256)(wsl)
            for a in range(2):
                pb, rb = nb_()
                for kk in range(16):
                    pe.emit(lambda: nc.tensor.matmul(pb[0:64, 0:T], lhsT=wv[:, kk, 64 * a:64 * a + 64],
                                                     rhs=hT[:, kk, 0:T], start=(kk == 0), stop=(kk == 15)),
                            r=[wreg, R("hT")], w=[rb])
                act.emit(lambda: nc.scalar.copy(out=KTn[:, a, :], in_=pb[0:64, 0:T]), r=[rb], w=[R("KTn")])
            pb, rb = nb_()
            for kk in range(16):
                pe.emit(lambda: nc.tensor.matmul(pb[:, 0:256], lhsT=hT[:, kk, 0:T], rhs=wv[:, kk, 0:256],
                                                 start=(kk == 0), stop=(kk == 15)), r=[wreg, R("hT")], w=[rb])
            act.emit(lambda: nc.scalar.copy(out=ktok[:, 128 * jp:128 * (jp + 1)], in_=pb[:, 0:128]), r=[rb],
                     w=[R("ktok")])
            act.emit(lambda: nc.scalar.copy(out=ktok[:, 512 + 128 * jp:512 + 128 * (jp + 1)], in_=pb[:, 128:256]),
                     r=[rb], w=[R("ktok")])
            for b in range(16):
                pool.dma(Vnb[0:8, b, :], ktok[8 * b:8 * b + 8, 512 + 128 * jp:512 + 128 * (jp + 1)], r=[R("ktok")],
                         w=[R(f"Vnb{b}")])
            for half in range(2):
                c0 = OFF_Q + 512 * jp + 256 * half
                wsl, wreg, _ = wload_multi([(w_in[:, c0:c0 + 256], wview(16, 256))])
                wv = wview(16, 256)(wsl)
                for hh in range(4):
                    pb, rb = nb_()
                    for kk in range(16):
                        pe.emit(lambda: nc.tensor.matmul(pb[0:64, 0:T], lhsT=wv[:, kk, 64 * hh:64 * hh + 64],
                                                         rhs=hT[:, kk, 0:T], start=(kk == 0), stop=(kk == 15)),
                                r=[wreg, R("hT")], w=[rb])
                    act.emit(lambda: nc.scalar.activation(out=QTs4[:, :, 4 * half + hh, :],
                                                          in_=pb[0:64, 0:T].rearrange("p (b l) -> p b l", b=16),
                                                          func=AF.Copy, scale=0.125), r=[rb], w=[R("QTs")])
            for half in range(2):
                c0 = OFF_GA + 512 * jp + 256 * half
                wsl, wreg, _ = wload_multi([(w_in[:, c0:c0 + 256], wview(16, 256))])
                wv = wview(16, 256)(wsl)
                for hh in range(4):
                    pb, rb = nb_()
                    for kk in range(16):
                        pe.emit(lambda: nc.tensor.matmul(pb[0:64, 0:T], lhsT=wv[:, kk, 64 * hh:64 * hh + 64],
                                                         rhs=hT[:, kk, 0:T], start=(kk == 0), stop=(kk == 15)),
                                r=[wreg, R("hT")], w=[rb])
                    act.emit(lambda: nc.scalar.activation(out=gaS[:, 4 * half + hh, :], in_=pb[0:64, 0:T],
                                                          func=AF.Silu), r=[rb], w=[R("gaS")])
            sitems = [(b, a) for b in range(16) for a in range(2)]

            def T1(i):
                b, a = sitems[i]
                bi = b % 4
                si = i % 4
                kv = 2 * jp + a
                if a == 0:
                    pool.dma(ckbf[bi], ck_d[b, :, 128 * jp:128 * (jp + 1)], w=[R(f"ckbf{bi}")])
                    pool.dma(cvbf[bi], cv_d[b, :, 128 * jp:128 * (jp + 1)], w=[R(f"cvbf{bi}")])
                    pbk_, rK = nb_()
                    pkb = pbk_[:].bitcast(BF16)
                    for a2 in range(2):
                        pe.emit(lambda: nc.tensor.transpose(
                            out=pkb[0:64, a2 * 128:(a2 + 1) * 128],
                            in_=ckbf[bi][:, 64 * a2:64 * a2 + 64], identity=cst_bf[:]),
                            r=[R(f"ckbf{bi}"), R("cst_bf")], w=[rK])
                    act.emit(lambda: nc.scalar.copy(
                        out=KTc[bi], in_=pkb[0:64, 0:256].rearrange("p (a k) -> p a k", a=2)),
                        r=[rK], w=[R(f"KTc{bi}")])
                qv = QTs[:, 64 * b + 32 * a:64 * b + 32 * a + 32]
                pbs, rbs = nb_()
                pe.emit(lambda: nc.tensor.matmul(pbs[0:32, 0:128], lhsT=qv, rhs=KTc[bi][:, a, :], start=True,
                                                 stop=True), r=[R("QTs"), R(f"KTc{bi}")], w=[rbs])
                pe.emit(lambda: nc.tensor.matmul(pbs[0:32, 128:136], lhsT=qv, rhs=KTn[:, a, 8 * b:8 * b + 8],
                                                 start=True, stop=True), r=[R("QTs"), R("KTn")], w=[rbs])
                rs = R(f"scS{si}")
                rsm = R(f"smS{si}")
                o4 = 4 * si
                dve.emit(lambda: nc.vector.tensor_tensor(out=scS[si], in0=pbs[0:32, 0:136], in1=biasS[:, kv, :],
                                                         op=ALU.add),
                         r=[rbs] + [R(f"biasS{4 * kv + r_}") for r_ in range(4)], w=[rs])
                dve.emit(lambda: nc.vector.reduce_max(out=smS[:, o4:o4 + 1], in_=scS[si], axis=AX.X), r=[rs], w=[rsm])
                dve.emit(lambda: nc.vector.tensor_scalar(out=smS[:, o4:o4 + 1], in0=smS[:, o4:o4 + 1],
                                                         scalar1=sinkS[:, kv:kv + 1], scalar2=-1.0, op0=ALU.max,
                                                         op1=ALU.mult), r=[rsm, R("sinkS")], w=[rsm])
                act.emit(lambda: nc.scalar.activation(out=scS[si], in_=scS[si], func=AF.Exp, bias=smS[:, o4:o4 + 1],
                                                      scale=1.0, accum_out=smS[:, o4 + 1:o4 + 2]), r=[rs, rsm],
                         w=[rs, rsm])
                act.emit(lambda: nc.scalar.activation(out=smS[:, o4 + 2:o4 + 3], in_=sinkS[:, kv:kv + 1], func=AF.Exp,
                                                      bias=smS[:, o4:o4 + 1], scale=1.0), r=[rsm, R("sinkS")],
                         w=[rsm])

            def T1c(i):
                si = i % 4
                rs = R(f"scS{si}")
                rsm = R(f"smS{si}")
                o4 = 4 * si
                dve.emit(lambda: nc.vector.tensor_tensor(out=smS[:, o4 + 1:o4 + 2], in0=smS[:, o4 + 1:o4 + 2],
                                                         in1=smS[:, o4 + 2:o4 + 3], op=ALU.add), r=[rsm], w=[rsm])
                dve.emit(lambda: nc.vector.reciprocal(out=smS[:, o4 + 3:o4 + 4], in_=smS[:, o4 + 1:o4 + 2]), r=[rsm],
                         w=[rsm])
                act.emit(lambda: nc.scalar.activation(out=PnS[si], in_=scS[si], func=AF.Identity,
                                                      scale=smS[:, o4 + 3:o4 + 4]), r=[rs, rsm], w=[R(f"PnS{si}")])

            def T2(i):
                si = i % 4
                pbs_, rbS = nb_()
                psb = pbs_[:].bitcast(BF16)
                pe.emit(lambda: nc.tensor.transpose(out=psb[:, 0:32], in_=PnS[si][:, 0:128],
                                                    identity=cst_bf[0:32, 0:32]),
                        r=[R(f"PnS{si}"), R("cst_bf")], w=[rbS])
                pe.emit(lambda: nc.tensor.transpose(out=psb[0:8, 32:64], in_=PnS[si][:, 128:136],
                                                    identity=cst_bf[0:32, 0:32]),
                        r=[R(f"PnS{si}"), R("cst_bf")], w=[rbS])
                act.emit(lambda: nc.scalar.copy(out=PTs[si], in_=psb[:, 0:32]), r=[rbS], w=[R(f"PTs{si}")])
                act.emit(lambda: nc.scalar.copy(out=PTn[si], in_=psb[0:8, 32:64]), r=[rbS],
                         w=[R(f"PTn{si}")])

            def T3(i):
                b, a = sitems[i]
                bi = b % 4
                si = i % 4
                pbo, rbo = nb_()
                pe.emit(lambda: nc.tensor.matmul(pbo[0:64, 0:32], lhsT=cvbf[bi][:, 64 * a:64 * a + 64],
                                                 rhs=PTs[si], start=True, stop=False),
                        r=[R(f"cvbf{bi}"), R(f"PTs{si}")], w=[rbo])
                pe.emit(lambda: nc.tensor.matmul(pbo[0:64, 0:32], lhsT=Vnb[0:8, b, 64 * a:64 * a + 64],
                                                 rhs=PTn[si], start=False, stop=True),
                        r=[R(f"Vnb{b}"), R(f"PTn{si}")], w=[rbo])
                dve.emit(lambda: nc.vector.tensor_tensor(
                    out=oTs[:, 8 * jp + 4 * a:8 * jp + 4 * a + 4, 8 * b:8 * b + 8],
                    in0=pbo[0:64, 0:32].rearrange("p (r l) -> p r l", r=4),
                    in1=gaS[:, 4 * a:4 * a + 4, 8 * b:8 * b + 8], op=ALU.mult), r=[rbo, R("gaS")], w=[R("oTs")])

            nsi = len(sitems)
            for step in range(nsi + 3):
                if step < nsi:
                    T1(step)
                if 0 <= step - 1 < nsi:
                    T1c(step - 1)
                if 0 <= step - 2 < nsi:
                    T2(step - 2)
                if 0 <= step - 3 < nsi:
                    T3(step - 3)
        for b in range(16):
            sp.dma(k_s[b, 120:128, :], ktok[8 * b:8 * b + 8, 0:512], r=[R("ktok")], w=[R("ksout")])
            sp.dma(v_s[b, 120:128, :], ktok[8 * b:8 * b + 8, 512:1024], r=[R("ktok")], w=[R("vsout")])

    import os
    STAGE = int(os.environ.get("KSTAGE", "99"))
    NMT = HALF // TM

    marks = []

    def mark(name):
        marks.append((name, pe.n, act.n, dve.n))

    def body():
        if STAGE < 1:
            return
        mark('setup_end')
        for mt in range(NMT):
            phase_a(xp[mt * TM:(mt + 1) * TM, :], TM // 128)
            if STAGE < 2:
                return
            mark(f'prevA{mt}')
            ssd_phase(TM, "state")
            mark(f'prevSSD{mt}')
            if STAGE < 3:
                return
            if mt == NMT - 1:
                attn_phase(TM, False, kv_only=True)
        if STAGE < 4:
            return
        dve.emit(lambda: nc.vector.tensor_scalar(out=state[:].rearrange("p g c -> p (g c)"),
                                                 in0=state[:].rearrange("p g c -> p (g c)"), scalar1=flag[:, 0:1],
                                                 scalar2=None, op0=ALU.mult), r=[R("state"), R("flag")],
                 w=[R("state")])
        dve.emit(lambda: nc.vector.tensor_scalar(out=carry[:].rearrange("p a b -> p (a b)"),
                                                 in0=carry[:].rearrange("p a b -> p (a b)"), scalar1=flag[:, 0:1],
                                                 scalar2=None, op0=ALU.mult), r=[R("carry"), R("flag")],
                 w=[R("carry")])
        for mt in range(NMT):
            mark(f'main_start{mt}')
            phase_a(xm[mt * TM:(mt + 1) * TM, :], TM // 128)
            mark(f'mainA{mt}')
            ssd_phase(TM, "main")
            mark(f'mainSSD{mt}')
            if STAGE < 5:
                return
            attn_phase(TM, mt == 0, emit_kv=(mt == NMT - 1))
            mark(f'mainATT{mt}')
            if STAGE < 6:
                return
            if mt == NMT - 1:
                cflat = carry[:].rearrange("p t c -> p (t c)")
                pbc, rbc = nb_()
                pe.emit(lambda: nc.tensor.transpose(out=pbc[:, 0:128], in_=cflat[:, 0:128], identity=ident),
                        r=[R("carry"), R("cst")], w=[rbc])
                pe.emit(lambda: nc.tensor.transpose(out=pbc[0:16, 128:256], in_=cflat[:, 128:144], identity=ident),
                        r=[R("carry"), R("cst")], w=[rbc])
                act.emit(lambda: nc.scalar.copy(out=cstage[0][:], in_=pbc[:, 0:128]), r=[rbc], w=[R("cstage0")])
                act.emit(lambda: nc.scalar.copy(out=cstage[1][0:16, :], in_=pbc[0:16, 128:256]), r=[rbc],
                         w=[R("cstage1")])
                cp3 = conv_p.rearrange("t (c p) -> t c p", p=128)
                act.dma(cp3[0], cstage[0][0:48, :], r=[R("cstage0")], w=[R("convpout")])
                act.dma(cp3[1], cstage[0][48:96, :], r=[R("cstage0")], w=[R("convpout")])
                act.dma(cp3[2, 0:32], cstage[0][96:128, :], r=[R("cstage0")], w=[R("convpout")])
                act.dma(cp3[2, 32:48], cstage[1][0:16, :], r=[R("cstage1")], w=[R("convpout")])
                sp.dma(k_p, ktok[:, 0:512], r=[R("ktok")], w=[R("kvout")])
                sp.dma(v_p, ktok[:, 512:1024], r=[R("ktok")], w=[R("kvout")])
            phase_de(TM, xm[mt * TM:(mt + 1) * TM, :], y_main[mt * TM:(mt + 1) * TM, :])
            mark(f'mainDE{mt}')
            if STAGE < 7:
                return
        for g in range(NG):
            pb, rb = nb_()
            for j in range(4):
                pe.emit(lambda: nc.tensor.transpose(out=pb[:, j * 128:(j + 1) * 128],
                                                    in_=state[:, g, j * 128:(j + 1) * 128], identity=ident),
                        r=[R("state"), R("cst")], w=[rb])
            act.emit(lambda: nc.scalar.copy(out=ta[:], in_=pb[:]), r=[rb], w=[R("ta")])
            sp.dma(ssm_p[512 * g:512 * (g + 1), :].rearrange("(j p) n -> p j n", p=128),
                   ta[:].rearrange("p (j n) -> p j n", j=4), r=[R("ta")], w=[R("ssmout")])

    body()
    if do_sample and STAGE >= 8:
        mark('sample_start')
        sample_begin()
        phase_a(xs, 1)
        ssd_phase(128, "sample", sample_ctx)
        mark('sampleSSD')
        if STAGE >= 9:
            attn_sample()
        mark('sampleATT')
        if STAGE >= 10:
            phase_de(128, xs, y_samp, sample=True)
        mark('sampleDE')
    if os.environ.get('KPRINT'):
        for m_ in marks:
            print('MARK', *m_)
    if os.environ.get('KPRINT'):
        print('EMIT COUNT', k.count, 'pe', pe.n, 'act', act.n, 'dve', dve.n, 'nsem', k.nsem)
    for e in (pe, act, dve, pool):
        if e.n > 0:
            ep = (e.n - 1) // e.EPOCH
            sp.h.wait_ge(e.sems[ep], e.n - ep * e.EPOCH)

    for slot_tok in sp.ringtok:
        if slot_tok is not None:
            sp.waited.pop(id(slot_tok[0]), None)
            sp.h.wait_ge(slot_tok[0], slot_tok[1])
    for slot_tok in pool.ringtok:
        if slot_tok is not None:
            pool.h.wait_ge(slot_tok[0], slot_tok[1])
    for slot_tok in act.ringtok:
        if slot_tok is not None:
            sp.h.wait_ge(slot_tok[0], slot_tok[1])
    return nc, stack


def _sel_consts():
    bt = _bucket_table()
    sel = np.zeros((32, 383), np.float32)
    negm = np.full((383,), NEG, np.float32)
    for m in range(127, 256):
        d = 255 - m
        sel[bt[d], m] = 1.0
        negm[m] = 0.0
    return sel, np.broadcast_to(negm, (128, 383)).copy()


def _retile_win(w):
    out = np.empty((128, _WTOT), np.float32)
    for (c0, n) in _WBL:
        off = _WTAB[(c0, n)]
        out[:, off:off + 16 * n] = w[:, c0:c0 + n].reshape(16, 128, n).transpose(1, 0, 2).reshape(128, 16 * n)
    return out

_CACHE = {}


def kernel(x_prompt, x_sample, cache_k, cache_v, state_conv, state_ssm, norm_w, w_in, conv_w, conv_b, dt_bias,
           a_log, d_skip, ssm_norm_w, w_ssm_branch, attn_sinks, w_attn_branch, w_out, rel_bias, final_norm_w,
           _cores=None):
    f = np.float32
    cores = list(range(NCORES)) if _cores is None else _cores
    if "nc" not in _CACHE:
        _CACHE["nc"] = build_program()
    nc, _stack = _CACHE["nc"]
    sel, negm = _sel_consts()
    cst = _host_consts(1)
    cstb = _host_consts(16)
    bsel = (np.arange(128)[:, None] // 8 == np.arange(16)[None, :]).astype(f)
    shared = {
        "w_in_t": _retile_win(np.asarray(w_in[0], f)), "w_ssm": np.ascontiguousarray(w_ssm_branch[0], f),
        "w_attn": np.ascontiguousarray(w_attn_branch[0], f), "w_out": np.ascontiguousarray(w_out[0], f),
        "normw": np.ascontiguousarray(norm_w[0].reshape(16, 128).T, f),
        "cw": np.ascontiguousarray(conv_w[0].reshape(4, 48, 128).transpose(2, 1, 0), f),
        "cb": np.ascontiguousarray(conv_b[0].reshape(48, 128).T, f),
        "snw": np.ascontiguousarray(ssm_norm_w[0].reshape(32, 128).T, f),
        "dtb": np.ascontiguousarray(dt_bias[0], f), "alog": np.ascontiguousarray(a_log[0], f),
        "dsk": np.ascontiguousarray(d_skip[0], f), "sinks": np.ascontiguousarray(attn_sinks[0], f),
        "fnw": np.ascontiguousarray(final_norm_w, f), "relb": np.ascontiguousarray(rel_bias, f),
        "sel": sel, "negm": negm, "cst": cst, "cstb": cstb, "bsel": bsel,
        "cmask": np.ascontiguousarray(np.broadcast_to((np.arange(128)[None, :] // 8 == np.arange(16)[:, None]).astype(f).reshape(1, 2048), (128, 2048))),
        "sinkS": np.ascontiguousarray(np.repeat(attn_sinks[0].reshape(8, 4).T, 8, axis=0), f),
    }
    in_maps = []
    for c in cores:
        b, h = c // 2, c % 2
        m = dict(shared)
        m["xm"] = np.ascontiguousarray(x_prompt[b, h * HALF:(h + 1) * HALF], f)
        m["xp"] = np.ascontiguousarray(x_prompt[b, 0:HALF], f) if h == 1 else np.zeros((HALF, D), f)
        m["xs"] = np.ascontiguousarray(x_sample[16 * c:16 * c + 16].reshape(128, D), f)
        fl = np.zeros((128, 2), f)
        fl[:, 0] = float(h)
        fl[:, 1] = (float(h) - 1.0) * 30000.0
        m["flag"] = fl
        m["ck"] = np.ascontiguousarray(cache_k[0, 16 * c:16 * c + 16].reshape(16, 128, 512), f)
        m["cv"] = np.ascontiguousarray(cache_v[0, 16 * c:16 * c + 16].reshape(16, 128, 512), f)
        m["sconv"] = np.ascontiguousarray(state_conv[0, 16 * c:16 * c + 16].reshape(48, CONV_DIM), f)
        m["sssm"] = np.ascontiguousarray(state_ssm[0, 16 * c:16 * c + 16].reshape(16, DI, NS), f)
        in_maps.append(m)
    res = run_bass_kernel_spmd(nc, in_maps, core_ids=list(range(len(cores))))
    rs = res.results
    B = x_prompt.shape[0]
    y_prompt = np.zeros((B, 2048, D), f)
    y_sample = np.zeros((128, 8, D), f)
    k_prompt = np.zeros((1, B, 128, 8, 64), f)
    v_prompt = np.zeros((1, B, 128, 8, 64), f)
    conv_prompt = np.zeros((1, B, 3, CONV_DIM), f)
    ssm_prompt = np.zeros((1, B, NH, HP, NS), f)
    k_sample = np.zeros((1, 128, 128, 8, 64), f)
    v_sample = np.zeros((1, 128, 128, 8, 64), f)
    conv_sample = np.zeros((1, 128, 3, CONV_DIM), f)
    ssm_sample = np.zeros((1, 128, NH, HP, NS), f)
    for i, c in enumerate(cores):
        b, h = c // 2, c % 2
        r = rs[i]
        y_prompt[b, h * HALF:(h + 1) * HALF] = r["y_main"]
        y_sample[16 * c:16 * c + 16] = r["y_samp"].reshape(16, 8, D)
        if h == 1:
            k_prompt[0, b] = r["k_p"].reshape(128, 8, 64)
            v_prompt[0, b] = r["v_p"].reshape(128, 8, 64)
            conv_prompt[0, b] = r["conv_p"]
            ssm_prompt[0, b] = r["ssm_p"].reshape(NH, HP, NS)
        k_sample[0, 16 * c:16 * c + 16] = r["k_s"].reshape(16, 128, 8, 64)
        v_sample[0, 16 * c:16 * c + 16] = r["v_s"].reshape(16, 128, 8, 64)
        conv_sample[0, 16 * c:16 * c + 16] = r["conv_s"].reshape(16, 3, CONV_DIM)
        ssm_sample[0, 16 * c:16 * c + 16] = r["ssm_s"].reshape(16, NH, HP, NS)
    return (y_prompt, y_sample, k_prompt, v_prompt, conv_prompt, ssm_prompt, k_sample, v_sample, conv_sample,
            ssm_sample)
```

```python
import math
import numpy as np
import concourse.bass as bass
import concourse.mybir as mybir
from concourse.bass_utils import run_bass_kernel_spmd

F32 = mybir.dt.float32
BF16 = mybir.dt.bfloat16
AF = mybir.ActivationFunctionType
ALU = mybir.AluOpType
AX = mybir.AxisListType

D = 2048
DI = 4096
NH = 64
HP = 64
NG = 8
NS = 128
GN = 1024
CONV_DIM = 6144
OFF_Z = 0
OFF_XBC = 4096
OFF_DT = OFF_XBC + CONV_DIM
OFF_Q = OFF_DT + 64
OFF_K = OFF_Q + 2048
OFF_V = OFF_K + 512
OFF_GA = OFF_V + 512
OFF_MS = OFF_GA + 2048
OFF_MA = OFF_MS + 2048
IN_COLS = OFF_MA + 2048
EPS = 1e-6
NEG = -30000.0
NCORES = 8
HALF = 1024
TM = 256


class Reg:
    __slots__ = ("name", "w", "r", "alias")

    def __init__(self, name):
        self.name = name
        self.w = None
        self.r = {}
        self.alias = []


def _deps(r, w):
    toks = []
    for x in r:
        toks.append(x.w)
        for y in x.alias:
            toks.append(y.w)
    for x in w:
        toks.append(x.w)
        toks.extend(x.r.values())
        for y in x.alias:
            toks.append(y.w)
            toks.extend(y.r.values())
    return toks


class Eng:
    EPOCH = 6000

    def __init__(self, K, name, h, inorder=True):
        self.K = K
        self.name = name
        self.h = h
        self.inorder = inorder
        self.n = 0
        self.sems = []
        self.waited = {}
        self.ring = []
        self.ringpos = 0
        self.ringtok = []

    def _wait(self, tok):
        if tok is None:
            return
        sem, val, src = tok
        key = id(sem)
        if self.waited.get(key, 0) >= val:
            return
        if src is self and (self.name == "pe" or self.K.nosame):
            return
        self.h.wait_ge(sem, val)
        self.waited[key] = val

    def emit(self, fn, r=(), w=()):
        if self.K.count >= self.K.maxi:
            return None
        self.K.count += 1
        for t in _deps(r, w):
            self._wait(t)
        ins = fn()
        if self.K.trace and self.K.trace[0] <= self.K.count <= self.K.trace[1]:
            print(self.K.count, str(ins)[:330])
        ep = self.n // self.EPOCH
        while len(self.sems) <= ep:
            self.sems.append(self.K.newsem(f"{self.name}{len(self.sems)}"))
        self.n += 1
        tok = (self.sems[ep], self.n - ep * self.EPOCH, self)
        ins.then_inc(tok[0], 1)
        for x in r:
            x.r[self.name] = tok
        for x in w:
            x.w = tok
            x.r = {}
        return tok

    def dma(self, out, in_, r=(), w=()):
        if self.K.count >= self.K.maxi:
            return None
        self.K.count += 1
        for t in _deps(r, w):
            self._wait(t)
        if not self.ring:
            for i in range(8):
                self.ring.append([self.K.newsem(f"{self.name}d{i}"), 0])
            self.ringtok = [None] * 8
        i = self.ringpos % len(self.ring)
        self.ringpos += 1
        if self.ringtok[i] is not None:
            self._wait(self.ringtok[i])
        slot = self.ring[i]
        if slot[1] + 16 > 16 * 2000:
            slot[0] = self.K.newsem(f"{self.name}d{i}x{self.ringpos}")
            slot[1] = 0
        slot[1] += 16
        ins = self.h.dma_start(out=out, in_=in_, max_dma_last_dim=8192) if self.name == "pool" else \
            self.h.dma_start(out=out, in_=in_)
        if self.K.trace and self.K.trace[0] <= self.K.count <= self.K.trace[1]:
            print(self.K.count, str(ins)[:330])
        ins.then_inc(slot[0], 16)
        tok = (slot[0], slot[1], None)
        self.ringtok[i] = tok
        self.K.dmacount += 1
        for x in r:
            x.r[f"{self.name}_dma{self.K.dmacount}"] = tok
        for x in w:
            x.w = tok
            x.r = {}
        return tok


class K:
    def __init__(self, nc, stack):
        self.nc = nc
        self.stack = stack
        self.dmacount = 0
        self.nsem = 0
        import os as _os
        self.count = 0
        self.nosame = _os.environ.get("KNOSAME", "0") == "1"
        self.maxi = int(_os.environ.get("KMAXI", "100000000"))
        tr = _os.environ.get("KTRACE")
        self.trace = [int(v) for v in tr.split(":")] if tr else None
        self.pe = Eng(self, "pe", nc.tensor)
        self.act = Eng(self, "act", nc.scalar)
        self.dve = Eng(self, "dve", nc.vector)
        self.pool = Eng(self, "pool", nc.gpsimd)
        self.sp = Eng(self, "sp", nc.sync)
        self.regs = {}
        self.nbank = 0

    def newsem(self, name):
        self.nsem += 1
        return self.stack.enter_context(self.nc.semaphore(f"s_{name}_{self.nsem}"))

    def sb(self, name, shape, dt=F32):
        t = self.stack.enter_context(self.nc.sbuf_tensor("sb_" + name, list(shape), dt))
        return t

    def reg(self, name):
        if name not in self.regs:
            self.regs[name] = Reg(name)
        return self.regs[name]

    def link(self, a, b):
        ra, rb = self.reg(a), self.reg(b)
        if rb not in ra.alias:
            ra.alias.append(rb)
        if ra not in rb.alias:
            rb.alias.append(ra)


def _bucket_table():
    out = []
    for d in range(0, 129):
        if d < 16:
            out.append(d)
        else:
            v = np.float32(np.log(np.float32(d) / np.float32(16.0))) / np.float32(math.log(128 / 16)) * np.float32(16)
            out.append(min(16 + int(np.int32(v)), 31))
    return out


def _host_consts(nb):
    L = 128 // nb
    idx = np.arange(128)
    same = (idx[:, None] // L) == (idx[None, :] // L)
    c = {}
    c["ident"] = np.eye(128, dtype=np.float32)
    c["ut"] = ((idx[:, None] <= idx[None, :]) & same).astype(np.float32)
    c["sl"] = ((idx[:, None] > idx[None, :]) & same).astype(np.float32)
    c["cm"] = ((idx[None, :] >= idx[:, None]) & same).astype(np.float32)
    c["same"] = same.astype(np.float32)
    return np.stack([c["ident"], c["ut"], c["sl"], c["cm"], c["same"]], axis=1).astype(np.float32)


def _win_blocks():
    bl = []
    for i in range(16):
        bl.append((OFF_Z + 256 * i, 256))
        bl.append((OFF_XBC + 256 * i, 256))
    for g in range(8):
        bl.append((OFF_XBC + DI + 128 * g, 128))
        bl.append((OFF_XBC + DI + GN + 128 * g, 128))
    bl.append((OFF_DT, 64))
    for i in range(8):
        bl.append((OFF_Q + 256 * i, 256))
        bl.append((OFF_GA + 256 * i, 256))
        bl.append((OFF_MS + 256 * i, 256))
        bl.append((OFF_MA + 256 * i, 256))
        bl.append((OFF_K + 64 * i, 64))
    for jp in range(4):
        bl.append((OFF_K + 128 * jp, 128))
        bl.append((OFF_V + 128 * jp, 128))
    off, table = 0, {}
    for (c0, n) in bl:
        table[(c0, n)] = off
        off += 16 * n
    return bl, table, off


_WBL, _WTAB, _WTOT = _win_blocks()


class _WinProxy:
    def __init__(self, ap):
        self.ap = ap

    def __getitem__(self, key):
        sl = key[1]
        c0, n = sl.start, sl.stop - sl.start
        off = _WTAB[(c0, n)]
        blk = self.ap[:, off:off + 16 * n].rearrange("p (k c) -> p k c", k=16)

        class _V:
            def rearrange(self_, *a, **kw):
                return blk
        return _V()


def _sq_blocks(nrow_blocks):
    bl, table, off = [], {}, 0
    for rb in range(nrow_blocks):
        for j in range(8):
            bl.append((2048 * rb, 256 * j, 256))
            table[(2048 * rb, 256 * j, 256)] = off
            off += 16 * 256
    return bl, table, off


_SSM_BL, _SSM_TAB, _SSM_TOT = _sq_blocks(2)
_SQ_BL, _SQ_TAB, _SQ_TOT = _sq_blocks(1)


class _TileProxy:
    def __init__(self, tiled, table, orig):
        self.tiled, self.table, self.orig = tiled, table, orig

    def __getitem__(self, key):
        rs, cs = key
        r0 = rs.start or 0
        c0, n = cs.start, cs.stop - cs.start
        tiled, table, orig = self.tiled, self.table, self.orig

        class _V:
            def rearrange(self_, pat, **kw):
                if kw.get("p") == 128 and (r0, c0, n) in table:
                    off = table[(r0, c0, n)]
                    return tiled[:, off:off + 16 * n].rearrange("p (k c) -> p k c", k=16)
                return orig[rs, cs].rearrange(pat, **kw)
        return _V()


def _retile_sq(w, bl, table, tot):
    out = np.empty((128, tot), np.float32)
    for (r0, c0, n) in bl:
        off = table[(r0, c0, n)]
        out[:, off:off + 16 * n] = w[r0:r0 + 2048, c0:c0 + n].reshape(16, 128, n).transpose(1, 0, 2).reshape(128, 16 * n)
    return out

def build_program(do_sample=True):
    from contextlib import ExitStack
    nc = bass.Bass("TRN2", target_bir_lowering=False)
    stack = ExitStack()
    k = K(nc, stack)
    pe, act, dve, pool, sp = k.pe, k.act, k.dve, k.pool, k.sp

    def din(name, shape, dt=F32):
        return nc.dram_tensor(name, list(shape), dt, kind="ExternalInput").ap()

    def dout(name, shape, dt=F32):
        return nc.dram_tensor(name, list(shape), dt, kind="ExternalOutput").ap()

    xm = din("xm", [HALF, D])
    xp = din("xp", [HALF, D])
    xs = din("xs", [128, D])
    flag_d = din("flag", [128, 2])
    w_in = _WinProxy(din("w_in_t", [128, _WTOT]))
    w_ssm = _TileProxy(din("w_ssm_t", [128, _SSM_TOT]), _SSM_TAB, None)
    w_attn = _TileProxy(din("w_attn_t", [128, _SQ_TOT]), _SQ_TAB, din("w_attn", [D, D]))
    w_out = _TileProxy(din("w_out_t", [128, _SQ_TOT]), _SQ_TAB, None)
    normw_d = din("normw", [128, 16])
    cw_d = din("cw", [128, 48, 4])
    cb_d = din("cb", [128, 48])
    snw_d = din("snw", [128, 32])
    dtb_d = din("dtb", [64])
    alog_d = din("alog", [64])
    dsk_d = din("dsk", [64])
    sink_d = din("sinks", [32])
    fnw_d = din("fnw", [D])
    relb_d = din("relb", [32, 32])
    sel_d = din("sel", [32, 383])
    negm_d = din("negm", [128, 383])
    cst_d = din("cst", [128, 5, 128])
    cstb_d = din("cstb", [128, 5, 128])
    bsel_d = din("bsel", [128, 16])
    cmask_d = din("cmask", [128, 2048])
    sinkS_d = din("sinkS", [32, 8])
    ck_d = din("ck", [16, 128, 512])
    cv_d = din("cv", [16, 128, 512])
    sconv_d = din("sconv", [48, CONV_DIM])
    sssm_d = din("sssm", [16, DI, NS])

    y_main = dout("y_main", [HALF, D])
    y_samp = dout("y_samp", [128, D])
    k_p = dout("k_p", [128, 512])
    v_p = dout("v_p", [128, 512])
    conv_p = dout("conv_p", [3, CONV_DIM])
    ssm_p = dout("ssm_p", [DI, NS])
    k_s = dout("k_s", [16, 128, 512])
    v_s = dout("v_s", [16, 128, 512])
    conv_s = dout("conv_s", [48, CONV_DIM])
    ssm_s = dout("ssm_s", [16, DI, NS])
    scr = nc.dram_tensor("scr", [32, 128, 383], F32).ap()

    hT = k.sb("hT", [128, 16, TM], BF16)
    ynT = k.sb("ynT", [128, 32, TM], BF16)
    oT = k.sb("oT", [128, 16, TM], BF16)
    mT = k.sb("mT", [128, 16, TM], BF16)
    state = k.sb("state", [128, NG, 512], F32)
    state_bf = k.sb("state_bf", [128, 512], BF16)
    NSLOT = 3
    wslot = [k.sb(f"wslot{i}", [128, 4096], BF16) for i in range(NSLOT)]
    bias = k.sb("bias", [128, 32, 256], F32)
    cst = k.sb("cst", [128, 5, 128], F32)
    cst_bf = k.sb("cst_bf", [128, 128], BF16)
    normw = k.sb("normw", [128, 16], F32)
    cw = k.sb("cw", [128, 48, 4], F32)
    cb = k.sb("cb", [128, 48], F32)
    snw = k.sb("snw", [128, 32], F32)
    dtb = k.sb("dtb", [128, 64], F32)
    Arow = k.sb("Arow", [128, 64], F32)
    Drow = k.sb("Drow", [128, 64], F32)
    sinkb = k.sb("sinkb", [128, 32], F32)
    fnw = k.sb("fnw", [128, D], F32)
    flag = k.sb("flag", [128, 2], F32)
    carry = k.sb("carry", [128, 3, 48], F32)
    cstage = [k.sb(f"cstage{i}", [128, 128], F32) for i in range(2)]
    ctmp = k.sb("ctmp", [128, 48], F32)
    kcar = k.sb("kcar", [128, 8, 128], BF16)
    vcar = k.sb("vcar", [128, 512], BF16)
    xt = k.sb("xt", [128, D], F32)
    xn = k.sb("xn", [128, D], F32)
    st4 = k.sb("st4", [128, 8], F32)
    zsbig = k.sb("zsbig", [128, 1024], F32)
    zsL = [zsbig[:, 0:512], zsbig[:, 512:1024]]
    ktok = zsbig
    k.link("ktok", "zs0")
    k.link("ktok", "zs1")
    xraw = k.sb("xraw", [128, 4, 260], F32)
    xaccL = [k.sb(f"xacc{i}", [128, TM], F32) for i in range(2)]
    xcTL = [k.sb(f"xcT{i}", [128, 4, TM], BF16) for i in range(2)]
    BTL = [k.sb(f"BT{i}", [128, TM], BF16) for i in range(2)]
    CTL = [k.sb(f"CT{i}", [128, TM], BF16) for i in range(2)]
    dtt = k.sb("dtt", [128, TM // 128, 64], F32)
    dAt = k.sb("dAt", [128, TM // 128, 64], F32)
    et = k.sb("et", [128, 64], F32)
    cdt = k.sb("cdt", [128, 64], F32)
    wsgL = [k.sb(f"wsg{i}", [128, 32], F32) for i in range(2)]
    dtwL = [k.sb(f"dtw{i}", [128, 8], F32) for i in range(2)]
    XtmL = [k.sb(f"Xtm{i}", [128, 512], F32) for i in range(2)]
    XdtL = [k.sb(f"Xdt{i}", [128, 512], BF16) for i in range(2)]
    wXL = [k.sb(f"wX{i}", [128, 512], BF16) for i in range(2)]
    BtmL = [k.sb(f"Btm{i}", [128, 128], BF16) for i in range(2)]
    cur = {"ci": 0}
    rseg = k.sb("rseg", [128, 8, 128], F32)
    LT = k.sb("LT", [128, 8, 128], F32)
    MT = k.sb("MT", [128, 8, 128], BF16)
    CBm = k.sb("CBm", [128, 128], F32)
    ta = k.sb("ta", [128, 512], F32)
    tb = k.sb("tb", [128, 512], F32)
    tc_ = k.sb("tc", [128, 512], F32)
    junkE = ynT[:].rearrange("p k t -> p (k t)")[:, 0:D]
    QT = k.sb("QT", [128, 4, TM], BF16)
    KT = k.sb("KT", [128, 2, 128 + TM], BF16)
    Vt = k.sb("Vt", [128, 1 + TM // 128, 128], BF16)
    gaT = k.sb("gaT", [128, 4, TM], BF16)
    NAB = 4
    sc = [k.sb(f"sc{i}", [128, 256], F32) for i in range(NAB)]
    Pn = [k.sb(f"Pn{i}", [128, 256], BF16) for i in range(NAB)]
    PT = [k.sb(f"PT{i}", [128, 256], BF16) for i in range(NAB)]
    sm = k.sb("sm", [128, 4 * NAB], F32)
    sgs = k.sb("sgs", [128, TM], F32)
    sga = k.sb("sga", [128, TM], F32)
    sgt = k.sb("sgt", [128, TM], F32)
    xo = [xt, xn]

    banks = [stack.enter_context(nc.psum_tensor(f"bank{i}", [128, 512], F32)) for i in range(7)]
    bankb = stack.enter_context(nc.psum_tensor("bankb", [128, 1024], BF16))
    RB = [k.reg(f"bank{i}") for i in range(7)]
    RBb = k.reg("bankb")
    for i_ in range(4):
        k.link("bankb", f"bankbP{i_}")
    k.link("bankb", "bankbK")
    for i_ in range(4):
        k.link("bankb", f"bankbS{i_}")
        k.link("bankbP2", f"bankbS{i_}")
    for i_ in range(2):
        k.link("bankb", f"bankbK{i_}")
        k.link("bankbP0", f"bankbK{i_}")
        k.link("bankbP1", f"bankbK{i_}")
    k.link("bankbP0", "bankbK")

    reserved = set()

    def nb_():
        while True:
            i = k.nbank % 7
            k.nbank += 1
            if i not in reserved:
                return banks[i], RB[i]

    def bank_index(pb):
        for i, b in enumerate(banks):
            if b is pb:
                return i
        raise ValueError

    R = k.reg

    def ld(dst, src, name):
        sp.dma(dst, src, w=[R(name)])

    ld(cst[:], cst_d, "cst")
    ld(normw[:], normw_d, "normw")
    ld(cw[:], cw_d, "cw")
    ld(cb[:], cb_d, "cb")
    ld(snw[:], snw_d, "snw")
    ld(dtb[:], dtb_d.partition_broadcast(128), "dtb")
    ld(Arow[:], alog_d.partition_broadcast(128), "Arow")
    ld(Drow[:], dsk_d.partition_broadcast(128), "Drow")
    ld(sinkb[:], sink_d.partition_broadcast(128), "sinkb")
    ld(fnw[:], fnw_d.partition_broadcast(128), "fnw")
    ld(flag[:], flag_d, "flag")
    ident = cst[:, 0, :]
    act.emit(lambda: nc.scalar.activation(out=Arow[:], in_=Arow[:], func=AF.Exp), r=[R("Arow")], w=[R("Arow")])
    dve.emit(lambda: nc.vector.tensor_scalar(out=Arow[:], in0=Arow[:], scalar1=-1.0, scalar2=None, op0=ALU.mult),
             r=[R("Arow")], w=[R("Arow")])
    dve.emit(lambda: nc.vector.tensor_copy(out=cst_bf[:], in_=cst[:, 0, :]), r=[R("cst")], w=[R("cst_bf")])
    dve.emit(lambda: nc.vector.memset(carry[:], 0.0), w=[R("carry")])
    dve.emit(lambda: nc.vector.memset(state[:], 0.0), w=[R("state")])
    dve.emit(lambda: nc.vector.memset(kcar[:], 0.0), w=[R("kcar")])
    dve.emit(lambda: nc.vector.memset(vcar[:], 0.0), w=[R("vcar")])

    ynf = ynT[:].rearrange("p k t -> p (k t)").bitcast(F32)
    relb = ynf[0:32, 0:32]
    relbb = ynf[0:32, 32:160]
    sel = ynf[0:32, 160:543]
    negm = ynf[:, 544:927]
    grow = [ynf[:, 928:1311], ynf[:, 1312:1695]]
    for nm in ("relb", "relbb", "sel", "negm", "grow0", "grow1"):
        k.link("ynT", nm)
    ld(relb, relb_d, "relb")
    ld(sel, sel_d, "sel")
    ld(negm, negm_d, "negm")
    for h in range(32):
        gi = h % 2
        dve.emit(lambda: nc.vector.tensor_copy(out=relbb, in_=relb[:, h:h + 1].to_broadcast([32, 128])),
                 r=[R("relb")], w=[R("relbb")])
        pb, rb = nb_()
        pe.emit(lambda: nc.tensor.matmul(pb[:, 0:383], lhsT=relbb, rhs=sel, start=True, stop=True),
                r=[R("relbb"), R("sel")], w=[rb])
        dve.emit(lambda: nc.vector.tensor_tensor(out=grow[gi], in0=pb[:, 0:383], in1=negm, op=ALU.add),
                 r=[rb, R("negm")], w=[R(f"grow{gi}")])
        sp.dma(scr[h], grow[gi], r=[R(f"grow{gi}")], w=[R(f"scr{h}")])
        src = bass.AP(tensor=scr.tensor, offset=h * 128 * 383 + 127, ap=[[382, 128], [1, 256]])
        sp.dma(bias[:, h, :], src, r=[R(f"scr{h}")], w=[R("bias")])

    wstate = {"i": 0}

    def wload_multi(parts):
        i = wstate["i"] % NSLOT
        wstate["i"] += 1
        reg = R(f"wslot{i}")
        toks = []
        prev_w, prev_r = reg.w, dict(reg.r)
        for part in parts:
            src, view = part[0], part[1]
            pp = part[2] if len(part) > 2 else 128
            reg.w, reg.r = prev_w, dict(prev_r)
            toks.append(pool.dma(view(wslot[i]), src.rearrange("(k p) c -> p k c", p=pp), w=[reg]))
        reg.w = toks[-1]
        reg.r = {}
        return wslot[i], reg, [t for t in toks[:-1] if t is not None]

    def wview(kt, cols, coff=0, tot=None):
        tot = tot or cols

        def f(slot):
            return slot[:, 0:kt * tot].rearrange("p (k c) -> p k c", k=kt)[:, :, coff:coff + cols]
        return f

    def wait_extra(eng, toks):
        for t in toks:
            eng._wait(t)

    def rstd_from_ss(ss_ap, n, out_ap, rname):
        dve.emit(lambda: nc.vector.tensor_scalar(out=out_ap, in0=ss_ap, scalar1=1.0 / n, scalar2=EPS, op0=ALU.mult,
                                                 op1=ALU.add), r=[R(rname)], w=[R(rname)])
        act.emit(lambda: nc.scalar.activation(out=out_ap, in_=out_ap, func=AF.Sqrt), r=[R(rname)], w=[R(rname)])
        dve.emit(lambda: nc.vector.reciprocal(out=out_ap, in_=out_ap), r=[R(rname)], w=[R(rname)])

    def phase_a(xsrc, ntile):
        for i in range(ntile):
            sp.dma(xt[:], xsrc[i * 128:(i + 1) * 128, :], w=[R("xt")])
            act.emit(lambda: nc.scalar.activation(out=xn[:], in_=xt[:], func=AF.Square, accum_out=st4[:, 0:1]),
                     r=[R("xt")], w=[R("xn"), R("st4")])
            rstd_from_ss(st4[:, 0:1], D, st4[:, 1:2], "st4")
            dve.emit(lambda: nc.vector.tensor_scalar(out=xn[:], in0=xt[:], scalar1=st4[:, 1:2], scalar2=None,
                                                     op0=ALU.mult), r=[R("xt"), R("st4")], w=[R("xn")])
            for kk in range(0, 16, 4):
                pb, rb = nb_()
                for j in range(4):
                    pe.emit(lambda: nc.tensor.transpose(out=pb[:, j * 128:(j + 1) * 128],
                                                        in_=xn[:, (kk + j) * 128:(kk + j + 1) * 128], identity=ident),
                            r=[R("xn"), R("cst")], w=[rb])
                for j in range(4):
                    eng = act if (kk // 4) % 2 == 0 else dve
                    if eng is act:
                        act.emit(lambda: nc.scalar.activation(out=hT[:, kk + j, i * 128:(i + 1) * 128],
                                                              in_=pb[:, j * 128:(j + 1) * 128], func=AF.Identity,
                                                              scale=normw[:, kk + j:kk + j + 1]),
                                 r=[rb, R("normw")], w=[R("hT")])
                    else:
                        dve.emit(lambda: nc.vector.tensor_scalar(out=hT[:, kk + j, i * 128:(i + 1) * 128],
                                                                 in0=pb[:, j * 128:(j + 1) * 128],
                                                                 scalar1=normw[:, kk + j:kk + j + 1], scalar2=None,
                                                                 op0=ALU.mult), r=[rb, R("normw")], w=[R("hT")])

    def proj_F(wsl, wreg, wv, ncol_tiles, T, evac):
        for ct in range(ncol_tiles):
            pb, rb = nb_()
            for kk in range(16):
                pe.emit(lambda: nc.tensor.matmul(pb[:, 0:T], lhsT=wv[:, kk, ct * 128:(ct + 1) * 128],
                                                 rhs=hT[:, kk, 0:T], start=(kk == 0), stop=(kk == 15)),
                        r=[wreg, R("hT")], w=[rb])
            evac(ct, pb, rb)

    def proj_T(wreg, wv, ncols, tok0, evac):
        pb, rb = nb_()
        for kk in range(16):
            pe.emit(lambda: nc.tensor.matmul(pb[:, 0:ncols], lhsT=hT[:, kk, tok0:tok0 + 128], rhs=wv[:, kk, 0:ncols],
                                             start=(kk == 0), stop=(kk == 15)), r=[wreg, R("hT")], w=[rb])
        evac(pb, rb)

    def conv_tile(ct_global, pb, rb, T, nblk, dst, dstreg, slot6, ctx_src=None):
        Lb = T // nblk
        xr = xraw[:, slot6, 0:nblk * (Lb + 3)].rearrange("p (b l) -> p b l", b=nblk)
        rr = R(f"xraw{slot6}")
        if nblk == 1:
            dve.emit(lambda: nc.vector.tensor_copy(out=xr[:, 0, 0:3], in_=carry[:, :, ct_global]),
                     r=[R("carry")], w=[rr])
        else:
            ctx_src(xr, rr)
        act.emit(lambda: nc.scalar.copy(out=xr[:, :, 3:3 + Lb], in_=pb[:, 0:T].rearrange("p (b l) -> p b l", b=nblk)),
                 r=[rb], w=[rr])
        if nblk == 1:
            dve.emit(lambda: nc.vector.tensor_copy(out=carry[:, :, ct_global], in_=xr[:, 0, Lb:Lb + 3]),
                     r=[rr], w=[R("carry")])
        else:
            cs_i = ct_global % 2
            dve.emit(lambda: nc.vector.tensor_copy(out=ctmp[:].rearrange("p (b t) -> p b t", t=3),
                                                   in_=xr[:, :, Lb:Lb + 3]), r=[rr], w=[R("ctmp")])
            pbc, rbc = nb_()
            pe.emit(lambda: nc.tensor.transpose(out=pbc[0:48, 0:128], in_=ctmp[:], identity=ident),
                    r=[R("ctmp"), R("cst")], w=[rbc])
            act.emit(lambda: nc.scalar.copy(out=cstage[cs_i][0:48, :], in_=pbc[0:48, 0:128]), r=[rbc],
                     w=[R(f"cstage{cs_i}")])
            act.dma(conv_s[:, ct_global * 128:(ct_global + 1) * 128], cstage[cs_i][0:48, :],
                    r=[R(f"cstage{cs_i}")], w=[R("convsout")])
        xi_ = cur.get("xi", 0) % 2
        cur["xi"] = cur.get("xi", 0) + 1
        xacc = xaccL[xi_]
        rxa = R(f"xacc{xi_}")
        xa = xacc[:, 0:T].rearrange("p (b l) -> p b l", b=nblk)
        dve.emit(lambda: nc.vector.tensor_scalar(out=xa, in0=xr[:, :, 0:Lb], scalar1=cw[:, ct_global, 0:1],
                                                 scalar2=None, op0=ALU.mult), r=[rr, R("cw")], w=[rxa])
        for i in range(1, 4):
            dve.emit(lambda: nc.vector.scalar_tensor_tensor(out=xa, in0=xr[:, :, i:i + Lb],
                                                            scalar=cw[:, ct_global, i:i + 1], in1=xa, op0=ALU.mult,
                                                            op1=ALU.add), r=[rr, R("cw"), rxa], w=[rxa])
        act.emit(lambda: nc.scalar.activation(out=dst, in_=xacc[:, 0:T], func=AF.Silu,
                                              bias=cb[:, ct_global:ct_global + 1], scale=1.0),
                 r=[rxa, R("cb")], w=[dstreg])

    def ssd_phase(T, mode, sample_ctx=None):
        ntile = T // 128
        sample = mode == "sample"
        C = cstb if sample else cst
        creg = R("cstb") if sample else R("cst")
        UT, SL, CM = C[:, 1, :], C[:, 2, :], C[:, 3, :]
        nblk = 16 if sample else 1
        wsl, wreg, ex = wload_multi([(w_in[:, OFF_DT:OFF_DT + 64], wview(16, 64))])
        wv = wview(16, 64)(wsl)
        for i in range(ntile):
            def ev(pb, rb, i=i):
                dve.emit(lambda: nc.vector.tensor_tensor(out=dtt[:, i, :], in0=pb[:, 0:64], in1=dtb[:], op=ALU.add),
                         r=[rb, R("dtb")], w=[R("dtt")])
                dve.emit(lambda: nc.vector.scalar_tensor_tensor(out=et[:], in0=dtt[:, i, :], scalar=-1.0,
                                                                in1=dtt[:, i, :], op0=ALU.mult, op1=ALU.max),
                         r=[R("dtt")], w=[R("et")])
                act.emit(lambda: nc.scalar.activation(out=et[:], in_=et[:], func=AF.Exp, scale=-1.0),
                         r=[R("et")], w=[R("et")])
                act.emit(lambda: nc.scalar.activation(out=et[:], in_=et[:], func=AF.Ln, bias=1.0, scale=1.0),
                         r=[R("et")], w=[R("et")])
                dve.emit(lambda: nc.vector.scalar_tensor_tensor(out=dtt[:, i, :], in0=dtt[:, i, :], scalar=0.0,
                                                                in1=et[:], op0=ALU.max, op1=ALU.add),
                         r=[R("dtt"), R("et")], w=[R("dtt")])
                dve.emit(lambda: nc.vector.tensor_tensor(out=dAt[:, i, :], in0=dtt[:, i, :], in1=Arow[:],
                                                         op=ALU.mult), r=[R("dtt"), R("Arow")], w=[R("dAt")])
            proj_T(wreg, wv, 64, i * 128, ev)

        def gen_A(g, par):
            xcT, BT, CT = xcTL[par], BTL[par], CTL[par]
            rxcT, rBT, rCT = R(f"xcT{par}"), R(f"BT{par}"), R(f"CT{par}")
            for half in range(2):
                c0 = OFF_XBC + 512 * g + 256 * half
                wsl, wreg, _ = wload_multi([(w_in[:, c0:c0 + 256], wview(16, 256))])
                wv = wview(16, 256)(wsl)

                def evx(ct, pb, rb, half=half):
                    ctg = 4 * g + 2 * half + ct
                    conv_tile(ctg, pb, rb, T, nblk, xcT[:, 2 * half + ct, 0:T], rxcT, ct % 2,
                              ctx_src=(lambda xr, rr, ctg=ctg: sample_ctx(ctg, xr, rr)) if sample else None)
                proj_F(wsl, wreg, wv, 2, T, evx)
                yield
            cB = OFF_XBC + DI + 128 * g
            cC = OFF_XBC + DI + GN + 128 * g
            wsl, wreg, ex = wload_multi([(w_in[:, cB:cB + 128], wview(16, 128, 0, 256)),
                                         (w_in[:, cC:cC + 128], wview(16, 128, 128, 256))])
            wait_extra(pe, ex)
            wv = wview(16, 256)(wsl)

            def evbc(ct, pb, rb):
                ctg = (32 + g) if ct == 0 else (40 + g)
                conv_tile(ctg, pb, rb, T, nblk, (BT if ct == 0 else CT)[:, 0:T], rBT if ct == 0 else rCT,
                          2 + ct, ctx_src=(lambda xr, rr, ctg=ctg: sample_ctx(ctg, xr, rr)) if sample else None)
            proj_F(wsl, wreg, wv, 2, T, evbc)
            yield
            if mode != "state":
                c0 = OFF_Z + 512 * g
                wslz = []
                for half in range(2):
                    wsl, wreg, _ = wload_multi([(w_in[:, c0 + 256 * half:c0 + 256 * (half + 1)], wview(16, 256))])
                    wslz.append((wsl, wreg))
                zw[par] = wslz
            yield


        def gen_B(g, par):
            xcT, BT, CT = xcTL[par], BTL[par], CTL[par]
            rxcT, rBT, rCT = R(f"xcT{par}"), R(f"BT{par}"), R(f"CT{par}")
            cur.update(CT=CT, rCT=rCT)
            wslz = zw.get(par)
            if mode != "state":
                for c in range(ntile):
                    t0 = c * 128
                    zs = zsL[c]
                    rzs = R(f"zs{c}")
                    for half in range(2):
                        wsl, wreg = wslz[half]
                        wv = wview(16, 256)(wsl)

                        def evz(pb, rb, half=half, zs=zs, rzs=rzs):
                            act.emit(lambda: nc.scalar.activation(out=zs[:, 256 * half:256 * (half + 1)],
                                                                  in_=pb[:, 0:256], func=AF.Silu),
                                     r=[rb], w=[rzs])
                        proj_T(wreg, wv, 256, t0, evz)
            for c in range(ntile):
                t0 = c * 128
                ci = cur["ci"] % 2
                cur["ci"] += 1
                zs, wsg, dtw, Xtm, Xdt, wX, Btm = zsL[c], wsgL[ci], dtwL[ci], XtmL[ci], XdtL[ci], wXL[ci], BtmL[ci]
                rzs, rwsg, rdtw, rXtm, rXdt, rwX, rBtm = (R(f"zs{c}"), R(f"wsg{ci}"), R(f"dtw{ci}"), R(f"Xtm{ci}"),
                                                          R(f"Xdt{ci}"), R(f"wX{ci}"), R(f"Btm{ci}"))
                cur.update(Btm=Btm, rBtm=rBtm, wX=wX, rwX=rwX)
                yield
                pbS, rbS = nb_()
                dA_g = dAt[:, c, 8 * g:8 * g + 8]
                pe.emit(lambda: nc.tensor.matmul(pbS[:, 0:8], lhsT=SL, rhs=dA_g, start=True, stop=True),
                        r=[creg, R("dAt")], w=[rbS])
                pe.emit(lambda: nc.tensor.matmul(pbS[:, 8:16], lhsT=UT, rhs=dA_g, start=True, stop=True),
                        r=[creg, R("dAt")], w=[rbS])
                if not sample:
                    pe.emit(lambda: nc.tensor.matmul(pbS[:, 16:24], lhsT=C[:, 4, :], rhs=dA_g, start=True, stop=True),
                            r=[creg, R("dAt")], w=[rbS])
                ncs = 16 if sample else 24
                act.emit(lambda: nc.scalar.activation(out=wsg[:, 0:ncs], in_=pbS[:, 0:ncs], func=AF.Exp),
                         r=[rbS], w=[rwsg])
                dve.emit(lambda: nc.vector.tensor_tensor(out=dtw[:, 0:8], in0=dtt[:, c, 8 * g:8 * g + 8],
                                                         in1=wsg[:, 0:8], op=ALU.mult),
                         r=[R("dtt"), rwsg], w=[rdtw])
                pbX, rbX = nb_()
                for j in range(4):
                    pe.emit(lambda: nc.tensor.transpose(out=bankb[:, j * 128:(j + 1) * 128],
                                                        in_=xcT[:, j, t0:t0 + 128], identity=cst_bf[:]),
                            r=[rxcT, R("cst_bf")], w=[RBb])
                pe.emit(lambda: nc.tensor.transpose(out=bankb[:, 512:640], in_=BT[:, t0:t0 + 128], identity=cst_bf[:]),
                        r=[rBT, R("cst_bf")], w=[RBb])
                act.emit(lambda: nc.scalar.copy(out=Xtm[:], in_=bankb[:, 0:512]), r=[RBb], w=[rXtm])
                act.emit(lambda: nc.scalar.copy(out=Btm[:], in_=bankb[:, 512:640]), r=[RBb], w=[rBtm])
                X3 = Xtm[:].rearrange("p (h d) -> p h d", h=8)
                dve.emit(lambda: nc.vector.tensor_tensor(out=wX[:].rearrange("p (h d) -> p h d", h=8), in0=X3,
                                                         in1=dtw[:, 0:8].unsqueeze(2).to_broadcast([128, 8, 64]),
                                                         op=ALU.mult), r=[rXtm, rdtw], w=[rwX])
                yield
                if mode != "state":
                    dve.emit(lambda: nc.vector.tensor_tensor(out=Xdt[:].rearrange("p (h d) -> p h d", h=8), in0=X3,
                                                             in1=dtt[:, c, 8 * g:8 * g + 8].unsqueeze(2).to_broadcast(
                                                                 [128, 8, 64]), op=ALU.mult),
                             r=[rXtm, R("dtt")], w=[rXdt])
                    dve.emit(lambda: nc.vector.tensor_tensor(out=rseg[:],
                                                             in0=dA_g.unsqueeze(2).to_broadcast([128, 8, 128]),
                                                             in1=UT.unsqueeze(1).to_broadcast([128, 8, 128]),
                                                             op=ALU.mult), r=[R("dAt"), creg], w=[R("rseg")])
                    pbs = []
                    for hh in range(2):
                        pb, rb = nb_()
                        pbs.append((pb, rb))
                        pe.emit(lambda: nc.tensor.matmul(pb[:], lhsT=SL,
                                                         rhs=rseg[:, 4 * hh:4 * hh + 4, :].rearrange("p h l -> p (h l)"),
                                                         start=True, stop=True), r=[creg, R("rseg")], w=[rb])
                    for hh in range(2):
                        pb, rb = pbs[hh]
                        act.emit(lambda: nc.scalar.activation(
                            out=LT[:, 4 * hh:4 * hh + 4, :].rearrange("p h l -> p (h l)"), in_=pb[:], func=AF.Exp),
                            r=[rb], w=[R("LT")])
                    yield
                    pbC, rbC = nb_()
                    pe.emit(lambda: nc.tensor.matmul(pbC[:, 0:128], lhsT=BT[:, t0:t0 + 128], rhs=CT[:, t0:t0 + 128],
                                                     start=True, stop=True), r=[rBT, rCT], w=[rbC])
                    dve.emit(lambda: nc.vector.tensor_tensor(out=CBm[:], in0=pbC[:, 0:128], in1=CM, op=ALU.mult),
                             r=[rbC, creg], w=[R("CBm")])
                    dve.emit(lambda: nc.vector.tensor_tensor(out=MT[:], in0=LT[:],
                                                             in1=CBm[:].unsqueeze(1).to_broadcast([128, 8, 128]),
                                                             op=ALU.mult), r=[R("LT"), R("CBm")], w=[R("MT")])
                    pbY, rbY = nb_()
                    for h in range(8):
                        pe.emit(lambda: nc.tensor.matmul(pbY[:, h * 64:(h + 1) * 64], lhsT=MT[:, h, :],
                                                         rhs=Xdt[:, h * 64:(h + 1) * 64], start=True, stop=True),
                                r=[R("MT"), rXdt], w=[rbY])
                    pbO, rbO = nb_()
                    if not sample:
                        dve.emit(lambda: nc.vector.tensor_copy(out=state_bf[:], in_=state[:, g, :]),
                                 r=[R("state")], w=[R("state_bf")])
                        pe.emit(lambda: nc.tensor.matmul(pbO[:], lhsT=CT[:, t0:t0 + 128], rhs=state_bf[:], start=True,
                                                         stop=True), r=[rCT, R("state_bf")], w=[rbO])
                    else:
                        reserved.add(bank_index(pbY))
                        sample_ctx("yoff", g, pbO, rbO)
                        reserved.discard(bank_index(pbY))
                    dve.emit(lambda: nc.vector.tensor_tensor(out=ta[:].rearrange("p (h d) -> p h d", h=8),
                                                             in0=pbO[:].rearrange("p (h d) -> p h d", h=8),
                                                             in1=wsg[:, 8:16].unsqueeze(2).to_broadcast([128, 8, 64]),
                                                             op=ALU.mult), r=[rbO, rwsg], w=[R("ta")])
                    dve.emit(lambda: nc.vector.tensor_tensor(out=ta[:], in0=ta[:], in1=pbY[:], op=ALU.add),
                             r=[rbY, R("ta")], w=[R("ta")])
                    dve.emit(lambda: nc.vector.tensor_tensor(out=tb[:].rearrange("p (h d) -> p h d", h=8), in0=X3,
                                                             in1=Drow[:, 8 * g:8 * g + 8].unsqueeze(2).to_broadcast(
                                                                 [128, 8, 64]), op=ALU.mult),
                             r=[rXtm, R("Drow")], w=[R("tb")])
                    dve.emit(lambda: nc.vector.tensor_tensor(out=ta[:], in0=ta[:], in1=tb[:], op=ALU.add),
                             r=[R("ta"), R("tb")], w=[R("ta")])
                    yield
                    dve.emit(lambda: nc.vector.tensor_tensor(out=ta[:], in0=ta[:], in1=zs[:], op=ALU.mult),
                             r=[R("ta"), rzs], w=[R("ta")])
                    act.emit(lambda: nc.scalar.activation(out=tb[:], in_=ta[:], func=AF.Square,
                                                          accum_out=st4[:, 2:3]), r=[R("ta")], w=[R("tb"), R("st4")])
                    rstd_from_ss(st4[:, 2:3], 512, st4[:, 3:4], "st4")
                    dve.emit(lambda: nc.vector.tensor_scalar(out=tc_[:], in0=ta[:], scalar1=st4[:, 3:4], scalar2=None,
                                                             op0=ALU.mult), r=[R("ta"), R("st4")], w=[R("tc")])
                    pbT, rbT = nb_()
                    for j in range(4):
                        pe.emit(lambda: nc.tensor.transpose(out=pbT[:, j * 128:(j + 1) * 128],
                                                            in_=tc_[:, j * 128:(j + 1) * 128], identity=ident),
                                r=[R("tc"), R("cst")], w=[rbT])
                    for j in range(4):
                        act.emit(lambda: nc.scalar.activation(out=ynT[:, 4 * g + j, t0:t0 + 128],
                                                              in_=pbT[:, j * 128:(j + 1) * 128], func=AF.Identity,
                                                              scale=snw[:, 4 * g + j:4 * g + j + 1]),
                                 r=[rbT, R("snw")], w=[R("ynT")])
                yield
                if not sample:
                    pbN, rbN = nb_()
                    pe.emit(lambda: nc.tensor.matmul(pbN[:], lhsT=Btm[:], rhs=wX[:], start=True, stop=True),
                            r=[rBtm, rwX], w=[rbN])
                    S3 = state[:, g, :].rearrange("p (h d) -> p h d", h=8)
                    dve.emit(lambda: nc.vector.tensor_tensor(out=S3, in0=S3,
                                                             in1=wsg[:, 16:24].unsqueeze(2).to_broadcast([128, 8, 64]),
                                                             op=ALU.mult), r=[R("state"), rwsg], w=[R("state")])
                    dve.emit(lambda: nc.vector.tensor_tensor(out=state[:, g, :], in0=state[:, g, :], in1=pbN[:],
                                                             op=ALU.add), r=[R("state"), rbN], w=[R("state")])
                else:
                    sample_ctx("state", g, None, None)


        zw = {}

        def run(gen):
            for _ in gen:
                pass

        if sample:
            for g in range(NG):
                run(gen_A(g, 0))
                run(gen_B(g, 0))
        else:
            run(gen_A(0, 0))
            for g in range(NG):
                ga_ = gen_A(g + 1, (g + 1) % 2) if g + 1 < NG else iter(())
                gb_ = gen_B(g, g % 2)
                alive_a, alive_b = True, True
                while alive_a or alive_b:
                    if alive_b:
                        try:
                            next(gb_)
                        except StopIteration:
                            alive_b = False
                    if alive_a:
                        try:
                            next(ga_)
                        except StopIteration:
                            alive_a = False

    def attn_phase(T, first_chunk_mask, kv_only=False, emit_kv=False):
        ntile = T // 128
        for jp in range(4):
            ka = OFF_K + 64 * (2 * jp)
            kb_ = ka + 64
            va = OFF_V + 128 * jp
            wsl, wreg, ex = wload_multi([
                (w_in[:, ka:ka + 64], wview(16, 64, 0, 256)), (w_in[:, ka:ka + 64], wview(16, 64, 64, 256)),
                (w_in[:, kb_:kb_ + 64], wview(16, 64, 128, 256)), (w_in[:, kb_:kb_ + 64], wview(16, 64, 192, 256))])
            wait_extra(pe, ex)
            wv = wview(16, 256)(wsl)
            tlo = T - 128 if kv_only else 0
            for a in range(2):
                pb, rb = nb_()
                for kk in range(16):
                    pe.emit(lambda: nc.tensor.matmul(pb[:, 0:T - tlo], lhsT=wv[:, kk, a * 128:(a + 1) * 128],
                                                     rhs=hT[:, kk, tlo:T], start=(kk == 0), stop=(kk == 15)),
                            r=[wreg, R("hT")], w=[rb])
                if not kv_only:
                    dve.emit(lambda: nc.vector.tensor_copy(out=KT[:, a, 0:128], in_=kcar[:, 2 * jp + a, :]),
                             r=[R("kcar")], w=[R("KT")])
                    act.emit(lambda: nc.scalar.copy(out=KT[:, a, 128:128 + T], in_=pb[:, 0:T]), r=[rb], w=[R("KT")])
                    dve.emit(lambda: nc.vector.tensor_copy(out=kcar[:, 2 * jp + a, :], in_=KT[:, a, T:T + 128]),
                             r=[R("KT")], w=[R("kcar")])
                else:
                    act.emit(lambda: nc.scalar.copy(out=kcar[:, 2 * jp + a, :], in_=pb[:, 0:128]), r=[rb],
                             w=[R("kcar")])
            if emit_kv:
                pbk, rbk = nb_()
                for kk in range(16):
                    pe.emit(lambda: nc.tensor.matmul(pbk[:, 0:256], lhsT=hT[:, kk, T - 128:T],
                                                     rhs=wv[:, kk, 0:256], start=(kk == 0), stop=(kk == 15)),
                            r=[wreg, R("hT")], w=[rbk])
                kt3 = ktok[:, 128 * jp:128 * (jp + 1)].rearrange("p (a d) -> p a d", a=2)
                dve.emit(lambda: nc.vector.tensor_scalar(
                    out=kt3, in0=pbk[:, 0:256].rearrange("p (a e d) -> p a e d", a=2, e=2)[:, :, 0, :], scalar1=1.0,
                    scalar2=None, op0=ALU.mult), r=[rbk], w=[R("ktok")])
            wsl, wreg, _ = wload_multi([(w_in[:, va:va + 128], wview(16, 128))])
            wv = wview(16, 128)(wsl)
            if not kv_only:
                dve.emit(lambda: nc.vector.tensor_copy(out=Vt[:, 0, :], in_=vcar[:, 128 * jp:128 * (jp + 1)]),
                         r=[R("vcar")], w=[R("Vt")])
            for i in range(tlo // 128, ntile):
                def evv(pb, rb, i=i):
                    if not kv_only:
                        act.emit(lambda: nc.scalar.copy(out=Vt[:, 1 + i, :], in_=pb[:, 0:128]), r=[rb], w=[R("Vt")])
                    if i == ntile - 1:
                        act.emit(lambda: nc.scalar.copy(out=vcar[:, 128 * jp:128 * (jp + 1)], in_=pb[:, 0:128]),
                                 r=[rb], w=[R("vcar")])
                        if emit_kv:
                            act.emit(lambda: nc.scalar.copy(out=ktok[:, 512 + 128 * jp:512 + 128 * (jp + 1)],
                                                            in_=pb[:, 0:128]), r=[rb], w=[R("ktok")])
                pbv, rbv = nb_()
                for kk in range(16):
                    pe.emit(lambda: nc.tensor.matmul(pbv[:, 0:128], lhsT=hT[:, kk, i * 128:(i + 1) * 128],
                                                     rhs=wv[:, kk, 0:128], start=(kk == 0), stop=(kk == 15)),
                            r=[wreg, R("hT")], w=[rbv])
                evv(pbv, rbv)
            if kv_only:
                continue
            for half in range(2):
                c0 = OFF_Q + 512 * jp + 256 * half
                wsl, wreg, _ = wload_multi([(w_in[:, c0:c0 + 256], wview(16, 256))])
                wv = wview(16, 256)(wsl)

                def evq(ct, pb, rb, half=half):
                    act.emit(lambda: nc.scalar.activation(out=QT[:, 2 * half + ct, 0:T], in_=pb[:, 0:T], func=AF.Copy,
                                                          scale=0.125), r=[rb], w=[R("QT")])
                proj_F(wsl, wreg, wv, 2, T, evq)
            for half in range(2):
                c0 = OFF_GA + 512 * jp + 256 * half
                wsl, wreg, _ = wload_multi([(w_in[:, c0:c0 + 256], wview(16, 256))])
                wv = wview(16, 256)(wsl)

                def evg(ct, pb, rb, half=half):
                    act.emit(lambda: nc.scalar.activation(out=gaT[:, 2 * half + ct, 0:T], in_=pb[:, 0:T],
                                                          func=AF.Silu), r=[rb], w=[R("gaT")])
                proj_F(wsl, wreg, wv, 2, T, evg)
            items = [(c, hq) for c in range(ntile) for hq in range(8)]
            ostate = {}

            def S1(i):
                c, hq = items[i]
                a = hq // 4
                qt = hq // 2
                base = 64 * (hq % 2)
                hglob = 8 * jp + hq
                bi = i % NAB
                pb, rb = nb_()
                pe.emit(lambda: nc.tensor.matmul(pb[:, 0:256], lhsT=QT[base:base + 64, qt, c * 128:(c + 1) * 128],
                                                 rhs=KT[base:base + 64, a, c * 128:c * 128 + 256], start=True,
                                                 stop=True), r=[R("QT"), R("KT")], w=[rb])
                rs = R(f"sc{bi}")
                dve.emit(lambda: nc.vector.tensor_tensor(out=sc[bi][:], in0=pb[:, 0:256], in1=bias[:, hglob, :],
                                                         op=ALU.add), r=[rb, R("bias")], w=[rs])
                if first_chunk_mask and c == 0:
                    dve.emit(lambda: nc.vector.tensor_scalar(out=sc[bi][:, 0:128], in0=sc[bi][:, 0:128],
                                                             scalar1=flag[:, 1:2], scalar2=None, op0=ALU.add),
                             r=[rs, R("flag")], w=[rs])
                rsm = R(f"sm{bi}")
                o4 = 4 * bi
                dve.emit(lambda: nc.vector.reduce_max(out=sm[:, o4:o4 + 1], in_=sc[bi][:], axis=AX.X),
                         r=[rs], w=[rsm])
                dve.emit(lambda: nc.vector.tensor_scalar(out=sm[:, o4:o4 + 1], in0=sm[:, o4:o4 + 1],
                                                         scalar1=sinkb[:, hglob:hglob + 1], scalar2=-1.0,
                                                         op0=ALU.max, op1=ALU.mult),
                         r=[rsm, R("sinkb")], w=[rsm])
                act.emit(lambda: nc.scalar.activation(out=sc[bi][:], in_=sc[bi][:], func=AF.Exp,
                                                      bias=sm[:, o4:o4 + 1], scale=1.0,
                                                      accum_out=sm[:, o4 + 1:o4 + 2]), r=[rs, rsm], w=[rs, rsm])
                act.emit(lambda: nc.scalar.activation(out=sm[:, o4 + 2:o4 + 3], in_=sinkb[:, hglob:hglob + 1],
                                                      func=AF.Exp, bias=sm[:, o4:o4 + 1], scale=1.0),
                         r=[rsm, R("sinkb")], w=[rsm])

            def S1c(i):
                bi = i % NAB
                rs = R(f"sc{bi}")
                rsm = R(f"sm{bi}")
                o4 = 4 * bi
                dve.emit(lambda: nc.vector.tensor_tensor(out=sm[:, o4 + 1:o4 + 2], in0=sm[:, o4 + 1:o4 + 2],
                                                         in1=sm[:, o4 + 2:o4 + 3], op=ALU.add), r=[rsm], w=[rsm])
                dve.emit(lambda: nc.vector.reciprocal(out=sm[:, o4 + 3:o4 + 4], in_=sm[:, o4 + 1:o4 + 2]),
                         r=[rsm], w=[rsm])
                act.emit(lambda: nc.scalar.activation(out=Pn[bi][:], in_=sc[bi][:], func=AF.Identity,
                                                      scale=sm[:, o4 + 3:o4 + 4]), r=[rs, rsm], w=[R(f"Pn{bi}")])

            def S2(i):
                bi = i % NAB
                pbt, rbP = nb_()
                ptb = pbt[:].bitcast(BF16)
                for blk in range(2):
                    pe.emit(lambda: nc.tensor.transpose(
                        out=ptb[:, blk * 128:(blk + 1) * 128],
                        in_=Pn[bi][:, blk * 128:(blk + 1) * 128], identity=cst_bf[:]),
                        r=[R(f"Pn{bi}"), R("cst_bf")], w=[rbP])
                act.emit(lambda: nc.scalar.copy(out=PT[bi][:], in_=ptb[:, 0:256]), r=[rbP],
                         w=[R(f"PT{bi}")])

            def S3(i):
                c, hq = items[i]
                a = hq // 4
                qt = hq // 2
                base = 64 * (hq % 2)
                bi = i % NAB
                if hq % 2 == 0:
                    ostate["o"] = nb_()
                    reserved.add(bank_index(ostate["o"][0]))
                pbo, rbo = ostate["o"]
                for blk in range(2):
                    pe.emit(lambda: nc.tensor.matmul(pbo[base:base + 64, 0:128],
                                                     lhsT=Vt[:, c + blk, 64 * a:64 * a + 64],
                                                     rhs=PT[bi][:, blk * 128:(blk + 1) * 128], start=(blk == 0),
                                                     stop=(blk == 1)), r=[R("Vt"), R(f"PT{bi}")], w=[rbo])
                if hq % 2 == 1:
                    dve.emit(lambda: nc.vector.tensor_tensor(out=oT[:, 4 * jp + qt, c * 128:(c + 1) * 128],
                                                             in0=pbo[:, 0:128],
                                                             in1=gaT[:, qt, c * 128:(c + 1) * 128], op=ALU.mult),
                             r=[rbo, R("gaT")], w=[R("oT")])
                    reserved.discard(bank_index(pbo))

            nit = len(items)
            for step in range(nit + 4):
                if step < nit:
                    S1(step)
                if 0 <= step - 1 < nit:
                    S1c(step - 1)
                if 0 <= step - 3 < nit:
                    S2(step - 3)
                if 0 <= step - 4 < nit:
                    S3(step - 4)

    def phase_de(T, xsrc, ydst, sample=False):
        ntile = T // 128
        t1b = [sgs, sga]
        for jp2 in range(8):
            c0 = 256 * jp2
            wl = []
            for hk in range(2):
                wsl, wreg, _ = wload_multi([(w_ssm[2048 * hk:2048 * (hk + 1), c0:c0 + 256], wview(16, 256))])
                wl.append((wview(16, 256)(wsl), wreg))
            pAs = []
            for t in range(2):
                pA, rA = nb_()
                reserved.add(bank_index(pA))
                for kk in range(32):
                    wv, wreg = wl[kk // 16]
                    pe.emit(lambda: nc.tensor.matmul(pA[:, 0:T], lhsT=wv[:, kk % 16, t * 128:(t + 1) * 128],
                                                     rhs=ynT[:, kk, 0:T], start=(kk == 0), stop=(kk == 31)),
                            r=[wreg, R("ynT")], w=[rA])
                pAs.append((pA, rA))
            pBs = []
            if sample:
                wl = []
                for hk in range(2):
                    wsl, wreg, _ = wload_multi([(w_attn[1024 * hk:1024 * (hk + 1), c0:c0 + 256],
                                                 lambda sl: sl[0:64, :].rearrange("p (k c) -> p k c", k=16), 64)])
                    wl.append((wsl[0:64, :].rearrange("p (k c) -> p k c", k=16), wreg))
                for t in range(2):
                    pB, rB = nb_()
                    reserved.add(bank_index(pB))
                    for kk in range(32):
                        wv, wreg = wl[kk // 16]
                        pe.emit(lambda: nc.tensor.matmul(pB[:, 0:T], lhsT=wv[:, kk % 16, t * 128:(t + 1) * 128],
                                                         rhs=oTs[:, kk, 0:T], start=(kk == 0), stop=(kk == 31)),
                                r=[wreg, R("oTs")], w=[rB])
                    pBs.append((pB, rB))
            else:
                wsl, wreg, _ = wload_multi([(w_attn[:, c0:c0 + 256], wview(16, 256))])
                wv = wview(16, 256)(wsl)
                for t in range(2):
                    pB, rB = nb_()
                    reserved.add(bank_index(pB))
                    for kk in range(16):
                        pe.emit(lambda: nc.tensor.matmul(pB[:, 0:T], lhsT=wv[:, kk, t * 128:(t + 1) * 128],
                                                         rhs=oT[:, kk, 0:T], start=(kk == 0), stop=(kk == 15)),
                                r=[wreg, R("oT")], w=[rB])
                    pBs.append((pB, rB))
            wsl, wreg, _ = wload_multi([(w_in[:, OFF_MS + c0:OFF_MS + c0 + 256], wview(16, 256))])
            wv = wview(16, 256)(wsl)
            for t in range(2):
                pC, rC = nb_()
                for kk in range(16):
                    pe.emit(lambda: nc.tensor.matmul(pC[:, 0:T], lhsT=wv[:, kk, t * 128:(t + 1) * 128],
                                                     rhs=hT[:, kk, 0:T], start=(kk == 0), stop=(kk == 15)),
                            r=[wreg, R("hT")], w=[rC])
                rt1 = R(f"t1b{t}")
                act.emit(lambda: nc.scalar.activation(out=t1b[t][:, 0:T], in_=pC[:, 0:T], func=AF.Sigmoid), r=[rC],
                         w=[rt1])
                pA, rA = pAs[t]
                dve.emit(lambda: nc.vector.tensor_tensor(out=t1b[t][:, 0:T], in0=t1b[t][:, 0:T], in1=pA[:, 0:T],
                                                         op=ALU.mult), r=[rt1, rA], w=[rt1])
                reserved.discard(bank_index(pA))
            wsl, wreg, _ = wload_multi([(w_in[:, OFF_MA + c0:OFF_MA + c0 + 256], wview(16, 256))])
            wv = wview(16, 256)(wsl)
            for t in range(2):
                pD, rD = nb_()
                for kk in range(16):
                    pe.emit(lambda: nc.tensor.matmul(pD[:, 0:T], lhsT=wv[:, kk, t * 128:(t + 1) * 128],
                                                     rhs=hT[:, kk, 0:T], start=(kk == 0), stop=(kk == 15)),
                            r=[wreg, R("hT")], w=[rD])
                act.emit(lambda: nc.scalar.activation(out=sgt[:, 0:T], in_=pD[:, 0:T], func=AF.Sigmoid), r=[rD],
                         w=[R("sgt")])
                pB, rB = pBs[t]
                dve.emit(lambda: nc.vector.tensor_tensor(out=sgt[:, 0:T], in0=sgt[:, 0:T], in1=pB[:, 0:T],
                                                         op=ALU.mult), r=[R("sgt"), rB], w=[R("sgt")])
                reserved.discard(bank_index(pB))
                dve.emit(lambda: nc.vector.tensor_tensor(out=mT[:, 2 * jp2 + t, 0:T], in0=t1b[t][:, 0:T],
                                                         in1=sgt[:, 0:T], op=ALU.add),
                         r=[R(f"t1b{t}"), R("sgt")], w=[R("mT")])
        for ip in range(0, ntile, 2):
            tl = list(range(ip, min(ip + 2, ntile)))
            for i in tl:
                sp.dma(xo[i % 2][:], xsrc[i * 128:(i + 1) * 128, :], w=[R(["xt", "xn"][i % 2])])
            for cbk in range(8):
                c0 = 256 * cbk
                wsl, wreg, _ = wload_multi([(w_out[:, c0:c0 + 256], wview(16, 256))])
                wv = wview(16, 256)(wsl)
                for i in tl:
                    pb, rb = nb_()
                    for kk in range(16):
                        pe.emit(lambda: nc.tensor.matmul(pb[:, 0:256], lhsT=mT[:, kk, i * 128:(i + 1) * 128],
                                                         rhs=wv[:, kk, :], start=(kk == 0), stop=(kk == 15)),
                                r=[wreg, R("mT")], w=[rb])
                    dve.emit(lambda: nc.vector.tensor_tensor(out=xo[i % 2][:, c0:c0 + 256],
                                                             in0=xo[i % 2][:, c0:c0 + 256], in1=pb[:, 0:256],
                                                             op=ALU.add), r=[rb, R(["xt", "xn"][i % 2])], w=[R(["xt", "xn"][i % 2])])
            for i in tl:
                rx = R(["xt", "xn"][i % 2])
                act.emit(lambda: nc.scalar.activation(out=junkE, in_=xo[i % 2][:], func=AF.Square,
                                                      accum_out=st4[:, 4:5]), r=[rx], w=[R("ynT"), R("st4")])
                rstd_from_ss(st4[:, 4:5], D, st4[:, 5:6], "st4")
                dve.emit(lambda: nc.vector.scalar_tensor_tensor(out=xo[i % 2][:], in0=xo[i % 2][:],
                                                                scalar=st4[:, 5:6], in1=fnw[:], op0=ALU.mult,
                                                                op1=ALU.mult), r=[rx, R("st4"), R("fnw")], w=[rx])
                sp.dma(ydst[i * 128:(i + 1) * 128, :], xo[i % 2][:], r=[rx], w=[R("yout")])

    def raw_xbc_rows(tok0, dst):
        for cbk in range(24):
            c0 = OFF_XBC + 256 * cbk
            wsl, wreg, _ = wload_multi([(w_in[:, c0:c0 + 256], wview(16, 256))])
            wv = wview(16, 256)(wsl)

            def ev(pb, rb, cbk=cbk):
                act.emit(lambda: nc.scalar.copy(out=xt[:, (cbk % 8) * 256:(cbk % 8 + 1) * 256], in_=pb[:, 0:256]),
                         r=[rb], w=[R("xt")])
            proj_T(wreg, wv, 256, tok0, ev)
            if cbk % 8 == 7:
                o0 = 2048 * (cbk // 8)
                sp.dma(dst[:, o0:o0 + 2048], xt[:], r=[R("xt")], w=[R("convout")])


    pools_ = [[bias[:].rearrange("p h c -> p (h c)"), 32 * 256, 0],
              [state[:].rearrange("p g c -> p (g c)"), NG * 512, 0],
              [oT[:].rearrange("p k t -> p (k t)").bitcast(F32), 16 * TM // 2, 0],
              [QT[:].rearrange("p k t -> p (k t)").bitcast(F32), 4 * TM // 2, 0],
              [gaT[:].rearrange("p k t -> p (k t)").bitcast(F32), 4 * TM // 2, 0],
              [KT[:].rearrange("p k t -> p (k t)").bitcast(F32), (128 + TM), 0]]
    for i_ in range(NAB):
        pools_.append([sc[i_][:], 256, 0])
        pools_.append([Pn[i_][:].bitcast(F32), 128, 0])
        pools_.append([PT[i_][:].bitcast(F32), 128, 0])
    sample_regs = []
    specs = [
        ("oTs", [64, 32, 128], BF16), ("cstb", [128, 5, 128], F32), ("sct", [48, 768], F32),
        ("s0n0", [128, 4, 128], F32), ("s0n1", [128, 4, 128], F32), ("s0n2", [128, 4, 128], F32),
        ("s0n3", [128, 4, 128], F32), ("dAx", [128, 512], F32),
        ("decP", [128, 64], F32), ("S0bf0", [128, 512], BF16), ("S0bf1", [128, 512], BF16),
        ("fin0", [128, 512], F32), ("fin1", [128, 512], F32), ("wXm0", [128, 512], BF16), ("wXm1", [128, 512], BF16),
        ("cmask", [128, 2048], BF16),
        ("CTm", [128, 16, 128], BF16), ("bsel", [128, 16], F32), ("QTs", [64, 1024], BF16), ("gaS", [64, 8, 128], BF16),
        ("KTn", [64, 2, 128], BF16), ("biasS", [32, 8, 136], F32), ("sinkS", [32, 8], F32),
        ("smS", [32, 16], F32), ("Vnb", [8, 16, 128], BF16),
    ]
    for i_ in range(4):
        specs += [(f"ckbf{i_}", [128, 128], BF16), (f"cvbf{i_}", [128, 128], BF16), (f"KTc{i_}", [64, 2, 128], BF16),
                  (f"scS{i_}", [32, 136], F32), (f"PnS{i_}", [32, 136], BF16), (f"PTs{i_}", [128, 32], BF16),
                  (f"PTn{i_}", [8, 32], BF16)]

    def _words(shape, dt):
        n = 1
        for d_ in shape[1:]:
            n *= d_
        w_ = (n * (2 if dt == BF16 else 4) + 3) // 4
        return n, (w_ + 7) // 8 * 8

    SV = {}
    for (name, shape, dt) in sorted(specs, key=lambda t: -_words(t[1], t[2])[1]):
        n, words = _words(shape, dt)
        for pl in pools_:
            if pl[2] + words <= pl[1]:
                o = pl[2]
                pl[2] += words
                flat = pl[0][0:shape[0], o:o + words]
                break
        else:
            raise ValueError("carve overflow " + name)
        flat = flat.bitcast(BF16)[:, 0:n] if dt == BF16 else flat[:, 0:n]
        sample_regs.append(R(name))
        SV[name] = flat if len(shape) == 2 else flat.rearrange("p (a b) -> p a b", a=shape[1])
    oTs, cstb, sct, dAx, decP, cmask, CTm, bsel = (SV[n_] for n_ in ("oTs", "cstb", "sct", "dAx", "decP", "cmask", "CTm",
                                                                     "bsel"))
    s0n = [SV[f"s0n{i_}"] for i_ in range(4)]
    S0bf = [SV["S0bf0"], SV["S0bf1"]]
    fin = [SV["fin0"], SV["fin1"]]
    wXm = [SV["wXm0"], SV["wXm1"]]
    QTs, gaS, KTn, biasS, sinkS, smS, Vnb = (SV[n_] for n_ in ("QTs", "gaS", "KTn", "biasS", "sinkS", "smS", "Vnb"))
    QTs4 = QTs.rearrange("p (b h l) -> p b h l", b=16, h=8)
    ckbf = [SV[f"ckbf{i_}"] for i_ in range(4)]
    cvbf = [SV[f"cvbf{i_}"] for i_ in range(4)]
    KTc = [SV[f"KTc{i_}"] for i_ in range(4)]
    scS = [SV[f"scS{i_}"] for i_ in range(4)]
    PnS = [SV[f"PnS{i_}"] for i_ in range(4)]
    PTs = [SV[f"PTs{i_}"] for i_ in range(4)]
    PTn = [SV[f"PTn{i_}"] for i_ in range(4)]

    def sample_begin():
        tok = dve.emit(lambda: nc.vector.memset(bias[:, 0, 0:2], 0.0), w=[R("bias"), R("state"), R("oT"), R("QT"), R("gaT"), R("KT"), R("sc0"), R("sc1"), R("sc2"), R("sc3"),
                          R("Pn0"), R("Pn1"), R("Pn2"), R("Pn3"), R("PT0"), R("PT1"), R("PT2"), R("PT3")])
        for rg in sample_regs:
            rg.w = tok
            rg.r = {}
        sp.dma(bsel, bsel_d, w=[R("bsel")])
        sp.dma(cstb, cstb_d, w=[R("cstb")])
        pool.dma(cmask, cmask_d, w=[R("cmask")])
        sp.dma(sinkS, sinkS_d, w=[R("sinkS")])
        for h in range(32):
            kv, r_ = h // 4, h % 4
            src = bass.AP(tensor=scr.tensor, offset=h * 128 * 383 + 127, ap=[[382, 8], [1, 136]])
            sp.dma(biasS[8 * r_:8 * r_ + 8, kv, :], src, r=[R(f"scr{h}")], w=[R(f"biasS{h}")])
        return

    sst = {"g": -1}

    def sample_ctx(what, *args):
        if what == "state":
            return
        if what == "yoff":
            g, pbO, rbO = args
            reserved.add(bank_index(pbO))
            dA_g = dAt[:, 0, 8 * g:8 * g + 8]
            dve.emit(lambda: nc.vector.tensor_copy(out=dAx.rearrange("p (h d) -> p h d", h=8),
                                                   in_=dA_g.unsqueeze(2).to_broadcast([128, 8, 64])),
                     r=[R("dAt")], w=[R("dAx")])
            pbd, rbdk = nb_()
            for j in range(4):
                pe.emit(lambda: nc.tensor.matmul(pbd[:, j * 16:(j + 1) * 16], lhsT=dAx[:, j * 128:(j + 1) * 128],
                                                 rhs=bsel, start=True, stop=True), r=[R("dAx"), R("bsel")], w=[rbdk])
            act.emit(lambda: nc.scalar.activation(out=decP, in_=pbd[:, 0:64], func=AF.Exp), r=[rbdk], w=[R("decP")])
            dve.emit(lambda: nc.vector.tensor_tensor(out=CTm,
                                                     in0=cur["CT"][:, 0:128].unsqueeze(1).to_broadcast([128, 16, 128]),
                                                     in1=cmask.rearrange("p (b l) -> p b l", b=16), op=ALU.mult),
                     r=[cur["rCT"], R("cmask")], w=[R("CTm")])

            def ld_s0(b_):
                sp.dma(s0n[b_ % 4], sssm_d[b_, 512 * g:512 * (g + 1), :].rearrange("(j p) n -> p j n", p=128),
                       w=[R(f"s0n{b_ % 4}")])

            ld_s0(0)
            ld_s0(1)
            for b in range(16):
                bi = b % 2
                b4 = b % 4
                if b + 2 < 16:
                    ld_s0(b + 2)
                pbt, rbt = nb_()
                for j in range(4):
                    pe.emit(lambda: nc.tensor.transpose(out=pbt[:, j * 128:(j + 1) * 128], in_=s0n[b4][:, j, :],
                                                        identity=ident), r=[R(f"s0n{b4}"), R("cst")], w=[rbt])
                act.emit(lambda: nc.scalar.copy(out=S0bf[bi], in_=pbt[:]), r=[rbt], w=[R(f"S0bf{bi}")])
                pe.emit(lambda: nc.tensor.matmul(pbO[:], lhsT=CTm[:, b, :], rhs=S0bf[bi], start=(b == 0),
                                                 stop=(b == 15)), r=[R("CTm"), R(f"S0bf{bi}")], w=[rbO])
                dve.emit(lambda: nc.vector.tensor_scalar(out=wXm[bi], in0=cur["wX"][:], scalar1=bsel[:, b:b + 1],
                                                         scalar2=None, op0=ALU.mult),
                         r=[cur["rwX"], R("bsel")], w=[R(f"wXm{bi}")])
                pbn, rbn = nb_()
                for j in range(4):
                    pe.emit(lambda: nc.tensor.matmul(pbn[:, j * 128:(j + 1) * 128],
                                                     lhsT=wXm[bi][:, j * 128:(j + 1) * 128], rhs=cur["Btm"][:],
                                                     start=True, stop=True),
                            r=[R(f"wXm{bi}"), cur["rBtm"]], w=[rbn])
                for j in range(4):
                    dve.emit(lambda: nc.vector.scalar_tensor_tensor(
                        out=fin[bi][:, j * 128:(j + 1) * 128], in0=s0n[b4][:, j, :],
                        scalar=decP[:, j * 16 + b:j * 16 + b + 1], in1=pbn[:, j * 128:(j + 1) * 128], op0=ALU.mult,
                        op1=ALU.add), r=[R(f"s0n{b4}"), R("decP"), rbn], w=[R(f"fin{bi}")])
                sp.dma(ssm_s[b, 512 * g:512 * (g + 1), :].rearrange("(j p) n -> p j n", p=128),
                       fin[bi].rearrange("p (j n) -> p j n", j=4), r=[R(f"fin{bi}")], w=[R("ssmsout")])
            reserved.discard(bank_index(pbO))
            return
        ctg, xr, rr = what, args[0], args[1]
        if ctg < 32:
            g = ctg // 4
            c0 = (ctg % 4) * 128
        elif ctg < 40:
            g = ctg - 32
            c0 = 512
        else:
            g = ctg - 40
            c0 = 640
        if sst["g"] != g:
            sst["g"] = g
            sp.dma(sct[:, 0:512], sconv_d[:, 512 * g:512 * (g + 1)], w=[R("sct")])
            sp.dma(sct[:, 512:640], sconv_d[:, DI + 128 * g:DI + 128 * (g + 1)], w=[R("sctB")])
            sp.dma(sct[:, 640:768], sconv_d[:, DI + GN + 128 * g:DI + GN + 128 * (g + 1)], w=[R("sctC")])
        pbx, rbx = nb_()
        pe.emit(lambda: nc.tensor.transpose(out=pbx[:, 0:48], in_=sct[0:48, c0:c0 + 128], identity=cst[0:48, 0, 0:48]),
                r=[R("sct"), R("sctB"), R("sctC"), R("cst")], w=[rbx])
        act.emit(lambda: nc.scalar.copy(out=xr[:, :, 0:3], in_=pbx[:, 0:48].rearrange("p (b t) -> p b t", t=3)),
                 r=[rbx], w=[rr])

    def attn_sample():
        T = 128
        sp.dma(k_s[:, 0:120, :], ck_d[:, 8:128, :], w=[R("ksout")])
        sp.dma(v_s[:, 0:120, :], cv_d[:, 8:128, :], w=[R("vsout")])
        cnt = 0
        for jp in range(4):
            ka = OFF_K + 128 * jp
            va = OFF_V + 128 * jp
            wsl, wreg, ex = wload_multi([(w_in[:, ka:ka + 128], wview(16, 128, 0, 256)),
                                         (w_in[:, va:va + 128], wview(16, 128, 128, 256))])
            wait_extra(pe, ex)
            wv = wview(16, 256)(wsl)
            for a in range(2):
                pb, rb = nb_()
                for kk in range(16):
                    pe.emit(lambda: nc.tensor.matmul(pb[0:64, 0:T], lhsT=wv[:, kk, 64 * a:64 * a + 64],
                                                     rhs=hT[:, kk, 0:T], start=(kk == 0), stop=(kk == 15)),
                            r=[wreg, R("hT")], w=[rb])
                act.emit(lambda: nc.scalar.copy(out=KTn[:, a, :], in_=pb[0:64, 0:T]), r=[rb], w=[R("KTn")])
            pb, rb = nb_()
            for kk in range(16):
                pe.emit(lambda: nc.tensor.matmul(pb[:, 0:256], lhsT=hT[:, kk, 0:T], rhs=wv[:, kk, 0:256],
                                                 start=(kk == 0), stop=(kk == 15)), r=[wreg, R("hT")], w=[rb])
            act.emit(lambda: nc.scalar.copy(out=ktok[:, 128 * jp:128 * (jp + 1)], in_=pb[:, 0:128]), r=[rb],
                     w=[R("ktok")])
            act.emit(lambda: nc.scalar.copy(out=ktok[:, 512 + 128 * jp:512 + 128 * (jp + 1)], in_=pb[:, 128:256]),
                     r=[rb], w=[R("ktok")])
            for b in range(16):
                pool.dma(Vnb[0:8, b, :], ktok[8 * b:8 * b + 8, 512 + 128 * jp:512 + 128 * (jp + 1)], r=[R("ktok")],
                         w=[R(f"Vnb{b}")])
            for half in range(2):
                c0 = OFF_Q + 512 * jp + 256 * half
                wsl, wreg, _ = wload_multi([(w_in[:, c0:c0 + 256], wview(16, 256))])
                wv = wview(16, 256)(wsl)
                for hh in range(4):
                    pb, rb = nb_()
                    for kk in range(16):
                        pe.emit(lambda: nc.tensor.matmul(pb[0:64, 0:T], lhsT=wv[:, kk, 64 * hh:64 * hh + 64],
                                                         rhs=hT[:, kk, 0:T], start=(kk == 0), stop=(kk == 15)),
                                r=[wreg, R("hT")], w=[rb])
                    act.emit(lambda: nc.scalar.activation(out=QTs4[:, :, 4 * half + hh, :],
                                                          in_=pb[0:64, 0:T].rearrange("p (b l) -> p b l", b=16),
                                                          func=AF.Copy, scale=0.125), r=[rb], w=[R("QTs")])
            for half in range(2):
                c0 = OFF_GA + 512 * jp + 256 * half
                wsl, wreg, _ = wload_multi([(w_in[:, c0:c0 + 256], wview(16, 256))])
                wv = wview(16, 256)(wsl)
                for hh in range(4):
                    pb, rb = nb_()
                    for kk in range(16):
                        pe.emit(lambda: nc.tensor.matmul(pb[0:64, 0:T], lhsT=wv[:, kk, 64 * hh:64 * hh + 64],
                                                         rhs=hT[:, kk, 0:T], start=(kk == 0), stop=(kk == 15)),
                                r=[wreg, R("hT")], w=[rb])
                    act.emit(lambda: nc.scalar.activation(out=gaS[:, 4 * half + hh, :], in_=pb[0:64, 0:T],
                                                          func=AF.Silu), r=[rb], w=[R("gaS")])
            sitems = [(b, a) for b in range(16) for a in range(2)]

            def T1(i):
                b, a = sitems[i]
                bi = b % 4
                si = i % 4
                kv = 2 * jp + a
                if a == 0:
                    pool.dma(ckbf[bi], ck_d[b, :, 128 * jp:128 * (jp + 1)], w=[R(f"ckbf{bi}")])
                    pool.dma(cvbf[bi], cv_d[b, :, 128 * jp:128 * (jp + 1)], w=[R(f"cvbf{bi}")])
                    pbk_, rK = nb_()
                    pkb = pbk_[:].bitcast(BF16)
                    for a2 in range(2):
                        pe.emit(lambda: nc.tensor.transpose(
                            out=pkb[0:64, a2 * 128:(a2 + 1) * 128],
                            in_=ckbf[bi][:, 64 * a2:64 * a2 + 64], identity=cst_bf[:]),
                            r=[R(f"ckbf{bi}"), R("cst_bf")], w=[rK])
                    act.emit(lambda: nc.scalar.copy(
                        out=KTc[bi], in_=pkb[0:64, 0:256].rearrange("p (a k) -> p a k", a=2)),
                        r=[rK], w=[R(f"KTc{bi}")])
                qv = QTs[:, 64 * b + 32 * a:64 * b + 32 * a + 32]
                pbs, rbs = nb_()
                pe.emit(lambda: nc.tensor.matmul(pbs[0:32, 0:128], lhsT=qv, rhs=KTc[bi][:, a, :], start=True,
                                                 stop=True), r=[R("QTs"), R(f"KTc{bi}")], w=[rbs])
                pe.emit(lambda: nc.tensor.matmul(pbs[0:32, 128:136], lhsT=qv, rhs=KTn[:, a, 8 * b:8 * b + 8],
                                                 start=True, stop=True), r=[R("QTs"), R("KTn")], w=[rbs])
                rs = R(f"scS{si}")
                rsm = R(f"smS{si}")
                o4 = 4 * si
                dve.emit(lambda: nc.vector.tensor_tensor(out=scS[si], in0=pbs[0:32, 0:136], in1=biasS[:, kv, :],
                                                         op=ALU.add),
                         r=[rbs] + [R(f"biasS{4 * kv + r_}") for r_ in range(4)], w=[rs])
                dve.emit(lambda: nc.vector.reduce_max(out=smS[:, o4:o4 + 1], in_=scS[si], axis=AX.X), r=[rs], w=[rsm])
                dve.emit(lambda: nc.vector.tensor_scalar(out=smS[:, o4:o4 + 1], in0=smS[:, o4:o4 + 1],
                                                         scalar1=sinkS[:, kv:kv + 1], scalar2=-1.0, op0=ALU.max,
                                                         op1=ALU.mult), r=[rsm, R("sinkS")], w=[rsm])
                act.emit(lambda: nc.scalar.activation(out=scS[si], in_=scS[si], func=AF.Exp, bias=smS[:, o4:o4 + 1],
                                                      scale=1.0, accum_out=smS[:, o4 + 1:o4 + 2]), r=[rs, rsm],
                         w=[rs, rsm])
                act.emit(lambda: nc.scalar.activation(out=smS[:, o4 + 2:o4 + 3], in_=sinkS[:, kv:kv + 1], func=AF.Exp,
                                                      bias=smS[:, o4:o4 + 1], scale=1.0), r=[rsm, R("sinkS")],
                         w=[rsm])

            def T1c(i):
                si = i % 4
                rs = R(f"scS{si}")
                rsm = R(f"smS{si}")
                o4 = 4 * si
                dve.emit(lambda: nc.vector.tensor_tensor(out=smS[:, o4 + 1:o4 + 2], in0=smS[:, o4 + 1:o4 + 2],
                                                         in1=smS[:, o4 + 2:o4 + 3], op=ALU.add), r=[rsm], w=[rsm])
                dve.emit(lambda: nc.vector.reciprocal(out=smS[:, o4 + 3:o4 + 4], in_=smS[:, o4 + 1:o4 + 2]), r=[rsm],
                         w=[rsm])
                act.emit(lambda: nc.scalar.activation(out=PnS[si], in_=scS[si], func=AF.Identity,
                                                      scale=smS[:, o4 + 3:o4 + 4]), r=[rs, rsm], w=[R(f"PnS{si}")])

            def T2(i):
                si = i % 4
                pbs_, rbS = nb_()
                psb = pbs_[:].bitcast(BF16)
                pe.emit(lambda: nc.tensor.transpose(out=psb[:, 0:32], in_=PnS[si][:, 0:128],
                                                    identity=cst_bf[0:32, 0:32]),
                        r=[R(f"PnS{si}"), R("cst_bf")], w=[rbS])
                pe.emit(lambda: nc.tensor.transpose(out=psb[0:8, 32:64], in_=PnS[si][:, 128:136],
                                                    identity=cst_bf[0:32, 0:32]),
                        r=[R(f"PnS{si}"), R("cst_bf")], w=[rbS])
                act.emit(lambda: nc.scalar.copy(out=PTs[si], in_=psb[:, 0:32]), r=[rbS], w=[R(f"PTs{si}")])
                act.emit(lambda: nc.scalar.copy(out=PTn[si], in_=psb[0:8, 32:64]), r=[rbS],
                         w=[R(f"PTn{si}")])

            def T3(i):
                b, a = sitems[i]
                bi = b % 4
                si = i % 4
                pbo, rbo = nb_()
                pe.emit(lambda: nc.tensor.matmul(pbo[0:64, 0:32], lhsT=cvbf[bi][:, 64 * a:64 * a + 64],
                                                 rhs=PTs[si], start=True, stop=False),
                        r=[R(f"cvbf{bi}"), R(f"PTs{si}")], w=[rbo])
                pe.emit(lambda: nc.tensor.matmul(pbo[0:64, 0:32], lhsT=Vnb[0:8, b, 64 * a:64 * a + 64],
                                                 rhs=PTn[si], start=False, stop=True),
                        r=[R(f"Vnb{b}"), R(f"PTn{si}")], w=[rbo])
                dve.emit(lambda: nc.vector.tensor_tensor(
                    out=oTs[:, 8 * jp + 4 * a:8 * jp + 4 * a + 4, 8 * b:8 * b + 8],
                    in0=pbo[0:64, 0:32].rearrange("p (r l) -> p r l", r=4),
                    in1=gaS[:, 4 * a:4 * a + 4, 8 * b:8 * b + 8], op=ALU.mult), r=[rbo, R("gaS")], w=[R("oTs")])

            nsi = len(sitems)
            for step in range(nsi + 3):
                if step < nsi:
                    T1(step)
                if 0 <= step - 1 < nsi:
                    T1c(step - 1)
                if 0 <= step - 2 < nsi:
                    T2(step - 2)
                if 0 <= step - 3 < nsi:
                    T3(step - 3)
        for b in range(16):
            sp.dma(k_s[b, 120:128, :], ktok[8 * b:8 * b + 8, 0:512], r=[R("ktok")], w=[R("ksout")])
            sp.dma(v_s[b, 120:128, :], ktok[8 * b:8 * b + 8, 512:1024], r=[R("ktok")], w=[R("vsout")])

    import os
    STAGE = int(os.environ.get("KSTAGE", "99"))
    NMT = HALF // TM

    marks = []

    def mark(name):
        marks.append((name, pe.n, act.n, dve.n))

    def body():
        if STAGE < 1:
            return
        mark('setup_end')
        for mt in range(NMT):
            phase_a(xp[mt * TM:(mt + 1) * TM, :], TM // 128)
            if STAGE < 2:
                return
            mark(f'prevA{mt}')
            ssd_phase(TM, "state")
            mark(f'prevSSD{mt}')
            if STAGE < 3:
                return
            if mt == NMT - 1:
                attn_phase(TM, False, kv_only=True)
        if STAGE < 4:
            return
        dve.emit(lambda: nc.vector.tensor_scalar(out=state[:].rearrange("p g c -> p (g c)"),
                                                 in0=state[:].rearrange("p g c -> p (g c)"), scalar1=flag[:, 0:1],
                                                 scalar2=None, op0=ALU.mult), r=[R("state"), R("flag")],
                 w=[R("state")])
        dve.emit(lambda: nc.vector.tensor_scalar(out=carry[:].rearrange("p a b -> p (a b)"),
                                                 in0=carry[:].rearrange("p a b -> p (a b)"), scalar1=flag[:, 0:1],
                                                 scalar2=None, op0=ALU.mult), r=[R("carry"), R("flag")],
                 w=[R("carry")])
        for mt in range(NMT):
            mark(f'main_start{mt}')
            phase_a(xm[mt * TM:(mt + 1) * TM, :], TM // 128)
            mark(f'mainA{mt}')
            ssd_phase(TM, "main")
            mark(f'mainSSD{mt}')
            if STAGE < 5:
                return
            attn_phase(TM, mt == 0, emit_kv=(mt == NMT - 1))
            mark(f'mainATT{mt}')
            if STAGE < 6:
                return
            if mt == NMT - 1:
                cflat = carry[:].rearrange("p t c -> p (t c)")
                pbc, rbc = nb_()
                pe.emit(lambda: nc.tensor.transpose(out=pbc[:, 0:128], in_=cflat[:, 0:128], identity=ident),
                        r=[R("carry"), R("cst")], w=[rbc])
                pe.emit(lambda: nc.tensor.transpose(out=pbc[0:16, 128:256], in_=cflat[:, 128:144], identity=ident),
                        r=[R("carry"), R("cst")], w=[rbc])
                act.emit(lambda: nc.scalar.copy(out=cstage[0][:], in_=pbc[:, 0:128]), r=[rbc], w=[R("cstage0")])
                act.emit(lambda: nc.scalar.copy(out=cstage[1][0:16, :], in_=pbc[0:16, 128:256]), r=[rbc],
                         w=[R("cstage1")])
                cp3 = conv_p.rearrange("t (c p) -> t c p", p=128)
                act.dma(cp3[0], cstage[0][0:48, :], r=[R("cstage0")], w=[R("convpout")])
                act.dma(cp3[1], cstage[0][48:96, :], r=[R("cstage0")], w=[R("convpout")])
                act.dma(cp3[2, 0:32], cstage[0][96:128, :], r=[R("cstage0")], w=[R("convpout")])
                act.dma(cp3[2, 32:48], cstage[1][0:16, :], r=[R("cstage1")], w=[R("convpout")])
                sp.dma(k_p, ktok[:, 0:512], r=[R("ktok")], w=[R("kvout")])
                sp.dma(v_p, ktok[:, 512:1024], r=[R("ktok")], w=[R("kvout")])
            phase_de(TM, xm[mt * TM:(mt + 1) * TM, :], y_main[mt * TM:(mt + 1) * TM, :])
            mark(f'mainDE{mt}')
            if STAGE < 7:
                return
        for g in range(NG):
            pb, rb = nb_()
            for j in range(4):
                pe.emit(lambda: nc.tensor.transpose(out=pb[:, j * 128:(j + 1) * 128],
                                                    in_=state[:, g, j * 128:(j + 1) * 128], identity=ident),
                        r=[R("state"), R("cst")], w=[rb])
            act.emit(lambda: nc.scalar.copy(out=ta[:], in_=pb[:]), r=[rb], w=[R("ta")])
            sp.dma(ssm_p[512 * g:512 * (g + 1), :].rearrange("(j p) n -> p j n", p=128),
                   ta[:].rearrange("p (j n) -> p j n", j=4), r=[R("ta")], w=[R("ssmout")])

    body()
    if do_sample and STAGE >= 8:
        mark('sample_start')
        sample_begin()
        phase_a(xs, 1)
        ssd_phase(128, "sample", sample_ctx)
        mark('sampleSSD')
        if STAGE >= 9:
            attn_sample()
        mark('sampleATT')
        if STAGE >= 10:
            phase_de(128, xs, y_samp, sample=True)
        mark('sampleDE')
    if os.environ.get('KPRINT'):
        for m_ in marks:
            print('MARK', *m_)
    if os.environ.get('KPRINT'):
        print('EMIT COUNT', k.count, 'pe', pe.n, 'act', act.n, 'dve', dve.n, 'nsem', k.nsem)
    for e in (pe, act, dve, pool):
        if e.n > 0:
            ep = (e.n - 1) // e.EPOCH
            sp.h.wait_ge(e.sems[ep], e.n - ep * e.EPOCH)

    for slot_tok in sp.ringtok:
        if slot_tok is not None:
            sp.waited.pop(id(slot_tok[0]), None)
            sp.h.wait_ge(slot_tok[0], slot_tok[1])
    for slot_tok in pool.ringtok:
        if slot_tok is not None:
            pool.h.wait_ge(slot_tok[0], slot_tok[1])
    for slot_tok in act.ringtok:
        if slot_tok is not None:
            sp.h.wait_ge(slot_tok[0], slot_tok[1])
    return nc, stack


def _sel_consts():
    bt = _bucket_table()
    sel = np.zeros((32, 383), np.float32)
    negm = np.full((383,), NEG, np.float32)
    for m in range(127, 256):
        d = 255 - m
        sel[bt[d], m] = 1.0
        negm[m] = 0.0
    return sel, np.broadcast_to(negm, (128, 383)).copy()


def _retile_win(w):
    out = np.empty((128, _WTOT), np.float32)
    for (c0, n) in _WBL:
        off = _WTAB[(c0, n)]
        out[:, off:off + 16 * n] = w[:, c0:c0 + n].reshape(16, 128, n).transpose(1, 0, 2).reshape(128, 16 * n)
    return out

_CACHE = {}


def kernel(x_prompt, x_sample, cache_k, cache_v, state_conv, state_ssm, norm_w, w_in, conv_w, conv_b, dt_bias,
           a_log, d_skip, ssm_norm_w, w_ssm_branch, attn_sinks, w_attn_branch, w_out, rel_bias, final_norm_w,
           _cores=None):
    f = np.float32
    cores = list(range(NCORES)) if _cores is None else _cores
    if "nc" not in _CACHE:
        _CACHE["nc"] = build_program()
    nc, _stack = _CACHE["nc"]
    sel, negm = _sel_consts()
    cst = _host_consts(1)
    cstb = _host_consts(16)
    bsel = (np.arange(128)[:, None] // 8 == np.arange(16)[None, :]).astype(f)
    shared = {
        "w_in_t": _retile_win(np.asarray(w_in[0], f)), "w_ssm_t": _retile_sq(np.asarray(w_ssm_branch[0], f), _SSM_BL, _SSM_TAB, _SSM_TOT),
        "w_attn": np.ascontiguousarray(w_attn_branch[0], f),
        "w_attn_t": _retile_sq(np.asarray(w_attn_branch[0], f), _SQ_BL, _SQ_TAB, _SQ_TOT),
        "w_out_t": _retile_sq(np.asarray(w_out[0], f), _SQ_BL, _SQ_TAB, _SQ_TOT),
        "normw": np.ascontiguousarray(norm_w[0].reshape(16, 128).T, f),
        "cw": np.ascontiguousarray(conv_w[0].reshape(4, 48, 128).transpose(2, 1, 0), f),
        "cb": np.ascontiguousarray(conv_b[0].reshape(48, 128).T, f),
        "snw": np.ascontiguousarray(ssm_norm_w[0].reshape(32, 128).T, f),
        "dtb": np.ascontiguousarray(dt_bias[0], f), "alog": np.ascontiguousarray(a_log[0], f),
        "dsk": np.ascontiguousarray(d_skip[0], f), "sinks": np.ascontiguousarray(attn_sinks[0], f),
        "fnw": np.ascontiguousarray(final_norm_w, f), "relb": np.ascontiguousarray(rel_bias, f),
        "sel": sel, "negm": negm, "cst": cst, "cstb": cstb, "bsel": bsel,
        "cmask": np.ascontiguousarray(np.broadcast_to((np.arange(128)[None, :] // 8 == np.arange(16)[:, None]).astype(f).reshape(1, 2048), (128, 2048))),
        "sinkS": np.ascontiguousarray(np.repeat(attn_sinks[0].reshape(8, 4).T, 8, axis=0), f),
    }
    in_maps = []
    for c in cores:
        b, h = c // 2, c % 2
        m = dict(shared)
        m["xm"] = np.ascontiguousarray(x_prompt[b, h * HALF:(h + 1) * HALF], f)
        m["xp"] = np.ascontiguousarray(x_prompt[b, 0:HALF], f) if h == 1 else np.zeros((HALF, D), f)
        m["xs"] = np.ascontiguousarray(x_sample[16 * c:16 * c + 16].reshape(128, D), f)
        fl = np.zeros((128, 2), f)
        fl[:, 0] = float(h)
        fl[:, 1] = (float(h) - 1.0) * 30000.0
        m["flag"] = fl
        m["ck"] = np.ascontiguousarray(cache_k[0, 16 * c:16 * c + 16].reshape(16, 128, 512), f)
        m["cv"] = np.ascontiguousarray(cache_v[0, 16 * c:16 * c + 16].reshape(16, 128, 512), f)
        m["sconv"] = np.ascontiguousarray(state_conv[0, 16 * c:16 * c + 16].reshape(48, CONV_DIM), f)
        m["sssm"] = np.ascontiguousarray(state_ssm[0, 16 * c:16 * c + 16].reshape(16, DI, NS), f)
        in_maps.append(m)
    res = run_bass_kernel_spmd(nc, in_maps, core_ids=list(range(len(cores))))
    rs = res.results
    B = x_prompt.shape[0]
    y_prompt = np.zeros((B, 2048, D), f)
    y_sample = np.zeros((128, 8, D), f)
    k_prompt = np.zeros((1, B, 128, 8, 64), f)
    v_prompt = np.zeros((1, B, 128, 8, 64), f)
    conv_prompt = np.zeros((1, B, 3, CONV_DIM), f)
    ssm_prompt = np.zeros((1, B, NH, HP, NS), f)
    k_sample = np.zeros((1, 128, 128, 8, 64), f)
    v_sample = np.zeros((1, 128, 128, 8, 64), f)
    conv_sample = np.zeros((1, 128, 3, CONV_DIM), f)
    ssm_sample = np.zeros((1, 128, NH, HP, NS), f)
    for i, c in enumerate(cores):
        b, h = c // 2, c % 2
        r = rs[i]
        y_prompt[b, h * HALF:(h + 1) * HALF] = r["y_main"]
        y_sample[16 * c:16 * c + 16] = r["y_samp"].reshape(16, 8, D)
        if h == 1:
            k_prompt[0, b] = r["k_p"].reshape(128, 8, 64)
            v_prompt[0, b] = r["v_p"].reshape(128, 8, 64)
            conv_prompt[0, b] = r["conv_p"]
            ssm_prompt[0, b] = r["ssm_p"].reshape(NH, HP, NS)
        k_sample[0, 16 * c:16 * c + 16] = r["k_s"].reshape(16, 128, 8, 64)
        v_sample[0, 16 * c:16 * c + 16] = r["v_s"].reshape(16, 128, 8, 64)
        conv_sample[0, 16 * c:16 * c + 16] = r["conv_s"].reshape(16, 3, CONV_DIM)
        ssm_sample[0, 16 * c:16 * c + 16] = r["ssm_s"].reshape(16, NH, HP, NS)
    return (y_prompt, y_sample, k_prompt, v_prompt, conv_prompt, ssm_prompt, k_sample, v_sample, conv_sample,
            ssm_sample)
```
